# Optimizing a Trainium2 kernel written in Bass

```python
import jax, jax.numpy as jnp
from jax import lax
import numpy as np


D_MODEL = 1024
BATCH = 16
SEQ = 2048
DEPTH = 2

N_META = 16
NORM_EPS = 1e-6
N_EVEN = (DEPTH + 1) // 2
N_ODD = DEPTH // 2

GDN_HEADS = 4
GDN_HEAD_DIM = 128
GDN_WIDTH = GDN_HEADS * GDN_HEAD_DIM
GDN_CONV = 4
GDN_CHUNK = 64

CONF_WIDTH = D_MODEL - GDN_WIDTH
CONF_CONV = 31

SWA_HEAD_DIM = 64
SWA_Q_HEADS = D_MODEL // SWA_HEAD_DIM
SWA_KV_HEADS = 2
SWA_WINDOW = 128
SWA_BLOCK = SWA_WINDOW
SWA_Q_WIDTH = SWA_Q_HEADS * SWA_HEAD_DIM
SWA_KV_WIDTH = SWA_KV_HEADS * SWA_HEAD_DIM

EVEN_SPLITS = [3 * GDN_WIDTH, GDN_WIDTH, GDN_HEADS, GDN_HEADS, CONF_WIDTH, CONF_WIDTH, CONF_WIDTH]
ODD_SPLITS = [SWA_Q_WIDTH, SWA_KV_WIDTH, SWA_KV_WIDTH, SWA_Q_WIDTH]
EVEN_IN = sum(EVEN_SPLITS)
ODD_IN = sum(ODD_SPLITS)

kernel_name = 'hybrid_gdn_conformer_swa_sink'


def rms_norm(x, w):
    xf = x.astype(jnp.float32)
    y = xf * lax.rsqrt(jnp.mean(xf * xf, axis=-1, keepdims=True) + NORM_EPS)
    return (y * w.astype(jnp.float32)).astype(x.dtype)


def layer_norm(x, w, b):
    xf = x.astype(jnp.float32)
    mu = jnp.mean(xf, axis=-1, keepdims=True)
    var = jnp.mean(jnp.square(xf - mu), axis=-1, keepdims=True)
    y = (xf - mu) * lax.rsqrt(var + NORM_EPS) * w.astype(jnp.float32) + b.astype(jnp.float32)
    return y.astype(x.dtype)


def l2_normalize(x):
    return x * lax.rsqrt(jnp.sum(x * x, axis=-1, keepdims=True) + NORM_EPS)


def split_cols(a, sizes):
    return jnp.split(a, np.cumsum(sizes)[:-1].tolist(), axis=-1)


def causal_depthwise_conv(x, w):
    width = w.shape[0]
    return lax.conv_general_dilated(
        x, w.astype(x.dtype)[:, None, :], window_strides=(1,), padding=[(width - 1, 0)],
        dimension_numbers=('NWC', 'WIO', 'NWC'), feature_group_count=x.shape[-1])


def chunked_gated_delta_rule(q, k, v, g, beta):
    b, l, h, dk = q.shape
    dv = v.shape[-1]
    c = GDN_CHUNK
    pad = (-l) % c
    n = (l + pad) // c

    def to_chunks(a):
        a = jnp.pad(a, [(0, 0), (pad, 0)] + [(0, 0)] * (a.ndim - 2))
        a = a.reshape((b, n, c) + a.shape[2:])
        return jnp.moveaxis(a, 3, 1)

    q, k, v, g, beta = (to_chunks(a) for a in (q, k, v, g, beta))
    gc = jnp.cumsum(g, axis=-1)
    idx = jnp.arange(c)
    incl = idx[:, None] >= idx[None, :]
    strict = idx[:, None] > idx[None, :]
    decay_incl = jnp.exp(jnp.where(incl, gc[..., :, None] - gc[..., None, :], -jnp.inf))
    decay_strict = jnp.where(strict, decay_incl, 0.0)
    kb = k * beta[..., None]
    vb = v * beta[..., None]
    a_mat = jnp.einsum('bhncd,bhnsd->bhncs', kb, k) * decay_strict
    eye = jnp.eye(c, dtype=jnp.float32)
    t_mat = lax.linalg.triangular_solve(eye + a_mat, jnp.broadcast_to(eye, a_mat.shape),
                                        left_side=True, lower=True, unit_diagonal=True)
    u = jnp.einsum('bhncs,bhnse->bhnce', t_mat, vb)
    w = jnp.einsum('bhncs,bhnsd->bhncd', t_mat, kb * jnp.exp(gc)[..., None])
    qk = jnp.einsum('bhncd,bhnsd->bhncs', q, k) * decay_incl
    q_dec = q * jnp.exp(gc)[..., None]
    k_dec = k * jnp.exp(gc[..., -1:] - gc)[..., None]
    state_decay = jnp.exp(gc[..., -1])
    xs = tuple(jnp.moveaxis(a, 2, 0) for a in (u, w, q_dec, qk, k_dec, state_decay))

    def step(state, inp):
        u_c, w_c, qd_c, qk_c, kd_c, sd_c = inp
        v_new = u_c - jnp.einsum('bhcd,bhde->bhce', w_c, state)
        out = (jnp.einsum('bhcd,bhde->bhce', qd_c, state)
               + jnp.einsum('bhcs,bhse->bhce', qk_c, v_new))
        state = state * sd_c[..., None, None] + jnp.einsum('bhcd,bhce->bhde', kd_c, v_new)
        return state, out

    s0 = jnp.zeros((b, h, dk, dv), jnp.float32)
    _, out = lax.scan(step, s0, xs)
    out = jnp.transpose(out, (1, 0, 3, 2, 4)).reshape(b, n * c, h, dv)
    return out[:, pad:]


def sliding_window_sink_attention(q, k, v, sinks):
    bsz, l, _ = q.shape
    blk = SWA_BLOCK
    pad = (-l) % blk
    nb = (l + pad) // blk
    grp = SWA_Q_HEADS // SWA_KV_HEADS
    k_meta = k[:, :N_META].reshape(bsz, N_META, SWA_KV_HEADS, SWA_HEAD_DIM)
    v_meta = v[:, :N_META].reshape(bsz, N_META, SWA_KV_HEADS, SWA_HEAD_DIM)

    def blocks(a, heads_shape):
        a = jnp.pad(a, ((0, 0), (pad, 0), (0, 0)))
        return a.reshape((bsz, nb, blk) + heads_shape)

    def band(a):
        prev = jnp.concatenate([jnp.zeros_like(a[:, :1]), a[:, :-1]], axis=1)
        return jnp.concatenate([prev, a], axis=2)

    qb = blocks(q, (SWA_KV_HEADS, grp, SWA_HEAD_DIM)) * (SWA_HEAD_DIM ** -0.5)
    kb = band(blocks(k, (SWA_KV_HEADS, SWA_HEAD_DIM)))
    vb = band(blocks(v, (SWA_KV_HEADS, SWA_HEAD_DIM)))
    s_meta = jnp.einsum('bnqhgd,bmhd->bhgnqm', qb, k_meta)
    s_band = jnp.einsum('bnqhgd,bnkhd->bhgnqk', qb, kb)
    scores = jnp.concatenate([s_meta, s_band], axis=-1).astype(jnp.float32)

    pos_q = jnp.arange(nb * blk).reshape(nb, blk) - pad
    pos_k = jnp.concatenate([pos_q - blk, pos_q], axis=1)
    rel = pos_q[:, :, None] - pos_k[:, None, :]
    band_mask = (pos_k[:, None, :] >= N_META) & (rel >= 0) & (rel < SWA_WINDOW)
    meta_mask = jnp.arange(N_META)[None, None, :] <= pos_q[:, :, None]
    mask = jnp.concatenate([meta_mask, band_mask], axis=-1)
    scores = jnp.where(mask, scores, -jnp.inf)

    sink = sinks.astype(jnp.float32).reshape(SWA_KV_HEADS, grp)[None, :, :, None, None, None]
    m = jnp.maximum(scores.max(axis=-1, keepdims=True), sink)
    p = jnp.exp(scores - m)
    denom = p.sum(axis=-1) + jnp.exp(sink - m)[..., 0]
    p = p / denom[..., None]
    o = (jnp.einsum('bhgnqm,bmhd->bnqhgd', p[..., :N_META], v_meta.astype(jnp.float32))
         + jnp.einsum('bhgnqk,bnkhd->bnqhgd', p[..., N_META:], vb.astype(jnp.float32)))
    return o.reshape(bsz, nb * blk, SWA_Q_WIDTH)[:, pad:].astype(q.dtype)


def even_layer(x, pre_norm, w_in, qkv_conv, a_log, dt_bias, out_norm,
               dw_conv, dw_bias, ln_w, ln_b, w_out, post_norm):
    bsz, l, _ = x.shape
    h = rms_norm(x, pre_norm)
    proj = jnp.einsum('bld,de->ble', h, w_in)
    qkv, z_a, b_a, a_a, glu_v, glu_g, z_b = split_cols(proj, EVEN_SPLITS)

    qkv = jax.nn.silu(causal_depthwise_conv(qkv, qkv_conv)).astype(jnp.float32)
    qkv = qkv.reshape(bsz, l, 3, GDN_HEADS, GDN_HEAD_DIM)
    q = l2_normalize(qkv[:, :, 0]) * (GDN_HEAD_DIM ** -0.5)
    k = l2_normalize(qkv[:, :, 1])
    v = qkv[:, :, 2]
    beta = jax.nn.sigmoid(b_a.astype(jnp.float32))
    g = -jnp.exp(a_log.astype(jnp.float32)) * jax.nn.softplus(
        a_a.astype(jnp.float32) + dt_bias.astype(jnp.float32))
    o_a = chunked_gated_delta_rule(q, k, v, g, beta)
    o_a = rms_norm(o_a, out_norm).reshape(bsz, l, GDN_WIDTH).astype(x.dtype) * jax.nn.silu(z_a)

    c = glu_v * jax.nn.sigmoid(glu_g)
    c = causal_depthwise_conv(c, dw_conv) + dw_bias
    c = jax.nn.silu(layer_norm(c, ln_w, ln_b))
    o_b = c * jax.nn.silu(z_b)

    y = jnp.einsum('ble,ed->bld', jnp.concatenate([o_a, o_b], axis=-1), w_out)
    return x + rms_norm(y, post_norm)


def odd_layer(x, pre_norm, w_in, sinks, w_out, post_norm):
    h = rms_norm(x, pre_norm)
    proj = jnp.einsum('bld,de->ble', h, w_in)
    q, k, v, z = split_cols(proj, ODD_SPLITS)
    o = sliding_window_sink_attention(q, k, v, sinks) * jax.nn.silu(z)
    y = jnp.einsum('ble,ed->bld', o, w_out)
    return x + rms_norm(y, post_norm)


def setup_inputs(seed: int = 0) -> dict:
    key = jax.random.key(seed)
    ks = jax.random.split(key, 20)
    f32 = jnp.float32
    d = D_MODEL

    def nrm(k, shape, scale):
        return jax.random.normal(k, shape, f32) * scale

    def gain(k, shape):
        return 1.0 + 0.02 * jax.random.normal(k, shape, f32)

    dt = jnp.exp(jax.random.uniform(ks[5], (N_EVEN, GDN_HEADS), f32, np.log(1e-3), np.log(1e-1)))
    return {
        'x': nrm(ks[0], (BATCH, SEQ, d), 1.0),
        'meta_tokens': nrm(ks[1], (N_META, d), 1.0),
        'even_pre_norm': gain(ks[2], (N_EVEN, d)),
        'even_w_in': nrm(ks[3], (N_EVEN, d, EVEN_IN), d ** -0.5),
        'even_qkv_conv': nrm(ks[4], (N_EVEN, GDN_CONV, 3 * GDN_WIDTH), GDN_CONV ** -0.5),
        'even_a_log': jnp.log(jax.random.uniform(ks[6], (N_EVEN, GDN_HEADS), f32, 1.0, 16.0)),
        'even_dt_bias': dt + jnp.log(-jnp.expm1(-dt)),
        'even_out_norm': gain(ks[7], (N_EVEN, GDN_HEAD_DIM)),
        'even_dw_conv': nrm(ks[8], (N_EVEN, CONF_CONV, CONF_WIDTH), CONF_CONV ** -0.5),
        'even_dw_bias': nrm(ks[9], (N_EVEN, CONF_WIDTH), 0.02),
        'even_ln_w': gain(ks[10], (N_EVEN, CONF_WIDTH)),
        'even_ln_b': nrm(ks[11], (N_EVEN, CONF_WIDTH), 0.02),
        'even_w_out': nrm(ks[12], (N_EVEN, GDN_WIDTH + CONF_WIDTH, d), (GDN_WIDTH + CONF_WIDTH) ** -0.5),
        'even_post_norm': gain(ks[13], (N_EVEN, d)),
        'odd_pre_norm': gain(ks[14], (N_ODD, d)),
        'odd_w_in': nrm(ks[15], (N_ODD, d, ODD_IN), d ** -0.5),
        'odd_sinks': nrm(ks[16], (N_ODD, SWA_Q_HEADS), 0.5),
        'odd_w_out': nrm(ks[17], (N_ODD, SWA_Q_WIDTH, d), SWA_Q_WIDTH ** -0.5),
        'odd_post_norm': gain(ks[18], (N_ODD, d)),
    }


def reference(x, meta_tokens, even_pre_norm, even_w_in, even_qkv_conv, even_a_log,
              even_dt_bias, even_out_norm, even_dw_conv, even_dw_bias, even_ln_w, even_ln_b,
              even_w_out, even_post_norm, odd_pre_norm, odd_w_in, odd_sinks, odd_w_out,
              odd_post_norm):
    bsz = x.shape[0]
    meta = jnp.broadcast_to(meta_tokens.astype(x.dtype)[None], (bsz, N_META, D_MODEL))
    h = jnp.concatenate([meta, x], axis=1)
    for layer in range(DEPTH):
        i = layer // 2
        if layer % 2 == 0:
            h = even_layer(h, even_pre_norm[i], even_w_in[i], even_qkv_conv[i], even_a_log[i],
                           even_dt_bias[i], even_out_norm[i], even_dw_conv[i], even_dw_bias[i],
                           even_ln_w[i], even_ln_b[i], even_w_out[i], even_post_norm[i])
        else:
            h = odd_layer(h, odd_pre_norm[i], odd_w_in[i], odd_sinks[i], odd_w_out[i],
                          odd_post_norm[i])
    return h[:, N_META:]
```

```python
import math
import numpy as np
from contextlib import ExitStack
import concourse.bass as bass
import concourse.mybir as mybir
from concourse.bass_utils import run_bass_kernel_spmd

F32 = mybir.dt.float32
BF16 = mybir.dt.bfloat16
ALU = mybir.AluOpType
AF = mybir.ActivationFunctionType
AX = mybir.AxisListType

ENGS = ("pe", "act", "dve", "pool", "sp")
NCORES = 8
L1_FLOAT = ("norm","proj","attn_s","tail")
L1_OFF, L1_SPAN, CF_OFF, CF_SPAN = 0.0, 1.0, 0.25, 1.0
NT = 17
EPS = 1e-6


class Buf:
    def __init__(self, ap, name):
        self.ap = ap
        self.name = name
        self.last_w = None
        self.readers = {}
        self.const = False
        self.gen = 0

    def __getitem__(self, key):
        return self.ap[key]


class PV:
    def __init__(self, bank):
        self.b = bank
        self.gen = bank.gen

    def _chk(self):
        assert self.b.gen == self.gen, f"stale psum handle {self.b.name}"

    def __getitem__(self, key):
        return self.b.ap[key]

    @property
    def bf(self):
        return self.b.bfv


class Sw:
    def __init__(self, k, key, bufs):
        self.k = k
        self.key = key
        self.bufs = bufs

    def cur(self):
        return self.bufs[self.k.ctx[self.key]]

    def __getitem__(self, key):
        return self.cur().ap[key]


class Instr:
    __slots__ = ("id", "eng", "fn", "deps", "dma", "group", "signals", "ordinal", "ctx")

    def __init__(self, id, eng, fn, deps, dma, group):
        self.ctx = None
        self.id = id
        self.eng = eng
        self.fn = fn
        self.deps = deps
        self.dma = dma
        self.group = group
        self.signals = False
        self.ordinal = 0


class _FakeIns:
    def then_inc(self, *a, **k):
        return self


class _FakeEng:
    def __init__(self):
        self.info = None

    def __getattr__(self, name):
        def f(*args, **kw):
            out = kw.get("out", args[0] if args else None)
            self.info = (name, out, kw, args)
            return _FakeIns()
        return f


def _est_cost(eng, fn, dma):
    fe = _FakeEng()
    try:
        fn(fe)
        name, out, kw, args = fe.info
        free = out.free_size()
        dt_ = out.dtype
    except Exception:
        return 0.3, 0.3, None
    if dma:
        nbytes = free * out.partition_size() * (2 if dt_ == BF16 else 4)
        return 0.08, 2.2 + nbytes / 150e3, None
    if eng == "pe":
        lhsT = kw.get("lhsT", None)
        mult = 2.2 if (lhsT is not None and lhsT.dtype == F32) else 1.0
        if name == "transpose" and args[1].dtype == F32:
            mult = 2.0
        c = max(0.1, free * mult / 1150.0) + 0.01
        grp = (bool(kw.get('start', True)), bool(kw.get('stop', True))) if name == 'matmul' else (True, True)
        return c, c + 0.1, grp
    if eng == "act":
        c = 0.2 + free / 1100.0 + (0.1 if kw.get("accum_out", None) is not None else 0.0)
        return c, c, None
    if eng == "dve":
        c = 0.08 + free / (1800.0 if dt_ == BF16 else 950.0)
        return c, c, None
    c = 0.3 + free / 520.0
    return c, c, None


class Seq:
    def __init__(self, banks, offset=0.0, span=1.0):
        self.items = []
        self.banks = banks
        self.ptr = 0
        self.offset = offset
        self.span = span


class Par:
    def __init__(self):
        self.branches = []


def _flatten(node):
    if isinstance(node, Seq):
        out = []
        for it in node.items:
            if isinstance(it, (Seq, Par)):
                out.extend(_flatten(it))
            else:
                out.append(it)
        return out
    lists = [_flatten(b) for b in node.branches]
    keyed = []
    for li, l in enumerate(lists):
        n = len(l)
        off = node.branches[li].offset
        spn = node.branches[li].span
        for j, it in enumerate(l):
            keyed.append((off + (spn - off) * (j + 0.5) / n, li, j, it))
    keyed.sort(key=lambda x: (x[0], x[1], x[2]))
    return [x[3] for x in keyed]


class _Branch:
    def __init__(self, k, seq):
        self.k = k
        self.seq = seq

    def __enter__(self):
        self.prev = self.k.cur
        self.k.cur = self.seq
        return self.seq

    def __exit__(self, *a):
        self.k.cur = self.prev
        return False


class Kern:
    def __init__(self, nc):
        self.nc = nc
        self.instrs = []
        self.stack = None
        self.banks = []
        self.root = Seq(list(range(8)))
        self.cur = self.root
        self.ctx = {"th": 0, "x": 0, "br": 0, "hp": 0}
        self.pe_inorder = False
        self.pe_tok = Buf(None, "pe_tok")
        self.pin_pe = False
        self.schedule = True

    def sbuf(self, name, shape, dtype):
        t = self.stack.enter_context(self.nc.sbuf_tensor(name, list(shape), dtype))
        return Buf(t, name)

    def psum_init(self):
        for i in range(8):
            t = self.stack.enter_context(self.nc.psum_tensor(f"psb{i}", [128, 512], F32))
            b = Buf(t, f"psb{i}")
            b.bfv = t.bitcast(BF16)
            b.is_psum = True
            self.banks.append(b)

    def psum(self):
        sq = self.cur
        b = self.banks[sq.banks[sq.ptr % len(sq.banks)]]
        sq.ptr += 1
        b.gen += 1
        return PV(b)

    def fork(self):
        p = Par()
        self.cur.items.append(p)
        return p

    def branch(self, par, banks, offset=0.0, span=1.0):
        sq = Seq(banks, offset, span)
        par.branches.append(sq)
        return _Branch(self, sq)

    def _emit(self, eng, fn, r=(), w=(), dma=False, group=None):
        rr = []
        for x in r:
            if isinstance(x, PV):
                x._chk()
                x = x.b
            elif isinstance(x, Sw):
                x = x.cur()
            rr.append(x)
        ww = []
        for x in w:
            if isinstance(x, PV):
                x._chk()
                x = x.b
            elif isinstance(x, Sw):
                x = x.cur()
            ww.append(x)
        if isinstance(group, Sw):
            group = group.cur()
        self.cur.items.append((eng, fn, rr, ww, dma, group, dict(self.ctx)))

    def _analyze(self, rec):
        eng, fn, rr, ww, dma, group, ctx = rec
        iid = len(self.instrs)
        deps = set()
        for b in rr:
            if b.last_w is not None:
                deps.add(b.last_w)
            if getattr(b, "is_psum", False):
                for key, rid in b.readers.items():
                    if key != eng:
                        deps.add(rid)
        for b in ww:
            for rid in b.readers.values():
                deps.add(rid)
            if b.last_w is not None:
                deps.add(b.last_w)
        fdeps = []
        for d in deps:
            di = self.instrs[d]
            if di.eng == eng and not di.dma:
                if eng == "pe":
                    continue
                israw = any(b.last_w == d for b in rr) or any(b.last_w == d for b in ww)
                if not israw:
                    continue
            fdeps.append(d)
        ins = Instr(iid, eng, fn, fdeps, dma, group)
        ins.ctx = ctx
        self.instrs.append(ins)
        for b in ww:
            b.last_w = iid
            b.readers = {}
        for b in rr:
            if b.const:
                continue
            key = ("dma", iid) if dma else eng
            b.readers[key] = iid

    def pe(self, fn, r=(), w=()):
        if self.pe_inorder or self.pin_pe:
            w = list(w) + [self.pe_tok]
        return self._emit("pe", fn, r, w)

    def act(self, fn, r=(), w=()):
        return self._emit("act", fn, r, w)

    def dve(self, fn, r=(), w=()):
        return self._emit("dve", fn, r, w)

    def pool(self, fn, r=(), w=()):
        return self._emit("pool", fn, r, w)

    def dma(self, eng, fn, r=(), w=(), group=None):
        return self._emit(eng, fn, r, w, dma=True, group=group)

    def _list_schedule(self, recs):
        import heapq
        n = len(recs)
        lastw = {}
        readers = {}
        preds = [None] * n
        for i, (eng, fn, rr, ww, dma, group, ctx) in enumerate(recs):
            d = set()
            for b in rr:
                if id(b) in lastw:
                    d.add(lastw[id(b)])
            for b in ww:
                if id(b) in lastw:
                    d.add(lastw[id(b)])
                for r_ in readers.get(id(b), ()):
                    d.add(r_)
            d.discard(i)
            preds[i] = d
            for b in ww:
                lastw[id(b)] = i
                readers[id(b)] = []
            for b in rr:
                readers.setdefault(id(b), []).append(i)
        succs = [[] for _ in range(n)]
        npred = [0] * n
        for i in range(n):
            npred[i] = len(preds[i])
            for p in preds[i]:
                succs[p].append(i)
        occ = [0.0] * n
        lat = [0.0] * n
        open_grp = {}
        grp_of = {}
        for i, (eng, fn, rr, ww, dma, group, ctx) in enumerate(recs):
            self.ctx.update(ctx)
            occ[i], lat[i], g_ = _est_cost(eng, fn, dma)
            if eng == "pe":
                bank = id(ww[0])
                if g_ is None:
                    g_ = (True, True)
                st_, sp_ = g_
                if st_ or bank not in open_grp:
                    open_grp[bank] = []
                open_grp[bank].append(i)
                grp_of[i] = open_grp[bank]
                if sp_:
                    del open_grp[bank]
        ready_t = [0.0] * n
        finish = [0.0] * n
        heaps = {e: [] for e in ENGS}
        for i in range(n):
            if npred[i] == 0:
                heapq.heappush(heaps[recs[i][0]], (0.0, i))
        eng_free = {e: 0.0 for e in ENGS}
        order = []
        done = 0
        released = [npred[i] == 0 for i in range(n)]
        lock = None
        self.lock_breaks = 0

        def commit(e, st, i):
            nonlocal done
            eng_free[e] = st + occ[i]
            finish[i] = st + lat[i]
            order.append((st, i))
            done += 1
            for s_ in succs[i]:
                ready_t[s_] = max(ready_t[s_], finish[i] + 0.06)
                npred[s_] -= 1
                if npred[s_] == 0:
                    released[s_] = True
                    heapq.heappush(heaps[recs[s_][0]], (ready_t[s_], s_))

        scheduled = [False] * n
        while done < n:
            if lock:
                nx = lock[0]
                if released[nx]:
                    lock.pop(0)
                    st = max(ready_t[nx], eng_free["pe"])
                    scheduled[nx] = True
                    commit("pe", st, nx)
                    continue
            best = None
            for e in ENGS:
                if e == "pe" and lock:
                    continue
                h = heaps[e]
                while h and scheduled[h[0][1]]:
                    heapq.heappop(h)
                if not h:
                    continue
                rt, i = h[0]
                st = max(rt, eng_free[e])
                if best is None or st < best[0] or (st == best[0] and i < best[2]):
                    best = (st, e, i)
            if best is None:
                self.lock_breaks += 1
                lock = None
                continue
            st, e, _ = best
            h = heaps[e]
            cands = []
            while h and h[0][0] <= st + 1e-9:
                c_ = heapq.heappop(h)
                if not scheduled[c_[1]]:
                    cands.append(c_)
            cands.sort(key=lambda x: x[1])
            rt, i = cands[0]
            for c_ in cands[1:]:
                heapq.heappush(h, c_)
            if e == "pe":
                g = grp_of[i]
                if g[0] != i:
                    pass
                rest = [m for m in g if m != i and not scheduled[m]]
                lock = rest if rest else None
            scheduled[i] = True
            commit(e, st, i)
        order.sort(key=lambda x: (x[0], x[1]))
        self.est_makespan = max(finish)
        return [recs[i] for (_, i) in order]

    def finalize(self):
        nc = self.nc
        recs = _flatten(self.root)
        if self.schedule:
            recs = self._list_schedule(recs)
        for rec in recs:
            self._analyze(rec)
        instrs = self.instrs
        for ins in instrs:
            for d in ins.deps:
                instrs[d].signals = True
        groups = {}
        cnt = {e: 0 for e in ENGS}
        for ins in instrs:
            if ins.dma:
                lst = groups.setdefault(id(ins.group), [ins.group, 0])
                lst[1] += 1
                ins.ordinal = lst[1]
            elif ins.signals:
                cnt[ins.eng] += 1
                ins.ordinal = cnt[ins.eng]
        self.counts = cnt
        sems = {e: self.stack.enter_context(nc.semaphore(f"sem_{e}")) for e in ENGS}
        gsems = {gid: self.stack.enter_context(nc.semaphore(f"ds_{g.name}")) for gid, (g, n) in groups.items()}
        per_eng = {e: [i for i in instrs if i.eng == e] for e in ENGS}

        def run_engine(ename, eng):
            seen = {}
            for ins in per_eng[ename]:
                waits = {}
                for d in ins.deps:
                    di = instrs[d]
                    if di.dma:
                        key = ("g", id(di.group))
                        val = 16 * di.ordinal
                    else:
                        key = ("e", di.eng)
                        val = di.ordinal
                    if waits.get(key, 0) < val:
                        waits[key] = val
                for key, val in waits.items():
                    if seen.get(key, 0) >= val:
                        continue
                    seen[key] = val
                    sem = gsems[key[1]] if key[0] == "g" else sems[key[1]]
                    eng.wait_ge(sem, val)
                self.ctx.update(ins.ctx)
                bi = ins.fn(eng)
                if ins.dma:
                    bi.then_inc(gsems[id(ins.group)], 16)
                elif ins.signals:
                    bi.then_inc(sems[ename], 1)
            if ename == "sp":
                for gid, (g, n) in groups.items():
                    if seen.get(("g", gid), 0) < 16 * n:
                        eng.wait_ge(gsems[gid], 16 * n)

        with nc.Block() as block:
            @block.sync
            def _(e):
                run_engine("sp", e)

            @block.tensor
            def _(e):
                run_engine("pe", e)

            @block.scalar
            def _(e):
                run_engine("act", e)

            @block.vector
            def _(e):
                run_engine("dve", e)

            @block.gpsimd
            def _(e):
                run_engine("pool", e)


C_ID, C_UI, C_LS, C_ON, C_MB = 0, 128, 256, 384, 512
NEG = -30000.0


def make_consts():
    c = np.zeros((128, 512 + 3 * 272), np.float32)
    p = np.arange(128)[:, None]
    f = np.arange(128)[None, :]
    c[:, C_ID:C_ID + 128] = (p == f)
    c[:, C_UI:C_UI + 128] = (f >= p)
    c[:, C_LS:C_LS + 128] = (f < p)
    c[:, C_ON:C_ON + 128] = 1.0
    m = np.arange(16)[None, :]
    KW = 272
    mb0 = np.full((128, KW), NEG, np.float32)
    mb0[:, 256:272] = np.where(p >= 112 + m, 0.0, NEG)
    mb1 = np.full((128, KW), NEG, np.float32)
    mb1[:, 256:272] = 0.0
    mb1[:, 128:256] = np.where(f <= p, 0.0, NEG)
    mb2 = np.full((128, KW), NEG, np.float32)
    mb2[:, 256:272] = 0.0
    mb2[:, 0:128] = np.where(f > p, 0.0, NEG)
    mb2[:, 128:256] = np.where(f <= p, 0.0, NEG)
    c[:, C_MB:C_MB + KW] = mb0
    c[:, C_MB + KW:C_MB + 2 * KW] = mb1
    c[:, C_MB + 2 * KW:C_MB + 3 * KW] = mb2
    return c


def build(mode, ntiles=NT, nseq=2, dbg=99):
    do0 = mode in ("l0", "full")
    do1 = mode in ("l1", "full")
    nc = bass.Bass("TRN2", target_bir_lowering=False)

    def din(name, shape):
        return nc.dram_tensor(name, list(shape), F32, kind="ExternalInput").ap()

    x_d = din("x", [nseq, 2048, 1024])
    hin_d = din("hin", [nseq, NT * 128, 1024]) if mode == "l1" else None
    meta_d = din("meta", [16, 1024])
    consts_d = din("consts", [128, 512 + 816])
    w0_d = din("w_in0", [1024, 3592])
    pre0_d = din("pre0c", [128, 8])
    qkvw_d = din("qkvw", [128, 12, 4])
    alog_d = din("a_log", [4])
    dtb_d = din("dt_bias", [4])
    onorm_d = din("onorm", [128, 1])
    dww_d = din("dww", [128, 4, 31])
    dwb_d = din("dwb", [128, 4])
    lnw_d = din("lnw", [128, 4])
    lnb_d = din("lnb", [128, 4])
    wo0_d = din("w_out0", [1024, 1024])
    post0_d = din("post0", [1024])
    pre1_d = din("pre1c", [128, 8])
    w1_d = din("w_in1", [1024, 2304])
    sinks_d = din("sinks", [16])
    wo1_d = din("w_out1", [1024, 1024])
    post1_d = din("post1", [1024])
    if mode == "l0":
        out_d = nc.dram_tensor("hout", [nseq, NT * 128, 1024], F32, kind="ExternalOutput").ap()
    else:
        out_d = nc.dram_tensor("out", [nseq, 2048, 1024], F32, kind="ExternalOutput").ap()

    k = Kern(nc)
    with ExitStack() as st:
        k.stack = st
        k.psum_init()

        cst = k.sbuf("cst", [128, 385], F32)
        k.dma("sp", lambda e: e.dma_start(out=cst[:], in_=consts_d[:, 0:385]), w=[cst], group=cst)
        maskb = k.sbuf("maskb", [128, 816], BF16)
        k.dma("pool", lambda e: e.dma_start(out=maskb[:], in_=consts_d[:, 512:512 + 816]), w=[maskb], group=maskb)
        rowm = k.sbuf("rowm", [128, 1], F32)
        k.dve(lambda e: e.reduce_sum(out=rowm[:], in_=ident[:, 112:128], axis=AX.X), r=[cst], w=[rowm])
        ident = cst[:, C_ID:C_ID + 128]
        uincl = cst[:, C_UI:C_UI + 128]
        lstrict = cst[:, C_LS:C_LS + 128]
        ones = cst[:, C_ON:C_ON + 1].to_broadcast([128, 128])
        identb = k.sbuf("identb", [128, 128], BF16)
        k.dve(lambda e: e.tensor_copy(out=identb[:], in_=ident), r=[cst], w=[identb])
        prm = k.sbuf("prm", [128, 256], F32)
        P_PRE0, P_PRE1, P_QKVW, P_DWB, P_LNW, P_LNB, P_ON, P_ALOG, P_DTB, P_SINK, P_DWW = 0, 8, 16, 64, 68, 72, 76, 80, 84, 88, 104
        loads = [
            (prm[:, P_PRE0:P_PRE0 + 8], pre0_d), (prm[:, P_PRE1:P_PRE1 + 8], pre1_d),
            (prm[:, P_QKVW:P_QKVW + 48], qkvw_d.rearrange("p c j -> p (c j)")),
            (prm[:, P_DWB:P_DWB + 4], dwb_d), (prm[:, P_LNW:P_LNW + 4], lnw_d), (prm[:, P_LNB:P_LNB + 4], lnb_d),
            (prm[:, P_ON:P_ON + 1], onorm_d),
            (prm[:, P_ALOG:P_ALOG + 4], alog_d.partition_broadcast(128)),
            (prm[:, P_DTB:P_DTB + 4], dtb_d.partition_broadcast(128)),
            (prm[:, P_SINK:P_SINK + 16], sinks_d.partition_broadcast(128)),
            (prm[:, P_DWW:P_DWW + 124], dww_d.rearrange("p c j -> p (c j)")),
        ]
        for (o_, i_) in loads:
            k.dma("sp", lambda e, o_=o_, i_=i_: e.dma_start(out=o_, in_=i_), w=[prm], group=prm)
        prm2 = k.sbuf("prm2", [128, 16], F32)
        k.act(lambda e: e.activation(out=prm2[:, 0:4], in_=prm[:, P_ALOG:P_ALOG + 4], func=AF.Exp), r=[prm], w=[prm2])
        k.dve(lambda e: e.tensor_scalar(out=prm2[:, 0:4], in0=prm2[:, 0:4], scalar1=-1.0, scalar2=None, op0=ALU.mult), r=[prm2], w=[prm2])
        k.dve(lambda e: e.tensor_scalar(out=prm2[:, 4:5], in0=prm[:, P_ON:P_ON + 1], scalar1=0.5, scalar2=None, op0=ALU.mult), r=[prm, prm2], w=[prm2])
        PN0 = k.sbuf("PN0", [128, 1024], BF16)
        PN1 = k.sbuf("PN1", [128, 1024], BF16)

        def load_w(name, d_ap, ncols):
            W = k.sbuf(name, [128, 8, ncols], BF16)
            for kc in range(8):
                k.dma("pool", lambda e, kc=kc: e.dma_start(out=W[:, kc, :], in_=d_ap[kc * 128:(kc + 1) * 128, :]), w=[W], group=W)
            return W

        if do0:
            W0 = load_w("W0", w0_d, 3592)
            Wo0 = load_w("Wo0", wo0_d, 1024)
        if do1:
            W1 = load_w("W1", w1_d, 2304)
            Wo1 = load_w("Wo1", wo1_d, 1024)

        nth = 2 if mode == "full" else 1
        xt = Sw(k, "x", [k.sbuf(f"xt{i}", [128, 1024], F32) for i in range(nth)])
        hn = Sw(k, "th", [k.sbuf(f"hn{i}", [128, 1024], BF16) for i in range(nth)])
        hnT = Sw(k, "th", [k.sbuf(f"hnT{i}", [128, 8, 128], BF16) for i in range(nth)])
        col = Sw(k, "th", [k.sbuf(f"col{i}", [128, 64], F32) for i in range(nth)])
        tmpA = Sw(k, "th", [k.sbuf(f"tmpA{i}", [128, 512], F32) for i in range(nth)])
        ytmp = tmpA
        oTa = [k.sbuf(f"oTa{h}", [128, 128], BF16) for h in range(4)]
        oTb = k.sbuf("oTb", [128, 4, 128], BF16)
        oT1 = k.sbuf("oT1", [128, 8, 128], BF16)
        oT_l0 = [(oTa[h][:], oTa[h]) for h in range(4)] + [(oTb[:, c, :], oTb) for c in range(4)]
        oT_l1 = [(oT1[:, e_, :], oT1) for e_ in range(8)]
        for PN_, pd_ in ((PN0, post0_d), (PN1, post1_d)):
            stg = xt.bufs[0]
            k.dma("sp", lambda e, pd_=pd_, stg=stg: e.dma_start(out=stg[:], in_=pd_.partition_broadcast(128)), w=[stg], group=stg)
            k.dve(lambda e, PN_=PN_, stg=stg: e.tensor_scalar(out=PN_[:], in0=stg[:], scalar1=-1.0, scalar2=None, op0=ALU.add), r=[stg], w=[PN_])
        if do0:
            qkv_pre = k.sbuf("qkv_pre", [128, 12, 132], BF16)
            NDG = 6
            dgq = [k.sbuf(f"dgq{i}", [128, 128], BF16) for i in range(NDG)]
            dgc = [k.sbuf(f"dgc{i}", [128, 128], BF16) for i in range(NDG)]
            qkvs = k.sbuf("qkvs", [128, 12, 128], BF16)
            cin = k.sbuf("cin", [128, 4, 158], BF16)
            gza = k.sbuf("gza", [128, 4, 128], BF16)
            gzb = k.sbuf("gzb", [128, 4, 128], BF16)
            cc = k.sbuf("cc", [128, 4, 128], F32)
            ccsq = k.sbuf("ccsq", [128, 4, 128], F32)
            lnm = k.sbuf("lnm", [128, 2, 128], F32)
            gcol = k.sbuf("gcol", [128, 32], F32)
            Sst = [k.sbuf(f"Sst{h}", [128, 128], F32) for h in range(4)]
            Sb = [k.sbuf(f"Sb{h}", [128, 128], BF16) for h in range(4)]
            NBR = 4
            def brb(name, dt_):
                return Sw(k, "br", [k.sbuf(f"{name}_{i}", [128, 128], dt_) for i in range(NBR)])
            qn_tok = brb("qn_tok", BF16)
            kn_tok = brb("kn_tok", BF16)
            vb = brb("vb", BF16)
            kbg = brb("kbg", BF16)
            kdec = brb("kdec", BF16)
            qnT = brb("qnT", BF16)
            knT = brb("knT", BF16)
            qdecT = brb("qdecT", BF16)
            QKmT = brb("QKmT", BF16)
            TTb = brb("TTb", BF16)
            wTn = kn_tok
            junk = qdecT
            vnew = kbg
            on_b = qn_tok
            EX = brb("EX", F32)
            EG = brb("EG", F32)
            M1 = brb("M1", F32)
            M2 = brb("M2", F32)
            Pm = [brb("Pm0", F32), EX]
            PTm = [brb("PTm0", F32), M1]
            TTm = [M2, EG]
            colg = Sw(k, "br", [k.sbuf(f"colg{i}", [128, 16], F32) for i in range(NBR)])
        if do1:
            qT1 = k.sbuf("qT1", [128, 8, 128], BF16)
            kT1 = k.sbuf("kT1", [128, 2, 272], BF16)
            Vt = k.sbuf("Vt", [128, 3, 128], BF16)
            gz1 = k.sbuf("gz1", [128, 1024], BF16)
            Pb = Sw(k, "hp", [k.sbuf(f"Pb{i}", [128, 272], BF16) for i in range(2)])
            PTb = k.sbuf("PTb", [128, 3, 128], BF16)
            og = hn.bufs[-1]
            acol = Sw(k, "hp", [k.sbuf(f"acol{i}", [128, 16], F32) for i in range(2)])
            pe_ser = Buf(None, "pe_ser")

        def rstd_from_ss(ss_ap, dst_ap, n, bufs_r, buf_w, extra_bias=0.0):
            k.act(lambda e: e.activation(out=dst_ap, in_=ss_ap, func=AF.Ln, bias=EPS, scale=1.0 / n), r=bufs_r, w=[buf_w])
            k.act(lambda e: e.activation(out=dst_ap, in_=dst_ap, func=AF.Exp, bias=extra_bias, scale=-0.5), r=[buf_w], w=[buf_w])

        def norm_transpose(prec_col):
            k.act(lambda e: e.activation(out=hn[:], in_=xt[:], func=AF.Square, accum_out=col[:, 0:1]), r=[xt], w=[hn, col])
            rstd_from_ss(col[:, 0:1], col[:, 1:2], 1024.0, [col], col)
            k.dve(lambda e: e.tensor_scalar(out=hn[:], in0=xt[:], scalar1=col[:, 1:2], scalar2=None, op0=ALU.mult), r=[xt, col], w=[hn])
            p = k.psum()
            for kc in range(8):
                k.pe(lambda e, kc=kc: e.transpose(p.bf[:, kc * 128:(kc + 1) * 128], hn[:, kc * 128:(kc + 1) * 128], identb[:]), r=[hn, identb], w=[p])
            k.dve(lambda e: e.tensor_tensor(out=hnT[:], in0=p.bf[:, 0:1024].rearrange("p (k t) -> p k t", k=8),
                                            in1=prm[:, prec_col:prec_col + 8].unsqueeze(2).to_broadcast([128, 8, 128]), op=ALU.mult),
                  r=[p, prm], w=[hnT])

        def proj_fm(W, col0, p, slot):
            for kc in range(8):
                k.pe(lambda e, kc=kc: e.matmul(p[:, slot * 128:(slot + 1) * 128], lhsT=W[:, kc, col0:col0 + 128], rhs=hnT[:, kc, :],
                                               start=(kc == 0), stop=(kc == 7)), r=[W, hnT], w=[p])

        def proj_tm(W, col0, n, p, off=0):
            for kc in range(8):
                k.pe(lambda e, kc=kc: e.matmul(p[:, off:off + n], lhsT=hnT[:, kc, :], rhs=W[:, kc, col0:col0 + n],
                                               start=(kc == 0), stop=(kc == 7)), r=[W, hnT], w=[p])

        def out_proj_residual(Wo, PN, oTl):
            ps = [k.psum(), k.psum()]
            for h in range(2):
                for e_ in range(8):
                    k.pe(lambda e, e_=e_, h=h: e.matmul(ps[h][:, 0:512], lhsT=oTl[e_][0], rhs=Wo[:, e_, h * 512:(h + 1) * 512],
                                                         start=(e_ == 0), stop=(e_ == 7)), r=[oTl[e_][1], Wo], w=[ps[h]])
            for h in range(2):
                k.act(lambda e, h=h: e.activation(out=ytmp[:], in_=ps[h][:, 0:512], func=AF.Square, accum_out=col[:, 4 + h:5 + h]),
                      r=[ps[h]], w=[ytmp, col])
            k.dve(lambda e: e.tensor_tensor(out=col[:, 6:7], in0=col[:, 4:5], in1=col[:, 5:6], op=ALU.add), r=[col], w=[col])
            rstd_from_ss(col[:, 6:7], col[:, 7:8], 1024.0, [col], col)
            for h in range(2):
                k.dve(lambda e, h=h: e.scalar_tensor_tensor(out=ytmp[:], in0=ps[h][:, 0:512], scalar=col[:, 7:8], in1=PN[:, h * 512:(h + 1) * 512],
                                                             op0=ALU.mult, op1=ALU.mult), r=[ps[h], col, PN], w=[ytmp])
                k.dve(lambda e, h=h: e.scalar_tensor_tensor(out=xt[:, h * 512:(h + 1) * 512], in0=ps[h][:, 0:512], scalar=col[:, 7:8], in1=xt[:, h * 512:(h + 1) * 512],
                                                             op0=ALU.mult, op1=ALU.add), r=[ps[h], col, xt], w=[xt])
                k.dve(lambda e, h=h: e.tensor_tensor(out=xt[:, h * 512:(h + 1) * 512], in0=xt[:, h * 512:(h + 1) * 512], in1=ytmp[:], op=ALU.add),
                      r=[xt, ytmp], w=[xt])

        def silu2(dst_ap, src_ap, n_shape_tmp, r_bufs, w_buf, src_psum=None):
            k.act(lambda e: e.activation(out=n_shape_tmp, in_=src_ap, func=AF.Tanh, scale=0.5), r=r_bufs, w=[tmpA])
            k.dve(lambda e: e.scalar_tensor_tensor(out=dst_ap, in0=n_shape_tmp, scalar=1.0, in1=src_ap, op0=ALU.add, op1=ALU.mult),
                  r=[tmpA] + list(r_bufs), w=[w_buf])

        L0B = [0, 1, 2, 3, 4] if mode == "full" else [0, 1, 2, 3, 4]
        def layer0(t):
            norm_transpose(P_PRE0)
            for grp in range(3):
                p = k.psum()
                for s in range(4):
                    proj_fm(W0, (grp * 4 + s) * 128, p, s)
                k.act(lambda e, grp=grp, p=p: e.activation(out=qkv_pre[:, grp * 4:(grp + 1) * 4, 3:131],
                                                           in_=p[:, 0:512].rearrange("p (c t) -> p c t", c=4), func=AF.Copy),
                      r=[p], w=[qkv_pre])
            pv = k.psum()
            pg = k.psum()
            for s in range(4):
                proj_fm(W0, 2056 + s * 128, pv, s)
            for s in range(4):
                proj_fm(W0, 2568 + s * 128, pg, s)
            k.act(lambda e: e.activation(out=tmpA[:, 0:512], in_=pg[:, 0:512], func=AF.Tanh, scale=0.5), r=[pg], w=[tmpA])
            k.dve(lambda e: e.scalar_tensor_tensor(out=cin[:, :, 30:158], in0=tmpA[:, 0:512].rearrange("p (c t) -> p c t", c=4), scalar=1.0,
                                                   in1=pv[:, 0:512].rearrange("p (c t) -> p c t", c=4), op0=ALU.add, op1=ALU.mult),
                  r=[tmpA, pv], w=[cin])
            for (c0, gz) in ((1536, gza), (3080, gzb)):
                p = k.psum()
                for s in range(4):
                    proj_fm(W0, c0 + s * 128, p, s)
                silu2(gz[:].rearrange("p c t -> p (c t)"), p[:, 0:512], tmpA[:, 0:512], [p], gz)
            pba = k.psum()
            proj_tm(W0, 2048, 8, pba)
            k.act(lambda e: e.activation(out=gcol[:, 0:4], in_=pba[:, 0:4], func=AF.Tanh, scale=0.5), r=[pba], w=[gcol])
            k.dve(lambda e: e.tensor_scalar(out=gcol[:, 0:4], in0=gcol[:, 0:4], scalar1=1.0, scalar2=0.5, op0=ALU.add, op1=ALU.mult), r=[gcol], w=[gcol])
            k.dve(lambda e: e.tensor_scalar(out=gcol[:, 4:8], in0=gcol[:, 0:4], scalar1=0.5, scalar2=None, op0=ALU.mult), r=[gcol], w=[gcol])
            k.dve(lambda e: e.tensor_tensor(out=gcol[:, 8:12], in0=pba[:, 4:8], in1=prm[:, P_DTB:P_DTB + 4], op=ALU.add), r=[pba, prm], w=[gcol])
            k.act(lambda e: e.activation(out=gcol[:, 8:12], in_=gcol[:, 8:12], func=AF.Exp), r=[gcol], w=[gcol])
            k.act(lambda e: e.activation(out=gcol[:, 8:12], in_=gcol[:, 8:12], func=AF.Ln, bias=1.0), r=[gcol], w=[gcol])
            k.dve(lambda e: e.tensor_tensor(out=gcol[:, 8:12], in0=gcol[:, 8:12], in1=prm2[:, 0:4], op=ALU.mult), r=[gcol, prm2], w=[gcol])
            pG = k.psum()
            k.pe(lambda e: e.matmul(pG[:, 0:4], lhsT=uincl, rhs=gcol[:, 8:12], start=True, stop=True), r=[cst, gcol], w=[pG])
            k.pe(lambda e: e.matmul(pG[:, 4:8], lhsT=ones, rhs=gcol[:, 8:12], start=True, stop=True), r=[cst, gcol], w=[pG])
            k.dve(lambda e: e.tensor_copy(out=gcol[:, 12:20], in_=pG[:, 0:8]), r=[pG], w=[gcol])
            k.act(lambda e: e.activation(out=gcol[:, 20:24], in_=gcol[:, 12:16], func=AF.Exp), r=[gcol], w=[gcol])
            k.dve(lambda e: e.tensor_tensor(out=gcol[:, 20:24], in0=gcol[:, 20:24], in1=gcol[:, 0:4], op=ALU.mult), r=[gcol], w=[gcol])
            k.dve(lambda e: e.tensor_tensor(out=gcol[:, 24:28], in0=gcol[:, 16:20], in1=gcol[:, 12:16], op=ALU.subtract), r=[gcol], w=[gcol])
            k.act(lambda e: e.activation(out=gcol[:, 24:28], in_=gcol[:, 24:28], func=AF.Exp), r=[gcol], w=[gcol])
            k.act(lambda e: e.activation(out=gcol[:, 28:32], in_=gcol[:, 16:20], func=AF.Exp), r=[gcol], w=[gcol])

            for g3 in range(3):
                pc = k.psum()
                for c4 in range(4):
                    c = g3 * 4 + c4
                    for j in range(4):
                        dg = dgq[(c * 4 + j) % NDG]
                        wcol = prm[:, P_QKVW + c * 4 + j:P_QKVW + c * 4 + j + 1]
                        k.dve(lambda e, dg=dg, wcol=wcol: e.tensor_scalar(out=dg[:], in0=identb[:], scalar1=wcol, scalar2=None, op0=ALU.mult), r=[identb, prm], w=[dg])
                        k.pe(lambda e, dg=dg, c=c, c4=c4, j=j, pc=pc: e.matmul(pc[:, c4 * 128:(c4 + 1) * 128], lhsT=dg[:], rhs=qkv_pre[:, c, j:j + 128],
                                                                              start=(j == 0), stop=(j == 3)), r=[dg, qkv_pre], w=[pc])
                silu2(qkvs[:, g3 * 4:(g3 + 1) * 4, :].rearrange("p c t -> p (c t)"), pc[:, 0:512], tmpA[:, 0:512], [pc], qkvs)
            k.pool(lambda e: e.tensor_copy(out=qkv_pre[:, :, 0:3], in_=qkv_pre[:, :, 128:131]), r=[qkv_pre], w=[qkv_pre])

            def conformer():
                pcv = k.psum()
                for c in range(4):
                    for j in range(31):
                        dg = dgc[(c * 31 + j) % NDG]
                        wcol = prm[:, P_DWW + c * 31 + j:P_DWW + c * 31 + j + 1]
                        k.dve(lambda e, dg=dg, wcol=wcol: e.tensor_scalar(out=dg[:], in0=identb[:], scalar1=wcol, scalar2=None, op0=ALU.mult), r=[identb, prm], w=[dg])
                        k.pe(lambda e, dg=dg, c=c, j=j: e.matmul(pcv[:, c * 128:(c + 1) * 128], lhsT=dg[:], rhs=cin[:, c, j:j + 128],
                                                                 start=(j == 0), stop=(j == 30)), r=[dg, cin], w=[pcv])
                for c in range(4):
                    k.dve(lambda e, c=c: e.tensor_scalar(out=cc[:, c, :], in0=pcv[:, c * 128:(c + 1) * 128], scalar1=0.5, scalar2=prm[:, P_DWB + c:P_DWB + c + 1],
                                                         op0=ALU.mult, op1=ALU.add), r=[pcv, prm], w=[cc])
                k.pool(lambda e: e.tensor_copy(out=cin[:, :, 0:30], in_=cin[:, :, 128:158]), r=[cin], w=[cin])
                k.pool(lambda e: e.tensor_tensor(out=ccsq[:], in0=cc[:], in1=cc[:], op=ALU.mult), r=[cc], w=[ccsq])
                pm = k.psum()
                for c in range(4):
                    k.pe(lambda e, c=c: e.matmul(pm[:, 0:128], lhsT=ones, rhs=cc[:, c, :], start=(c == 0), stop=(c == 3)), r=[cst, cc], w=[pm])
                for c in range(4):
                    k.pe(lambda e, c=c: e.matmul(pm[:, 128:256], lhsT=ones, rhs=ccsq[:, c, :], start=(c == 0), stop=(c == 3)), r=[cst, ccsq], w=[pm])
                k.act(lambda e: e.activation(out=lnm[:, 0, :], in_=pm[:, 0:128], func=AF.Copy, scale=1.0 / 512), r=[pm], w=[lnm])
                k.dve(lambda e: e.tensor_tensor(out=lnm[:, 1, :], in0=lnm[:, 0, :], in1=lnm[:, 0, :], op=ALU.mult), r=[lnm], w=[lnm])
                k.dve(lambda e: e.scalar_tensor_tensor(out=lnm[:, 1, :], in0=pm[:, 128:256], scalar=1.0 / 512, in1=lnm[:, 1, :], op0=ALU.mult, op1=ALU.subtract),
                      r=[pm, lnm], w=[lnm])
                k.act(lambda e: e.activation(out=lnm[:, 1, :], in_=lnm[:, 1, :], func=AF.Ln, bias=EPS), r=[lnm], w=[lnm])
                k.act(lambda e: e.activation(out=lnm[:, 1, :], in_=lnm[:, 1, :], func=AF.Exp, scale=-0.5), r=[lnm], w=[lnm])
                k.dve(lambda e: e.tensor_tensor(out=cc[:], in0=cc[:], in1=lnm[:, 0, :].unsqueeze(1).to_broadcast([128, 4, 128]), op=ALU.subtract), r=[cc, lnm], w=[cc])
                k.dve(lambda e: e.tensor_tensor(out=cc[:], in0=cc[:], in1=lnm[:, 1, :].unsqueeze(1).to_broadcast([128, 4, 128]), op=ALU.mult), r=[cc, lnm], w=[cc])
                for c in range(4):
                    k.dve(lambda e, c=c: e.tensor_scalar(out=cc[:, c, :], in0=cc[:, c, :], scalar1=prm[:, P_LNW + c:P_LNW + c + 1], scalar2=prm[:, P_LNB + c:P_LNB + c + 1],
                                                         op0=ALU.mult, op1=ALU.add), r=[cc, prm], w=[cc])
                silu2(ccsq[:].rearrange("p c t -> p (c t)"), cc[:].rearrange("p c t -> p (c t)"), tmpA[:, 0:512], [cc], ccsq)
                k.dve(lambda e: e.scalar_tensor_tensor(out=oTb[:], in0=ccsq[:], scalar=0.25, in1=gzb[:], op0=ALU.mult, op1=ALU.mult), r=[ccsq, gzb], w=[oTb])

            def gdn_head(h):
                pt = k.psum()
                for i_, cidx in enumerate((h, 4 + h, 8 + h)):
                    k.pe(lambda e, i_=i_, cidx=cidx: e.transpose(pt.bf[:, i_ * 128:(i_ + 1) * 128], qkvs[:, cidx, :], identb[:]), r=[qkvs, identb], w=[pt])
                for i_, dst, xb in ((0, qn_tok, math.log(128.0 ** -0.5)), (1, kn_tok, 0.0)):
                    k.act(lambda e, i_=i_: e.activation(out=junk[:], in_=pt.bf[:, i_ * 128:(i_ + 1) * 128], func=AF.Square, accum_out=colg[:, 8 + i_:9 + i_]),
                          r=[pt], w=[junk, colg])
                    k.act(lambda e, i_=i_: e.activation(out=colg[:, 10 + i_:11 + i_], in_=colg[:, 8 + i_:9 + i_], func=AF.Ln, bias=4 * EPS), r=[colg], w=[colg])
                    k.act(lambda e, i_=i_, xb=xb: e.activation(out=colg[:, 10 + i_:11 + i_], in_=colg[:, 10 + i_:11 + i_], func=AF.Exp, bias=xb, scale=-0.5), r=[colg], w=[colg])
                    k.dve(lambda e, i_=i_, dst=dst: e.tensor_scalar(out=dst[:], in0=pt.bf[:, i_ * 128:(i_ + 1) * 128], scalar1=colg[:, 10 + i_:11 + i_], scalar2=None, op0=ALU.mult),
                          r=[pt, colg], w=[dst])
                k.dve(lambda e, h=h: e.tensor_scalar(out=vb[:], in0=pt.bf[:, 256:384], scalar1=gcol[:, 4 + h:5 + h], scalar2=None, op0=ALU.mult), r=[pt, gcol], w=[vb])
                k.dve(lambda e, h=h: e.tensor_scalar(out=kbg[:], in0=kn_tok[:], scalar1=gcol[:, 20 + h:21 + h], scalar2=None, op0=ALU.mult), r=[kn_tok, gcol], w=[kbg])
                k.dve(lambda e, h=h: e.tensor_scalar(out=kdec[:], in0=kn_tok[:], scalar1=gcol[:, 24 + h:25 + h], scalar2=None, op0=ALU.mult), r=[kn_tok, gcol], w=[kdec])
                pt2 = k.psum()
                k.pe(lambda e: e.transpose(pt2.bf[:, 0:128], qn_tok[:], identb[:]), r=[qn_tok, identb], w=[pt2])
                k.pe(lambda e: e.transpose(pt2.bf[:, 128:256], kn_tok[:], identb[:]), r=[kn_tok, identb], w=[pt2])
                k.act(lambda e: e.activation(out=qnT[:], in_=pt2.bf[:, 0:128], func=AF.Copy), r=[pt2], w=[qnT])
                k.act(lambda e: e.activation(out=knT[:], in_=pt2.bf[:, 128:256], func=AF.Copy), r=[pt2], w=[knT])
                pg_ = k.psum()
                k.pe(lambda e, h=h: e.matmul(pg_[:, 0:128], lhsT=gcol[:, 8 + h:9 + h].to_broadcast([128, 128]), rhs=uincl, start=True, stop=True), r=[gcol, cst], w=[pg_])
                k.dve(lambda e, h=h: e.tensor_scalar(out=EX[:], in0=pg_[:, 0:128], scalar1=gcol[:, 12 + h:13 + h], scalar2=None, op0=ALU.subtract), r=[pg_, gcol], w=[EX])
                k.act(lambda e: e.activation(out=EX[:], in_=EX[:], func=AF.Abs), r=[EX], w=[EX])
                k.act(lambda e: e.activation(out=EX[:], in_=EX[:], func=AF.Exp, scale=-1.0), r=[EX], w=[EX])
                k.act(lambda e: e.activation(out=EG[:], in_=pg_[:, 0:128], func=AF.Exp), r=[pg_], w=[EG])
                k.pool(lambda e: e.tensor_tensor(out=M1[:], in0=EX[:], in1=lstrict, op=ALU.mult), r=[EX, cst], w=[M1])
                k.dve(lambda e: e.tensor_tensor(out=M2[:], in0=EX[:], in1=uincl, op=ALU.mult), r=[EX, cst], w=[M2])
                k.dve(lambda e: e.tensor_tensor(out=qdecT[:], in0=qnT[:], in1=EG[:], op=ALU.mult), r=[qnT, EG], w=[qdecT])
                pk = k.psum()
                k.pe(lambda e: e.matmul(pk[:, 0:128], lhsT=knT[:], rhs=knT[:], start=True, stop=True), r=[knT], w=[pk])
                k.pe(lambda e: e.matmul(pk[:, 128:256], lhsT=knT[:], rhs=qnT[:], start=True, stop=True), r=[knT, qnT], w=[pk])
                k.dve(lambda e, h=h: e.scalar_tensor_tensor(out=Pm[0][:], in0=pk[:, 0:128], scalar=gcol[:, h:h + 1], in1=M1[:], op0=ALU.mult, op1=ALU.mult),
                      r=[pk, gcol, M1], w=[Pm[0]])
                k.dve(lambda e: e.tensor_tensor(out=QKmT[:], in0=pk[:, 128:256], in1=M2[:], op=ALU.mult), r=[pk, M2], w=[QKmT])
                pa = k.psum()
                k.pe(lambda e: e.transpose(pa[:, 0:128], Pm[0][:], ident), r=[Pm[0], cst], w=[pa])
                k.act(lambda e: e.activation(out=PTm[0][:], in_=pa[:, 0:128], func=AF.Copy), r=[pa], w=[PTm[0]])
                k.dve(lambda e: e.tensor_tensor(out=TTm[0][:], in0=ident, in1=pa[:, 0:128], op=ALU.subtract), r=[cst, pa], w=[TTm[0]])
                cur = 0
                for it in range(6):
                    nxt = 1 - cur
                    last = (it == 5)
                    pq = k.psum()
                    k.pe(lambda e, cur=cur: e.matmul(pq[:, 0:128], lhsT=PTm[cur][:], rhs=Pm[cur][:], start=True, stop=True), r=[PTm[cur], Pm[cur]], w=[pq])
                    if not last:
                        k.pe(lambda e, cur=cur: e.matmul(pq[:, 128:256], lhsT=Pm[cur][:], rhs=PTm[cur][:], start=True, stop=True), r=[PTm[cur], Pm[cur]], w=[pq])
                    k.act(lambda e, nxt=nxt: e.activation(out=Pm[nxt][:], in_=pq[:, 0:128], func=AF.Copy), r=[pq], w=[Pm[nxt]])
                    if not last:
                        k.dve(lambda e, nxt=nxt: e.tensor_copy(out=PTm[nxt][:], in_=pq[:, 128:256]), r=[pq], w=[PTm[nxt]])
                    pt3 = k.psum()
                    k.pe(lambda e, cur=cur, nxt=nxt: e.matmul(pt3[:, 0:128], lhsT=Pm[nxt][:], rhs=TTm[cur][:], start=True, stop=True), r=[Pm[nxt], TTm[cur]], w=[pt3])
                    if last:
                        k.dve(lambda e, cur=cur: e.tensor_tensor(out=TTb[:], in0=pt3[:, 0:128], in1=TTm[cur][:], op=ALU.add), r=[pt3, TTm[cur]], w=[TTb])
                    else:
                        k.dve(lambda e, cur=cur, nxt=nxt: e.tensor_tensor(out=TTm[nxt][:], in0=pt3[:, 0:128], in1=TTm[cur][:], op=ALU.add), r=[pt3, TTm[cur]], w=[TTm[nxt]])
                    cur = nxt
                pw = k.psum()
                k.pe(lambda e: e.matmul(pw[:, 0:128], lhsT=kbg[:], rhs=TTb[:], start=True, stop=True), r=[kbg, TTb], w=[pw])
                k.act(lambda e: e.activation(out=wTn[:], in_=pw[:, 0:128], func=AF.Copy, scale=-1.0), r=[pw], w=[wTn])
                k.pe(lambda e: e.matmul(pw[:, 128:256], lhsT=TTb[:], rhs=vb[:], start=True, stop=False), r=[TTb, vb], w=[pw])
                k.pe(lambda e, h=h: e.matmul(pw[:, 128:256], lhsT=wTn[:], rhs=Sb[h][:], start=False, stop=True), r=[wTn, Sb[h]], w=[pw])
                k.act(lambda e: e.activation(out=vnew[:], in_=pw[:, 128:256], func=AF.Copy), r=[pw], w=[vnew])
                po = k.psum()
                k.pe(lambda e, h=h: e.matmul(po[:, 0:128], lhsT=qdecT[:], rhs=Sb[h][:], start=True, stop=False), r=[qdecT, Sb[h]], w=[po])
                k.pe(lambda e: e.matmul(po[:, 0:128], lhsT=QKmT[:], rhs=vnew[:], start=False, stop=True), r=[QKmT, vnew], w=[po])
                k.pe(lambda e: e.matmul(po[:, 128:256], lhsT=kdec[:], rhs=vnew[:], start=True, stop=True), r=[kdec, vnew], w=[po])
                k.dve(lambda e, h=h: e.scalar_tensor_tensor(out=Sst[h][:], in0=Sst[h][:], scalar=gcol[:, 28 + h:29 + h], in1=po[:, 128:256],
                                                             op0=ALU.mult, op1=ALU.add), r=[Sst[h], gcol, po], w=[Sst[h]])
                k.act(lambda e, h=h: e.activation(out=Sb[h][:], in_=Sst[h][:], func=AF.Copy), r=[Sst[h]], w=[Sb[h]])
                k.act(lambda e: e.activation(out=junk[:], in_=po[:, 0:128], func=AF.Square, accum_out=colg[:, 12:13]), r=[po], w=[junk, colg])
                rstd_from_ss(colg[:, 12:13], colg[:, 13:14], 128.0, [colg], colg)
                k.dve(lambda e: e.tensor_scalar(out=on_b[:], in0=po[:, 0:128], scalar1=colg[:, 13:14], scalar2=None, op0=ALU.mult), r=[po, colg], w=[on_b])
                pot = k.psum()
                k.pe(lambda e: e.transpose(pot.bf[:, 0:128], on_b[:], identb[:]), r=[on_b, identb], w=[pot])
                k.dve(lambda e, h=h: e.scalar_tensor_tensor(out=oTa[h][:], in0=pot.bf[:, 0:128], scalar=prm2[:, 4:5], in1=gza[:, h, :], op0=ALU.mult, op1=ALU.mult),
                      r=[pot, prm2, gza], w=[oTa[h]])
            par = k.fork()
            for h_ in range(4):
                with k.branch(par, L0B[h_:h_ + 1], offset=0.05 * h_):
                    k.ctx["br"] = h_
                    gdn_head(h_)
            with k.branch(par, L0B[4:5], offset=CF_OFF, span=CF_SPAN):
                k.ctx["br"] = 0
                conformer()
            k.ctx["br"] = 0
            out_proj_residual(Wo0, PN0, oT_l0)

        def layer1(t):
            k.pin_pe = ("norm" not in L1_FLOAT)
            norm_transpose(P_PRE1)
            k.pin_pe = ("proj" not in L1_FLOAT)
            if dbg < 1.1:
                return
            for grp in range(2):
                p = k.psum()
                for s in range(4):
                    proj_fm(W1, (grp * 4 + s) * 128, p, s)
                k.act(lambda e, grp=grp, p=p: e.activation(out=qT1[:, grp * 4:(grp + 1) * 4, :], in_=p[:, 0:512].rearrange("p (c t) -> p c t", c=4),
                                                           func=AF.Copy, scale=0.125), r=[p], w=[qT1])
            if dbg < 1.3:
                return
            pkv = k.psum()
            proj_fm(W1, 1024, pkv, 0)
            if dbg != 1.36:
                k.act(lambda e: e.activation(out=kT1[0:64, 0, 128:256], in_=pkv[0:64, 0:128], func=AF.Copy), r=[pkv], w=[kT1])
                k.act(lambda e: e.activation(out=kT1[64:128, 1, 128:256], in_=pkv[64:128, 0:128], func=AF.Copy), r=[pkv], w=[kT1])
            if dbg == 1.35:
                return
            pkv2 = k.psum()
            proj_tm(W1, 1152, 128, pkv2, off=0)
            if dbg == 1.37:
                return
            k.dve(lambda e: e.tensor_scalar(out=Vt[:, 1, :], in0=pkv2[:, 0:128], scalar1=0.5, scalar2=None, op0=ALU.mult), r=[pkv2], w=[Vt])
            if dbg < 1.5:
                return
            if t == 0:
                k.dve(lambda e: e.tensor_copy(out=kT1[:, :, 256:272], in_=kT1[:, :, 240:256]), r=[kT1], w=[kT1])
                k.dve(lambda e: e.tensor_scalar(out=Vt[:, 2, :], in0=Vt[:, 1, :], scalar1=rowm[:, 0:1], scalar2=None, op0=ALU.mult), r=[Vt, rowm], w=[Vt])
            if dbg < 1.7:
                return
            for hh in range(2):
                p = k.psum()
                proj_tm(W1, 1280 + hh * 512, 512, p)
                silu2(gz1[:, hh * 512:(hh + 1) * 512], p[:, 0:512], tmpA[:, 0:512], [p], gz1)
            mbo = 272 * min(t, 2)
            k.pin_pe = ("attn" not in L1_FLOAT)
            if dbg < 3:
                return
            for c in range(8 if not (3 <= dbg < 4) else max(1, int(round((dbg - 3) * 10)))):
                for half in range(2):
                    hc = 2 * c + half
                    k.ctx["hp"] = hc % 2
                    lo, hi = half * 64, (half + 1) * 64
                    ps_ = k.psum()
                    k.pin_pe = ("attn" not in L1_FLOAT) and ("attn_s" not in L1_FLOAT)
                    k.pe(lambda e, c=c, half=half: e.matmul(ps_[:, 0:272], lhsT=qT1[:, c, :], rhs=kT1[:, half, :], start=True, stop=False),
                         r=[qT1, kT1], w=[ps_])
                    k.pe(lambda e: e.matmul(ps_[:, 0:272], lhsT=identb[:], rhs=maskb[:, mbo:mbo + 272], start=False, stop=True), r=[identb, maskb], w=[ps_])
                    k.dve(lambda e: e.reduce_max(out=acol[:, 0:1], in_=ps_[:, 0:272], axis=AX.X), r=[ps_], w=[acol])
                    k.dve(lambda e, hc=hc: e.tensor_scalar(out=acol[:, 2:3], in0=acol[:, 0:1], scalar1=prm[:, P_SINK + hc:P_SINK + hc + 1], scalar2=-1.0, op0=ALU.max, op1=ALU.mult),
                          r=[acol, prm], w=[acol])
                    k.act(lambda e: e.activation(out=Pb[:], in_=ps_[:, 0:272], func=AF.Exp, bias=acol[:, 2:3], accum_out=acol[:, 3:4]), r=[ps_, acol], w=[Pb, acol])
                    k.act(lambda e, hc=hc: e.activation(out=acol[:, 4:5], in_=prm[:, P_SINK + hc:P_SINK + hc + 1], func=AF.Exp, bias=acol[:, 2:3]), r=[prm, acol], w=[acol])
                    k.dve(lambda e: e.tensor_tensor(out=acol[:, 5:6], in0=acol[:, 3:4], in1=acol[:, 4:5], op=ALU.add), r=[acol], w=[acol])
                    k.dve(lambda e: e.reciprocal(out=acol[:, 7:8], in_=acol[:, 5:6]), r=[acol], w=[acol])
                    ptp = k.psum()
                    k.pin_pe = ("attn" not in L1_FLOAT) and ("attn_t" not in L1_FLOAT)
                    for g_, c0_ in enumerate((0, 128, 144)):
                        k.pe(lambda e, g_=g_, c0_=c0_: e.transpose(ptp.bf[:, g_ * 128:(g_ + 1) * 128], Pb[:, c0_:c0_ + 128], identb[:]), r=[Pb, identb], w=[ptp])
                    k.act(lambda e: e.activation(out=PTb[:], in_=ptp.bf[:, 0:384].rearrange("p (c t) -> p c t", c=3), func=AF.Copy), r=[ptp], w=[PTb])
                    pov = k.psum()
                    k.pin_pe = ("attn" not in L1_FLOAT) and ("attn_pv" not in L1_FLOAT)
                    for g_ in range(3):
                        k.pe(lambda e, g_=g_, lo=lo, hi=hi: e.matmul(pov[:, 0:64], lhsT=PTb[:, g_, :], rhs=Vt[:, g_, lo:hi], start=(g_ == 0), stop=(g_ == 2)), r=[PTb, Vt], w=[pov])
                    k.dve(lambda e, hc=hc: e.scalar_tensor_tensor(out=og[:, hc * 64:(hc + 1) * 64], in0=pov[:, 0:64], scalar=acol[:, 7:8], in1=gz1[:, hc * 64:(hc + 1) * 64],
                                                                   op0=ALU.mult, op1=ALU.mult), r=[pov, acol, gz1], w=[og])
            if dbg < 5:
                return
            k.pool(lambda e: e.tensor_copy(out=kT1[:, :, 0:128], in_=kT1[:, :, 128:256]), r=[kT1], w=[kT1])
            k.pool(lambda e: e.tensor_copy(out=Vt[:, 0, :], in_=Vt[:, 1, :]), r=[Vt], w=[Vt])
            k.pin_pe = ("tail" not in L1_FLOAT)
            pt_ = k.psum()
            for kc in range(8):
                k.pe(lambda e, kc=kc: e.transpose(pt_.bf[:, kc * 128:(kc + 1) * 128], og[:, kc * 128:(kc + 1) * 128], identb[:]), r=[og, identb], w=[pt_])
            k.act(lambda e: e.activation(out=oT1[:], in_=pt_.bf[:, 0:1024].rearrange("p (k t) -> p k t", k=8), func=AF.Copy), r=[pt_], w=[oT1])
            out_proj_residual(Wo1, PN1, oT_l1)

        steps = [(s_, t_) for s_ in range(nseq) for t_ in range(ntiles)]

        def load(i):
            s_, t = steps[i]
            if mode == "l1":
                k.dma("sp", lambda e: e.dma_start(out=xt[:], in_=hin_d[s_, t * 128:(t + 1) * 128, :]), w=[xt], group=xt)
            elif t == 0:
                k.pool(lambda e: e.memset(xt[:], 0.0), w=[xt])
                k.dma("sp", lambda e: e.dma_start(out=xt[112:128, :], in_=meta_d), w=[xt], group=xt)
            else:
                k.dma("sp", lambda e: e.dma_start(out=xt[:], in_=x_d[s_, (t - 1) * 128:t * 128, :]), w=[xt], group=xt)

        def store(i):
            s_, t = steps[i]
            if mode == "l0":
                k.dma("sp", lambda e: e.dma_start(out=out_d[s_, t * 128:(t + 1) * 128, :], in_=xt[:]), r=[xt], group=xt)
            elif t >= 1:
                k.dma("sp", lambda e: e.dma_start(out=out_d[s_, (t - 1) * 128:t * 128, :], in_=xt[:]), r=[xt], group=xt)

        def do_l0(i, th):
            s_, t = steps[i]
            k.ctx["th"] = th
            k.ctx["x"] = i % nth
            if t == 0:
                k.pool(lambda e: e.memset(qkv_pre[:, :, 0:3], 0.0), w=[qkv_pre])
                k.pool(lambda e: e.memset(cin[:, :, 0:30], 0.0), w=[cin])
                for h_ in range(4):
                    k.pool(lambda e, h_=h_: e.memset(Sst[h_][:], 0.0), w=[Sst[h_]])
                    k.pool(lambda e, h_=h_: e.memset(Sb[h_][:], 0.0), w=[Sb[h_]])
            load(i)
            layer0(t)
            if mode == "l0":
                store(i)

        def do_l1(i, th):
            s_, t = steps[i]
            k.pin_pe = True
            k.ctx["th"] = th
            k.ctx["x"] = i % nth
            if t == 0:
                k.pool(lambda e: e.memset(kT1[:], 0.0), w=[kT1])
                k.pool(lambda e: e.memset(Vt[:], 0.0), w=[Vt])
                k.pool(lambda e: e.memset(PTb[:], 0.0), w=[PTb])
            if mode == "l1":
                load(i)
            layer1(t)
            k.pin_pe = False
            store(i)

        n = len(steps)
        if mode == "full":
            BA, BB = L0B, [5, 6, 7]
            for i in range(n + 1):
                par = k.fork()
                if i < n:
                    with k.branch(par, BA):
                        do_l0(i, 0)
                if i >= 1:
                    with k.branch(par, BB, offset=L1_OFF, span=L1_SPAN):
                        do_l1(i - 1, 1)
        else:
            for i in range(n):
                if do0:
                    do_l0(i, 0)
                else:
                    do_l1(i, 0)
        k.finalize()
    return nc, k


_PERM = np.array([h * 64 + d for c in range(8) for h in (c, 8 + c) for d in range(64)])
_HPERM = np.array([h for c in range(8) for h in (c, 8 + c)])


def _common_inputs(inp):
    f = lambda a: np.ascontiguousarray(np.asarray(a, dtype=np.float32))
    w1 = f(inp["odd_w_in"])[0]
    w1p = np.concatenate([w1[:, 0:1024][:, _PERM], w1[:, 1024:1280], w1[:, 1280:2304][:, _PERM]], axis=1)
    d = {
        "meta": f(inp["meta_tokens"]),
        "consts": make_consts(),
        "w_in0": f(inp["even_w_in"])[0],
        "pre0c": f(f(inp["even_pre_norm"])[0].reshape(8, 128).T),
        "qkvw": f(f(inp["even_qkv_conv"])[0].T.reshape(12, 128, 4).transpose(1, 0, 2)),
        "a_log": f(inp["even_a_log"])[0],
        "dt_bias": f(inp["even_dt_bias"])[0],
        "onorm": f(f(inp["even_out_norm"])[0].reshape(128, 1)),
        "dww": f(f(inp["even_dw_conv"])[0].T.reshape(4, 128, 31).transpose(1, 0, 2)),
        "dwb": f(f(inp["even_dw_bias"])[0].reshape(4, 128).T),
        "lnw": f(f(inp["even_ln_w"])[0].reshape(4, 128).T),
        "lnb": f(f(inp["even_ln_b"])[0].reshape(4, 128).T),
        "w_out0": f(inp["even_w_out"])[0],
        "post0": f(inp["even_post_norm"])[0],
        "pre1c": f(f(inp["odd_pre_norm"])[0].reshape(8, 128).T),
        "w_in1": f(w1p),
        "sinks": f(f(inp["odd_sinks"])[0][_HPERM]),
        "w_out1": f(f(inp["odd_w_out"])[0][_PERM, :]),
        "post1": f(inp["odd_post_norm"])[0],
    }
    return d


_CACHE = {}


def _get(mode):
    if mode not in _CACHE:
        _CACHE[mode] = build(mode)[0]
    return _CACHE[mode]


def kernel(**inputs):
    x = np.ascontiguousarray(np.asarray(inputs["x"], dtype=np.float32))
    common = _common_inputs(inputs)
    nc = _get("full")
    in_maps = []
    for c in range(NCORES):
        m = dict(common)
        m["x"] = x[2 * c:2 * c + 2]
        in_maps.append(m)
    res = run_bass_kernel_spmd(nc, in_maps, core_ids=list(range(NCORES)))
    return np.concatenate([r["out"] for r in res.results], axis=0)
```

```python
import math
import numpy as np
from contextlib import ExitStack
import concourse.bass as bass
import concourse.mybir as mybir
from concourse.bass_utils import run_bass_kernel_spmd

F32 = mybir.dt.float32
BF16 = mybir.dt.bfloat16
ALU = mybir.AluOpType
AF = mybir.ActivationFunctionType
AX = mybir.AxisListType

ENGS = ("pe", "act", "dve", "pool", "sp")
NCORES = 8
L1_FLOAT = ("norm","proj","attn_s","tail")
L1_OFF, L1_SPAN, CF_OFF, CF_SPAN = 0.06, 1.0, 0.12, 1.0
NT = 17
EPS = 1e-6


class Buf:
    def __init__(self, ap, name):
        self.ap = ap
        self.name = name
        self.last_w = None
        self.readers = {}
        self.const = False
        self.gen = 0

    def __getitem__(self, key):
        return self.ap[key]


class PV:
    def __init__(self, bank):
        self.b = bank
        self.gen = bank.gen

    def _chk(self):
        assert self.b.gen == self.gen, f"stale psum handle {self.b.name}"

    def __getitem__(self, key):
        return self.b.ap[key]

    @property
    def bf(self):
        return self.b.bfv


class Sw:
    def __init__(self, k, key, bufs):
        self.k = k
        self.key = key
        self.bufs = bufs

    def cur(self):
        return self.bufs[self.k.ctx[self.key]]

    def __getitem__(self, key):
        return self.cur().ap[key]


class Instr:
    __slots__ = ("id", "eng", "fn", "deps", "dma", "group", "signals", "ordinal", "ctx")

    def __init__(self, id, eng, fn, deps, dma, group):
        self.ctx = None
        self.id = id
        self.eng = eng
        self.fn = fn
        self.deps = deps
        self.dma = dma
        self.group = group
        self.signals = False
        self.ordinal = 0


class _FakeIns:
    def then_inc(self, *a, **k):
        return self


class _FakeEng:
    def __init__(self):
        self.info = None

    def __getattr__(self, name):
        def f(*args, **kw):
            out = kw.get("out", args[0] if args else None)
            self.info = (name, out, kw, args)
            return _FakeIns()
        return f


def _est_cost(eng, fn, dma):
    fe = _FakeEng()
    try:
        fn(fe)
        name, out, kw, args = fe.info
        free = out.free_size()
        dt_ = out.dtype
    except Exception:
        return 0.3, 0.3, None
    if dma:
        nbytes = free * out.partition_size() * (2 if dt_ == BF16 else 4)
        return 0.08, 2.2 + nbytes / 150e3, None
    if eng == "pe":
        lhsT = kw.get("lhsT", None)
        mult = 2.2 if (lhsT is not None and lhsT.dtype == F32) else 1.0
        if name == "transpose" and args[1].dtype == F32:
            mult = 2.0
        c = max(0.1, free * mult / 1150.0) + 0.01
        grp = (bool(kw.get('start', True)), bool(kw.get('stop', True))) if name == 'matmul' else (True, True)
        return c, c + 0.1, grp
    if eng == "act":
        c = 0.2 + free / 1100.0 + (0.1 if kw.get("accum_out", None) is not None else 0.0)
        return c, c, None
    if eng == "dve":
        c = 0.08 + free / (1800.0 if dt_ == BF16 else 950.0)
        return c, c, None
    c = 0.3 + free / 520.0
    return c, c, None


class Seq:
    def __init__(self, banks, offset=0.0, span=1.0):
        self.items = []
        self.banks = banks
        self.ptr = 0
        self.offset = offset
        self.span = span


class Par:
    def __init__(self):
        self.branches = []


def _flatten(node):
    if isinstance(node, Seq):
        out = []
        for it in node.items:
            if isinstance(it, (Seq, Par)):
                out.extend(_flatten(it))
            else:
                out.append(it)
        return out
    lists = [_flatten(b) for b in node.branches]
    keyed = []
    for li, l in enumerate(lists):
        n = len(l)
        off = node.branches[li].offset
        spn = node.branches[li].span
        for j, it in enumerate(l):
            keyed.append((off + (spn - off) * (j + 0.5) / n, li, j, it))
    keyed.sort(key=lambda x: (x[0], x[1], x[2]))
    return [x[3] for x in keyed]


class _Branch:
    def __init__(self, k, seq):
        self.k = k
        self.seq = seq

    def __enter__(self):
        self.prev = self.k.cur
        self.k.cur = self.seq
        return self.seq

    def __exit__(self, *a):
        self.k.cur = self.prev
        return False


class Kern:
    def __init__(self, nc):
        self.nc = nc
        self.instrs = []
        self.stack = None
        self.banks = []
        self.root = Seq(list(range(8)))
        self.cur = self.root
        self.ctx = {"th": 0, "x": 0, "br": 0, "hp": 0}
        self.pe_inorder = False
        self.pe_tok = Buf(None, "pe_tok")
        self.pin_pe = False
        self.schedule = True

    def sbuf(self, name, shape, dtype):
        t = self.stack.enter_context(self.nc.sbuf_tensor(name, list(shape), dtype))
        return Buf(t, name)

    def psum_init(self):
        for i in range(8):
            t = self.stack.enter_context(self.nc.psum_tensor(f"psb{i}", [128, 512], F32))
            b = Buf(t, f"psb{i}")
            b.bfv = t.bitcast(BF16)
            b.is_psum = True
            self.banks.append(b)

    def psum(self):
        sq = self.cur
        b = self.banks[sq.banks[sq.ptr % len(sq.banks)]]
        sq.ptr += 1
        b.gen += 1
        return PV(b)

    def fork(self):
        p = Par()
        self.cur.items.append(p)
        return p

    def branch(self, par, banks, offset=0.0, span=1.0):
        sq = Seq(banks, offset, span)
        par.branches.append(sq)
        return _Branch(self, sq)

    def _emit(self, eng, fn, r=(), w=(), dma=False, group=None):
        rr = []
        for x in r:
            if isinstance(x, PV):
                x._chk()
                x = x.b
            elif isinstance(x, Sw):
                x = x.cur()
            rr.append(x)
        ww = []
        for x in w:
            if isinstance(x, PV):
                x._chk()
                x = x.b
            elif isinstance(x, Sw):
                x = x.cur()
            ww.append(x)
        if isinstance(group, Sw):
            group = group.cur()
        self.cur.items.append((eng, fn, rr, ww, dma, group, dict(self.ctx)))

    def _analyze(self, rec):
        eng, fn, rr, ww, dma, group, ctx = rec
        iid = len(self.instrs)
        deps = set()
        for b in rr:
            if b.last_w is not None:
                deps.add(b.last_w)
            if getattr(b, "is_psum", False):
                for key, rid in b.readers.items():
                    if key != eng:
                        deps.add(rid)
        for b in ww:
            for rid in b.readers.values():
                deps.add(rid)
            if b.last_w is not None:
                deps.add(b.last_w)
        fdeps = []
        for d in deps:
            di = self.instrs[d]
            if di.eng == eng and not di.dma:
                if eng == "pe":
                    continue
                israw = any(b.last_w == d for b in rr) or any(b.last_w == d for b in ww)
                if not israw:
                    continue
            fdeps.append(d)
        ins = Instr(iid, eng, fn, fdeps, dma, group)
        ins.ctx = ctx
        self.instrs.append(ins)
        for b in ww:
            b.last_w = iid
            b.readers = {}
        for b in rr:
            if b.const:
                continue
            key = ("dma", iid) if dma else eng
            b.readers[key] = iid

    def pe(self, fn, r=(), w=()):
        if self.pe_inorder or self.pin_pe:
            w = list(w) + [self.pe_tok]
        return self._emit("pe", fn, r, w)

    def act(self, fn, r=(), w=()):
        return self._emit("act", fn, r, w)

    def dve(self, fn, r=(), w=()):
        return self._emit("dve", fn, r, w)

    def pool(self, fn, r=(), w=()):
        return self._emit("pool", fn, r, w)

    def dma(self, eng, fn, r=(), w=(), group=None):
        return self._emit(eng, fn, r, w, dma=True, group=group)

    def _list_schedule(self, recs):
        import heapq
        n = len(recs)
        lastw = {}
        readers = {}
        preds = [None] * n
        for i, (eng, fn, rr, ww, dma, group, ctx) in enumerate(recs):
            d = set()
            for b in rr:
                if id(b) in lastw:
                    d.add(lastw[id(b)])
            for b in ww:
                if id(b) in lastw:
                    d.add(lastw[id(b)])
                for r_ in readers.get(id(b), ()):
                    d.add(r_)
            d.discard(i)
            preds[i] = d
            for b in ww:
                lastw[id(b)] = i
                readers[id(b)] = []
            for b in rr:
                readers.setdefault(id(b), []).append(i)
        succs = [[] for _ in range(n)]
        npred = [0] * n
        for i in range(n):
            npred[i] = len(preds[i])
            for p in preds[i]:
                succs[p].append(i)
        occ = [0.0] * n
        lat = [0.0] * n
        open_grp = {}
        grp_of = {}
        for i, (eng, fn, rr, ww, dma, group, ctx) in enumerate(recs):
            self.ctx.update(ctx)
            occ[i], lat[i], g_ = _est_cost(eng, fn, dma)
            if eng == "pe":
                bank = id(ww[0])
                if g_ is None:
                    g_ = (True, True)
                st_, sp_ = g_
                if st_ or bank not in open_grp:
                    open_grp[bank] = []
                open_grp[bank].append(i)
                grp_of[i] = open_grp[bank]
                if sp_:
                    del open_grp[bank]
        ready_t = [0.0] * n
        finish = [0.0] * n
        heaps = {e: [] for e in ENGS}
        for i in range(n):
            if npred[i] == 0:
                heapq.heappush(heaps[recs[i][0]], (0.0, i))
        eng_free = {e: 0.0 for e in ENGS}
        order = []
        done = 0
        released = [npred[i] == 0 for i in range(n)]
        lock = None
        self.lock_breaks = 0

        def commit(e, st, i):
            nonlocal done
            eng_free[e] = st + occ[i]
            finish[i] = st + lat[i]
            order.append((st, i))
            done += 1
            for s_ in succs[i]:
                ready_t[s_] = max(ready_t[s_], finish[i] + 0.06)
                npred[s_] -= 1
                if npred[s_] == 0:
                    released[s_] = True
                    heapq.heappush(heaps[recs[s_][0]], (ready_t[s_], s_))

        scheduled = [False] * n
        while done < n:
            if lock:
                nx = lock[0]
                if released[nx]:
                    lock.pop(0)
                    st = max(ready_t[nx], eng_free["pe"])
                    scheduled[nx] = True
                    commit("pe", st, nx)
                    continue
            best = None
            for e in ENGS:
                if e == "pe" and lock:
                    continue
                h = heaps[e]
                while h and scheduled[h[0][1]]:
                    heapq.heappop(h)
                if not h:
                    continue
                rt, i = h[0]
                st = max(rt, eng_free[e])
                if best is None or st < best[0] or (st == best[0] and i < best[2]):
                    best = (st, e, i)
            if best is None:
                self.lock_breaks += 1
                lock = None
                continue
            st, e, _ = best
            h = heaps[e]
            cands = []
            while h and h[0][0] <= st + 1e-9:
                c_ = heapq.heappop(h)
                if not scheduled[c_[1]]:
                    cands.append(c_)
            cands.sort(key=lambda x: x[1])
            rt, i = cands[0]
            for c_ in cands[1:]:
                heapq.heappush(h, c_)
            if e == "pe":
                g = grp_of[i]
                if g[0] != i:
                    pass
                rest = [m for m in g if m != i and not scheduled[m]]
                lock = rest if rest else None
            scheduled[i] = True
            commit(e, st, i)
        order.sort(key=lambda x: (x[0], x[1]))
        self.est_makespan = max(finish)
        return [recs[i] for (_, i) in order]

    def finalize(self):
        nc = self.nc
        recs = _flatten(self.root)
        if self.schedule:
            recs = self._list_schedule(recs)
        for rec in recs:
            self._analyze(rec)
        instrs = self.instrs
        for ins in instrs:
            for d in ins.deps:
                instrs[d].signals = True
        groups = {}
        cnt = {e: 0 for e in ENGS}
        for ins in instrs:
            if ins.dma:
                lst = groups.setdefault(id(ins.group), [ins.group, 0])
                lst[1] += 1
                ins.ordinal = lst[1]
            elif ins.signals:
                cnt[ins.eng] += 1
                ins.ordinal = cnt[ins.eng]
        self.counts = cnt
        sems = {e: self.stack.enter_context(nc.semaphore(f"sem_{e}")) for e in ENGS}
        gsems = {gid: self.stack.enter_context(nc.semaphore(f"ds_{g.name}")) for gid, (g, n) in groups.items()}
        per_eng = {e: [i for i in instrs if i.eng == e] for e in ENGS}

        def run_engine(ename, eng):
            seen = {}
            for ins in per_eng[ename]:
                waits = {}
                for d in ins.deps:
                    di = instrs[d]
                    if di.dma:
                        key = ("g", id(di.group))
                        val = 16 * di.ordinal
                    else:
                        key = ("e", di.eng)
                        val = di.ordinal
                    if waits.get(key, 0) < val:
                        waits[key] = val
                for key, val in waits.items():
                    if seen.get(key, 0) >= val:
                        continue
                    seen[key] = val
                    sem = gsems[key[1]] if key[0] == "g" else sems[key[1]]
                    eng.wait_ge(sem, val)
                self.ctx.update(ins.ctx)
                bi = ins.fn(eng)
                if ins.dma:
                    bi.then_inc(gsems[id(ins.group)], 16)
                elif ins.signals:
                    bi.then_inc(sems[ename], 1)
            if ename == "sp":
                for gid, (g, n) in groups.items():
                    if seen.get(("g", gid), 0) < 16 * n:
                        eng.wait_ge(gsems[gid], 16 * n)

        with nc.Block() as block:
            @block.sync
            def _(e):
                run_engine("sp", e)

            @block.tensor
            def _(e):
                run_engine("pe", e)

            @block.scalar
            def _(e):
                run_engine("act", e)

            @block.vector
            def _(e):
                run_engine("dve", e)

            @block.gpsimd
            def _(e):
                run_engine("pool", e)


C_ID, C_UI, C_LS, C_ON, C_MB = 0, 128, 256, 384, 512
NEG = -30000.0


def make_consts():
    c = np.zeros((128, 512 + 3 * 272), np.float32)
    p = np.arange(128)[:, None]
    f = np.arange(128)[None, :]
    c[:, C_ID:C_ID + 128] = (p == f)
    c[:, C_UI:C_UI + 128] = (f >= p)
    c[:, C_LS:C_LS + 128] = (f < p)
    c[:, C_ON:C_ON + 128] = 1.0
    m = np.arange(16)[None, :]
    KW = 272
    mb0 = np.full((128, KW), NEG, np.float32)
    mb0[:, 256:272] = np.where(p >= 112 + m, 0.0, NEG)
    mb1 = np.full((128, KW), NEG, np.float32)
    mb1[:, 256:272] = 0.0
    mb1[:, 128:256] = np.where(f <= p, 0.0, NEG)
    mb2 = np.full((128, KW), NEG, np.float32)
    mb2[:, 256:272] = 0.0
    mb2[:, 0:128] = np.where(f > p, 0.0, NEG)
    mb2[:, 128:256] = np.where(f <= p, 0.0, NEG)
    c[:, C_MB:C_MB + KW] = mb0
    c[:, C_MB + KW:C_MB + 2 * KW] = mb1
    c[:, C_MB + 2 * KW:C_MB + 3 * KW] = mb2
    return c


def build(mode, ntiles=NT, nseq=2, dbg=99):
    do0 = mode in ("l0", "full")
    do1 = mode in ("l1", "full")
    nc = bass.Bass("TRN2", target_bir_lowering=False)

    def din(name, shape):
        return nc.dram_tensor(name, list(shape), F32, kind="ExternalInput").ap()

    x_d = din("x", [nseq, 2048, 1024])
    hin_d = din("hin", [nseq, NT * 128, 1024]) if mode == "l1" else None
    meta_d = din("meta", [16, 1024])
    consts_d = din("consts", [128, 512 + 816])
    w0_d = din("w_in0", [1024, 3592])
    pre0_d = din("pre0c", [128, 8])
    qkvw_d = din("qkvw", [128, 12, 4])
    alog_d = din("a_log", [4])
    dtb_d = din("dt_bias", [4])
    onorm_d = din("onorm", [128, 1])
    dww_d = din("dww", [128, 4, 31])
    dwb_d = din("dwb", [128, 4])
    lnw_d = din("lnw", [128, 4])
    lnb_d = din("lnb", [128, 4])
    wo0_d = din("w_out0", [1024, 1024])
    post0_d = din("post0", [1024])
    pre1_d = din("pre1c", [128, 8])
    w1_d = din("w_in1", [1024, 2304])
    sinks_d = din("sinks", [16])
    wo1_d = din("w_out1", [1024, 1024])
    post1_d = din("post1", [1024])
    if mode == "l0":
        out_d = nc.dram_tensor("hout", [nseq, NT * 128, 1024], F32, kind="ExternalOutput").ap()
    else:
        out_d = nc.dram_tensor("out", [nseq, 2048, 1024], F32, kind="ExternalOutput").ap()

    k = Kern(nc)
    with ExitStack() as st:
        k.stack = st
        k.psum_init()

        cst = k.sbuf("cst", [128, 385], F32)
        k.dma("sp", lambda e: e.dma_start(out=cst[:], in_=consts_d[:, 0:385]), w=[cst], group=cst)
        maskb = k.sbuf("maskb", [128, 816], BF16)
        k.dma("pool", lambda e: e.dma_start(out=maskb[:], in_=consts_d[:, 512:512 + 816]), w=[maskb], group=maskb)
        rowm = k.sbuf("rowm", [128, 1], F32)
        k.dve(lambda e: e.reduce_sum(out=rowm[:], in_=ident[:, 112:128], axis=AX.X), r=[cst], w=[rowm])
        ident = cst[:, C_ID:C_ID + 128]
        uincl = cst[:, C_UI:C_UI + 128]
        lstrict = cst[:, C_LS:C_LS + 128]
        ones = cst[:, C_ON:C_ON + 1].to_broadcast([128, 128])
        identb = k.sbuf("identb", [128, 128], BF16)
        k.dve(lambda e: e.tensor_copy(out=identb[:], in_=ident), r=[cst], w=[identb])
        prm = k.sbuf("prm", [128, 256], F32)
        P_PRE0, P_PRE1, P_QKVW, P_DWB, P_LNW, P_LNB, P_ON, P_ALOG, P_DTB, P_SINK, P_DWW = 0, 8, 16, 64, 68, 72, 76, 80, 84, 88, 104
        loads = [
            (prm[:, P_PRE0:P_PRE0 + 8], pre0_d), (prm[:, P_PRE1:P_PRE1 + 8], pre1_d),
            (prm[:, P_QKVW:P_QKVW + 48], qkvw_d.rearrange("p c j -> p (c j)")),
            (prm[:, P_DWB:P_DWB + 4], dwb_d), (prm[:, P_LNW:P_LNW + 4], lnw_d), (prm[:, P_LNB:P_LNB + 4], lnb_d),
            (prm[:, P_ON:P_ON + 1], onorm_d),
            (prm[:, P_ALOG:P_ALOG + 4], alog_d.partition_broadcast(128)),
            (prm[:, P_DTB:P_DTB + 4], dtb_d.partition_broadcast(128)),
            (prm[:, P_SINK:P_SINK + 16], sinks_d.partition_broadcast(128)),
            (prm[:, P_DWW:P_DWW + 124], dww_d.rearrange("p c j -> p (c j)")),
        ]
        for (o_, i_) in loads:
            k.dma("sp", lambda e, o_=o_, i_=i_: e.dma_start(out=o_, in_=i_), w=[prm], group=prm)
        prm2 = k.sbuf("prm2", [128, 16], F32)
        k.act(lambda e: e.activation(out=prm2[:, 0:4], in_=prm[:, P_ALOG:P_ALOG + 4], func=AF.Exp), r=[prm], w=[prm2])
        k.dve(lambda e: e.tensor_scalar(out=prm2[:, 0:4], in0=prm2[:, 0:4], scalar1=-1.0, scalar2=None, op0=ALU.mult), r=[prm2], w=[prm2])
        k.dve(lambda e: e.tensor_scalar(out=prm2[:, 4:5], in0=prm[:, P_ON:P_ON + 1], scalar1=0.5, scalar2=None, op0=ALU.mult), r=[prm, prm2], w=[prm2])
        PN0 = k.sbuf("PN0", [128, 1024], BF16)
        PN1 = k.sbuf("PN1", [128, 1024], BF16)

        def load_w(name, d_ap, ncols):
            W = k.sbuf(name, [128, 8, ncols], BF16)
            for kc in range(8):
                k.dma("pool", lambda e, kc=kc: e.dma_start(out=W[:, kc, :], in_=d_ap[kc * 128:(kc + 1) * 128, :]), w=[W], group=W)
            return W

        if do0:
            W0 = load_w("W0", w0_d, 3592)
            Wo0 = load_w("Wo0", wo0_d, 1024)
        if do1:
            W1 = load_w("W1", w1_d, 2304)
            Wo1 = load_w("Wo1", wo1_d, 1024)

        nth = 2 if mode == "full" else 1
        xt = Sw(k, "x", [k.sbuf(f"xt{i}", [128, 1024], F32) for i in range(nth)])
        hn = Sw(k, "th", [k.sbuf(f"hn{i}", [128, 1024], BF16) for i in range(nth)])
        hnT = Sw(k, "th", [k.sbuf(f"hnT{i}", [128, 8, 128], BF16) for i in range(nth)])
        col = Sw(k, "th", [k.sbuf(f"col{i}", [128, 64], F32) for i in range(nth)])
        tmpA = Sw(k, "th", [k.sbuf(f"tmpA{i}", [128, 512], F32) for i in range(nth)])
        ytmp = tmpA
        oTa = [k.sbuf(f"oTa{h}", [128, 128], BF16) for h in range(4)]
        oTb = k.sbuf("oTb", [128, 4, 128], BF16)
        oT1 = k.sbuf("oT1", [128, 8, 128], BF16)
        oT_l0 = [(oTa[h][:], oTa[h]) for h in range(4)] + [(oTb[:, c, :], oTb) for c in range(4)]
        oT_l1 = [(oT1[:, e_, :], oT1) for e_ in range(8)]
        for PN_, pd_ in ((PN0, post0_d), (PN1, post1_d)):
            stg = xt.bufs[0]
            k.dma("sp", lambda e, pd_=pd_, stg=stg: e.dma_start(out=stg[:], in_=pd_.partition_broadcast(128)), w=[stg], group=stg)
            k.dve(lambda e, PN_=PN_, stg=stg: e.tensor_scalar(out=PN_[:], in0=stg[:], scalar1=-1.0, scalar2=None, op0=ALU.add), r=[stg], w=[PN_])
        if do0:
            qkv_pre = k.sbuf("qkv_pre", [128, 12, 132], BF16)
            NDG = 6
            dgq = [k.sbuf(f"dgq{i}", [128, 128], BF16) for i in range(NDG)]
            dgc = [k.sbuf(f"dgc{i}", [128, 128], BF16) for i in range(NDG)]
            qkvs = k.sbuf("qkvs", [128, 12, 128], BF16)
            cin = k.sbuf("cin", [128, 4, 158], BF16)
            gza = k.sbuf("gza", [128, 4, 128], BF16)
            gzb = k.sbuf("gzb", [128, 4, 128], BF16)
            cc = k.sbuf("cc", [128, 4, 128], F32)
            ccsq = k.sbuf("ccsq", [128, 4, 128], F32)
            lnm = k.sbuf("lnm", [128, 2, 128], F32)
            gcol = k.sbuf("gcol", [128, 32], F32)
            Sst = [k.sbuf(f"Sst{h}", [128, 128], F32) for h in range(4)]
            Sb = [k.sbuf(f"Sb{h}", [128, 128], BF16) for h in range(4)]
            NBR = 4
            def brb(name, dt_):
                return Sw(k, "br", [k.sbuf(f"{name}_{i}", [128, 128], dt_) for i in range(NBR)])
            qn_tok = brb("qn_tok", BF16)
            kn_tok = brb("kn_tok", BF16)
            vb = brb("vb", BF16)
            kbg = brb("kbg", BF16)
            kdec = brb("kdec", BF16)
            qnT = brb("qnT", BF16)
            knT = brb("knT", BF16)
            qdecT = brb("qdecT", BF16)
            QKmT = brb("QKmT", BF16)
            TTb = brb("TTb", BF16)
            wTn = kn_tok
            junk = qdecT
            vnew = kbg
            on_b = qn_tok
            EX = brb("EX", F32)
            EG = brb("EG", F32)
            M1 = brb("M1", F32)
            M2 = brb("M2", F32)
            Pm = [brb("Pm0", F32), EX]
            PTm = [brb("PTm0", F32), M1]
            TTm = [M2, EG]
            colg = Sw(k, "br", [k.sbuf(f"colg{i}", [128, 16], F32) for i in range(NBR)])
        if do1:
            qT1 = k.sbuf("qT1", [128, 8, 128], BF16)
            kT1 = k.sbuf("kT1", [128, 2, 272], BF16)
            Vt = k.sbuf("Vt", [128, 3, 128], BF16)
            gz1 = k.sbuf("gz1", [128, 1024], BF16)
            Pb = Sw(k, "hp", [k.sbuf(f"Pb{i}", [128, 272], BF16) for i in range(2)])
            PTb = k.sbuf("PTb", [128, 3, 128], BF16)
            og = hn.bufs[-1]
            acol = Sw(k, "hp", [k.sbuf(f"acol{i}", [128, 16], F32) for i in range(2)])
            pe_ser = Buf(None, "pe_ser")

        def rstd_from_ss(ss_ap, dst_ap, n, bufs_r, buf_w, extra_bias=0.0):
            k.act(lambda e: e.activation(out=dst_ap, in_=ss_ap, func=AF.Ln, bias=EPS, scale=1.0 / n), r=bufs_r, w=[buf_w])
            k.act(lambda e: e.activation(out=dst_ap, in_=dst_ap, func=AF.Exp, bias=extra_bias, scale=-0.5), r=[buf_w], w=[buf_w])

        def norm_transpose(prec_col):
            k.act(lambda e: e.activation(out=hn[:], in_=xt[:], func=AF.Square, accum_out=col[:, 0:1]), r=[xt], w=[hn, col])
            rstd_from_ss(col[:, 0:1], col[:, 1:2], 1024.0, [col], col)
            k.dve(lambda e: e.tensor_scalar(out=hn[:], in0=xt[:], scalar1=col[:, 1:2], scalar2=None, op0=ALU.mult), r=[xt, col], w=[hn])
            p = k.psum()
            for kc in range(8):
                k.pe(lambda e, kc=kc: e.transpose(p.bf[:, kc * 128:(kc + 1) * 128], hn[:, kc * 128:(kc + 1) * 128], identb[:]), r=[hn, identb], w=[p])
            k.dve(lambda e: e.tensor_tensor(out=hnT[:], in0=p.bf[:, 0:1024].rearrange("p (k t) -> p k t", k=8),
                                            in1=prm[:, prec_col:prec_col + 8].unsqueeze(2).to_broadcast([128, 8, 128]), op=ALU.mult),
                  r=[p, prm], w=[hnT])

        def proj_fm(W, col0, p, slot):
            for kc in range(8):
                k.pe(lambda e, kc=kc: e.matmul(p[:, slot * 128:(slot + 1) * 128], lhsT=W[:, kc, col0:col0 + 128], rhs=hnT[:, kc, :],
                                               start=(kc == 0), stop=(kc == 7)), r=[W, hnT], w=[p])

        def proj_tm(W, col0, n, p, off=0):
            for kc in range(8):
                k.pe(lambda e, kc=kc: e.matmul(p[:, off:off + n], lhsT=hnT[:, kc, :], rhs=W[:, kc, col0:col0 + n],
                                               start=(kc == 0), stop=(kc == 7)), r=[W, hnT], w=[p])

        def out_proj_residual(Wo, PN, oTl):
            ps = [k.psum(), k.psum()]
            for h in range(2):
                for e_ in range(8):
                    k.pe(lambda e, e_=e_, h=h: e.matmul(ps[h][:, 0:512], lhsT=oTl[e_][0], rhs=Wo[:, e_, h * 512:(h + 1) * 512],
                                                         start=(e_ == 0), stop=(e_ == 7)), r=[oTl[e_][1], Wo], w=[ps[h]])
            for h in range(2):
                k.act(lambda e, h=h: e.activation(out=ytmp[:], in_=ps[h][:, 0:512], func=AF.Square, accum_out=col[:, 4 + h:5 + h]),
                      r=[ps[h]], w=[ytmp, col])
            k.dve(lambda e: e.tensor_tensor(out=col[:, 6:7], in0=col[:, 4:5], in1=col[:, 5:6], op=ALU.add), r=[col], w=[col])
            rstd_from_ss(col[:, 6:7], col[:, 7:8], 1024.0, [col], col)
            for h in range(2):
                k.dve(lambda e, h=h: e.scalar_tensor_tensor(out=ytmp[:], in0=ps[h][:, 0:512], scalar=col[:, 7:8], in1=PN[:, h * 512:(h + 1) * 512],
                                                             op0=ALU.mult, op1=ALU.mult), r=[ps[h], col, PN], w=[ytmp])
                k.dve(lambda e, h=h: e.scalar_tensor_tensor(out=xt[:, h * 512:(h + 1) * 512], in0=ps[h][:, 0:512], scalar=col[:, 7:8], in1=xt[:, h * 512:(h + 1) * 512],
                                                             op0=ALU.mult, op1=ALU.add), r=[ps[h], col, xt], w=[xt])
                k.dve(lambda e, h=h: e.tensor_tensor(out=xt[:, h * 512:(h + 1) * 512], in0=xt[:, h * 512:(h + 1) * 512], in1=ytmp[:], op=ALU.add),
                      r=[xt, ytmp], w=[xt])

        def silu2(dst_ap, src_ap, n_shape_tmp, r_bufs, w_buf, src_psum=None):
            k.act(lambda e: e.activation(out=n_shape_tmp, in_=src_ap, func=AF.Tanh, scale=0.5), r=r_bufs, w=[tmpA])
            k.dve(lambda e: e.scalar_tensor_tensor(out=dst_ap, in0=n_shape_tmp, scalar=1.0, in1=src_ap, op0=ALU.add, op1=ALU.mult),
                  r=[tmpA] + list(r_bufs), w=[w_buf])

        L0B = [0, 1, 2, 3, 4] if mode == "full" else [0, 1, 2, 3, 4]
        def layer0(t):
            norm_transpose(P_PRE0)
            for grp in range(3):
                p = k.psum()
                for s in range(4):
                    proj_fm(W0, (grp * 4 + s) * 128, p, s)
                k.act(lambda e, grp=grp, p=p: e.activation(out=qkv_pre[:, grp * 4:(grp + 1) * 4, 3:131],
                                                           in_=p[:, 0:512].rearrange("p (c t) -> p c t", c=4), func=AF.Copy),
                      r=[p], w=[qkv_pre])
            pv = k.psum()
            pg = k.psum()
            for s in range(4):
                proj_fm(W0, 2056 + s * 128, pv, s)
            for s in range(4):
                proj_fm(W0, 2568 + s * 128, pg, s)
            k.act(lambda e: e.activation(out=tmpA[:, 0:512], in_=pg[:, 0:512], func=AF.Tanh, scale=0.5), r=[pg], w=[tmpA])
            k.dve(lambda e: e.scalar_tensor_tensor(out=cin[:, :, 30:158], in0=tmpA[:, 0:512].rearrange("p (c t) -> p c t", c=4), scalar=1.0,
                                                   in1=pv[:, 0:512].rearrange("p (c t) -> p c t", c=4), op0=ALU.add, op1=ALU.mult),
                  r=[tmpA, pv], w=[cin])
            for (c0, gz) in ((1536, gza), (3080, gzb)):
                p = k.psum()
                for s in range(4):
                    proj_fm(W0, c0 + s * 128, p, s)
                silu2(gz[:].rearrange("p c t -> p (c t)"), p[:, 0:512], tmpA[:, 0:512], [p], gz)
            pba = k.psum()
            proj_tm(W0, 2048, 8, pba)
            k.act(lambda e: e.activation(out=gcol[:, 0:4], in_=pba[:, 0:4], func=AF.Tanh, scale=0.5), r=[pba], w=[gcol])
            k.dve(lambda e: e.tensor_scalar(out=gcol[:, 0:4], in0=gcol[:, 0:4], scalar1=1.0, scalar2=0.5, op0=ALU.add, op1=ALU.mult), r=[gcol], w=[gcol])
            k.dve(lambda e: e.tensor_scalar(out=gcol[:, 4:8], in0=gcol[:, 0:4], scalar1=0.5, scalar2=None, op0=ALU.mult), r=[gcol], w=[gcol])
            k.dve(lambda e: e.tensor_tensor(out=gcol[:, 8:12], in0=pba[:, 4:8], in1=prm[:, P_DTB:P_DTB + 4], op=ALU.add), r=[pba, prm], w=[gcol])
            k.act(lambda e: e.activation(out=gcol[:, 8:12], in_=gcol[:, 8:12], func=AF.Exp), r=[gcol], w=[gcol])
            k.act(lambda e: e.activation(out=gcol[:, 8:12], in_=gcol[:, 8:12], func=AF.Ln, bias=1.0), r=[gcol], w=[gcol])
            k.dve(lambda e: e.tensor_tensor(out=gcol[:, 8:12], in0=gcol[:, 8:12], in1=prm2[:, 0:4], op=ALU.mult), r=[gcol, prm2], w=[gcol])
            pG = k.psum()
            k.pe(lambda e: e.matmul(pG[:, 0:4], lhsT=uincl, rhs=gcol[:, 8:12], start=True, stop=True), r=[cst, gcol], w=[pG])
            k.pe(lambda e: e.matmul(pG[:, 4:8], lhsT=ones, rhs=gcol[:, 8:12], start=True, stop=True), r=[cst, gcol], w=[pG])
            k.dve(lambda e: e.tensor_copy(out=gcol[:, 12:20], in_=pG[:, 0:8]), r=[pG], w=[gcol])
            k.act(lambda e: e.activation(out=gcol[:, 20:24], in_=gcol[:, 12:16], func=AF.Exp), r=[gcol], w=[gcol])
            k.dve(lambda e: e.tensor_tensor(out=gcol[:, 20:24], in0=gcol[:, 20:24], in1=gcol[:, 0:4], op=ALU.mult), r=[gcol], w=[gcol])
            k.dve(lambda e: e.tensor_tensor(out=gcol[:, 24:28], in0=gcol[:, 16:20], in1=gcol[:, 12:16], op=ALU.subtract), r=[gcol], w=[gcol])
            k.act(lambda e: e.activation(out=gcol[:, 24:28], in_=gcol[:, 24:28], func=AF.Exp), r=[gcol], w=[gcol])
            k.act(lambda e: e.activation(out=gcol[:, 28:32], in_=gcol[:, 16:20], func=AF.Exp), r=[gcol], w=[gcol])

            for g3 in range(3):
                pc = k.psum()
                for c4 in range(4):
                    c = g3 * 4 + c4
                    for j in range(4):
                        dg = dgq[(c * 4 + j) % NDG]
                        wcol = prm[:, P_QKVW + c * 4 + j:P_QKVW + c * 4 + j + 1]
                        k.dve(lambda e, dg=dg, wcol=wcol: e.tensor_scalar(out=dg[:], in0=identb[:], scalar1=wcol, scalar2=None, op0=ALU.mult), r=[identb, prm], w=[dg])
                        k.pe(lambda e, dg=dg, c=c, c4=c4, j=j, pc=pc: e.matmul(pc[:, c4 * 128:(c4 + 1) * 128], lhsT=dg[:], rhs=qkv_pre[:, c, j:j + 128],
                                                                              start=(j == 0), stop=(j == 3)), r=[dg, qkv_pre], w=[pc])
                silu2(qkvs[:, g3 * 4:(g3 + 1) * 4, :].rearrange("p c t -> p (c t)"), pc[:, 0:512], tmpA[:, 0:512], [pc], qkvs)
            k.pool(lambda e: e.tensor_copy(out=qkv_pre[:, :, 0:3], in_=qkv_pre[:, :, 128:131]), r=[qkv_pre], w=[qkv_pre])

            def conformer():
                pcv = k.psum()
                for c in range(4):
                    for j in range(31):
                        dg = dgc[(c * 31 + j) % NDG]
                        wcol = prm[:, P_DWW + c * 31 + j:P_DWW + c * 31 + j + 1]
                        k.dve(lambda e, dg=dg, wcol=wcol: e.tensor_scalar(out=dg[:], in0=identb[:], scalar1=wcol, scalar2=None, op0=ALU.mult), r=[identb, prm], w=[dg])
                        k.pe(lambda e, dg=dg, c=c, j=j: e.matmul(pcv[:, c * 128:(c + 1) * 128], lhsT=dg[:], rhs=cin[:, c, j:j + 128],
                                                                 start=(j == 0), stop=(j == 30)), r=[dg, cin], w=[pcv])
                for c in range(4):
                    k.dve(lambda e, c=c: e.tensor_scalar(out=cc[:, c, :], in0=pcv[:, c * 128:(c + 1) * 128], scalar1=0.5, scalar2=prm[:, P_DWB + c:P_DWB + c + 1],
                                                         op0=ALU.mult, op1=ALU.add), r=[pcv, prm], w=[cc])
                k.pool(lambda e: e.tensor_copy(out=cin[:, :, 0:30], in_=cin[:, :, 128:158]), r=[cin], w=[cin])
                k.pool(lambda e: e.tensor_tensor(out=ccsq[:], in0=cc[:], in1=cc[:], op=ALU.mult), r=[cc], w=[ccsq])
                pm = k.psum()
                for c in range(4):
                    k.pe(lambda e, c=c: e.matmul(pm[:, 0:128], lhsT=ones, rhs=cc[:, c, :], start=(c == 0), stop=(c == 3)), r=[cst, cc], w=[pm])
                for c in range(4):
                    k.pe(lambda e, c=c: e.matmul(pm[:, 128:256], lhsT=ones, rhs=ccsq[:, c, :], start=(c == 0), stop=(c == 3)), r=[cst, ccsq], w=[pm])
                k.act(lambda e: e.activation(out=lnm[:, 0, :], in_=pm[:, 0:128], func=AF.Copy, scale=1.0 / 512), r=[pm], w=[lnm])
                k.dve(lambda e: e.tensor_tensor(out=lnm[:, 1, :], in0=lnm[:, 0, :], in1=lnm[:, 0, :], op=ALU.mult), r=[lnm], w=[lnm])
                k.dve(lambda e: e.scalar_tensor_tensor(out=lnm[:, 1, :], in0=pm[:, 128:256], scalar=1.0 / 512, in1=lnm[:, 1, :], op0=ALU.mult, op1=ALU.subtract),
                      r=[pm, lnm], w=[lnm])
                k.act(lambda e: e.activation(out=lnm[:, 1, :], in_=lnm[:, 1, :], func=AF.Ln, bias=EPS), r=[lnm], w=[lnm])
                k.act(lambda e: e.activation(out=lnm[:, 1, :], in_=lnm[:, 1, :], func=AF.Exp, scale=-0.5), r=[lnm], w=[lnm])
                k.dve(lambda e: e.tensor_tensor(out=cc[:], in0=cc[:], in1=lnm[:, 0, :].unsqueeze(1).to_broadcast([128, 4, 128]), op=ALU.subtract), r=[cc, lnm], w=[cc])
                k.dve(lambda e: e.tensor_tensor(out=cc[:], in0=cc[:], in1=lnm[:, 1, :].unsqueeze(1).to_broadcast([128, 4, 128]), op=ALU.mult), r=[cc, lnm], w=[cc])
                for c in range(4):
                    k.dve(lambda e, c=c: e.tensor_scalar(out=cc[:, c, :], in0=cc[:, c, :], scalar1=prm[:, P_LNW + c:P_LNW + c + 1], scalar2=prm[:, P_LNB + c:P_LNB + c + 1],
                                                         op0=ALU.mult, op1=ALU.add), r=[cc, prm], w=[cc])
                silu2(ccsq[:].rearrange("p c t -> p (c t)"), cc[:].rearrange("p c t -> p (c t)"), tmpA[:, 0:512], [cc], ccsq)
                k.dve(lambda e: e.scalar_tensor_tensor(out=oTb[:], in0=ccsq[:], scalar=0.25, in1=gzb[:], op0=ALU.mult, op1=ALU.mult), r=[ccsq, gzb], w=[oTb])

            def gdn_head(h):
                pt = k.psum()
                for i_, cidx in enumerate((h, 4 + h, 8 + h)):
                    k.pe(lambda e, i_=i_, cidx=cidx: e.transpose(pt.bf[:, i_ * 128:(i_ + 1) * 128], qkvs[:, cidx, :], identb[:]), r=[qkvs, identb], w=[pt])
                for i_, dst, xb in ((0, qn_tok, math.log(128.0 ** -0.5)), (1, kn_tok, 0.0)):
                    k.act(lambda e, i_=i_: e.activation(out=junk[:], in_=pt.bf[:, i_ * 128:(i_ + 1) * 128], func=AF.Square, accum_out=colg[:, 8 + i_:9 + i_]),
                          r=[pt], w=[junk, colg])
                    k.act(lambda e, i_=i_: e.activation(out=colg[:, 10 + i_:11 + i_], in_=colg[:, 8 + i_:9 + i_], func=AF.Ln, bias=4 * EPS), r=[colg], w=[colg])
                    k.act(lambda e, i_=i_, xb=xb: e.activation(out=colg[:, 10 + i_:11 + i_], in_=colg[:, 10 + i_:11 + i_], func=AF.Exp, bias=xb, scale=-0.5), r=[colg], w=[colg])
                    k.dve(lambda e, i_=i_, dst=dst: e.tensor_scalar(out=dst[:], in0=pt.bf[:, i_ * 128:(i_ + 1) * 128], scalar1=colg[:, 10 + i_:11 + i_], scalar2=None, op0=ALU.mult),
                          r=[pt, colg], w=[dst])
                k.dve(lambda e, h=h: e.tensor_scalar(out=vb[:], in0=pt.bf[:, 256:384], scalar1=gcol[:, 4 + h:5 + h], scalar2=None, op0=ALU.mult), r=[pt, gcol], w=[vb])
                k.dve(lambda e, h=h: e.tensor_scalar(out=kbg[:], in0=kn_tok[:], scalar1=gcol[:, 20 + h:21 + h], scalar2=None, op0=ALU.mult), r=[kn_tok, gcol], w=[kbg])
                k.dve(lambda e, h=h: e.tensor_scalar(out=kdec[:], in0=kn_tok[:], scalar1=gcol[:, 24 + h:25 + h], scalar2=None, op0=ALU.mult), r=[kn_tok, gcol], w=[kdec])
                pt2 = k.psum()
                k.pe(lambda e: e.transpose(pt2.bf[:, 0:128], qn_tok[:], identb[:]), r=[qn_tok, identb], w=[pt2])
                k.pe(lambda e: e.transpose(pt2.bf[:, 128:256], kn_tok[:], identb[:]), r=[kn_tok, identb], w=[pt2])
                k.act(lambda e: e.activation(out=qnT[:], in_=pt2.bf[:, 0:128], func=AF.Copy), r=[pt2], w=[qnT])
                k.act(lambda e: e.activation(out=knT[:], in_=pt2.bf[:, 128:256], func=AF.Copy), r=[pt2], w=[knT])
                pg_ = k.psum()
                k.pe(lambda e, h=h: e.matmul(pg_[:, 0:128], lhsT=gcol[:, 8 + h:9 + h].to_broadcast([128, 128]), rhs=uincl, start=True, stop=True), r=[gcol, cst], w=[pg_])
                k.dve(lambda e, h=h: e.tensor_scalar(out=EX[:], in0=pg_[:, 0:128], scalar1=gcol[:, 12 + h:13 + h], scalar2=None, op0=ALU.subtract), r=[pg_, gcol], w=[EX])
                k.act(lambda e: e.activation(out=EX[:], in_=EX[:], func=AF.Abs), r=[EX], w=[EX])
                k.act(lambda e: e.activation(out=EX[:], in_=EX[:], func=AF.Exp, scale=-1.0), r=[EX], w=[EX])
                k.act(lambda e: e.activation(out=EG[:], in_=pg_[:, 0:128], func=AF.Exp), r=[pg_], w=[EG])
                k.pool(lambda e: e.tensor_tensor(out=M1[:], in0=EX[:], in1=lstrict, op=ALU.mult), r=[EX, cst], w=[M1])
                k.dve(lambda e: e.tensor_tensor(out=M2[:], in0=EX[:], in1=uincl, op=ALU.mult), r=[EX, cst], w=[M2])
                k.dve(lambda e: e.tensor_tensor(out=qdecT[:], in0=qnT[:], in1=EG[:], op=ALU.mult), r=[qnT, EG], w=[qdecT])
                pk = k.psum()
                k.pe(lambda e: e.matmul(pk[:, 0:128], lhsT=knT[:], rhs=knT[:], start=True, stop=True), r=[knT], w=[pk])
                k.pe(lambda e: e.matmul(pk[:, 128:256], lhsT=knT[:], rhs=qnT[:], start=True, stop=True), r=[knT, qnT], w=[pk])
                k.dve(lambda e, h=h: e.scalar_tensor_tensor(out=Pm[0][:], in0=pk[:, 0:128], scalar=gcol[:, h:h + 1], in1=M1[:], op0=ALU.mult, op1=ALU.mult),
                      r=[pk, gcol, M1], w=[Pm[0]])
                k.dve(lambda e: e.tensor_tensor(out=QKmT[:], in0=pk[:, 128:256], in1=M2[:], op=ALU.mult), r=[pk, M2], w=[QKmT])
                pa = k.psum()
                k.pe(lambda e: e.transpose(pa[:, 0:128], Pm[0][:], ident), r=[Pm[0], cst], w=[pa])
                k.act(lambda e: e.activation(out=PTm[0][:], in_=pa[:, 0:128], func=AF.Copy), r=[pa], w=[PTm[0]])
                k.dve(lambda e: e.tensor_tensor(out=TTm[0][:], in0=ident, in1=pa[:, 0:128], op=ALU.subtract), r=[cst, pa], w=[TTm[0]])
                cur = 0
                for it in range(6):
                    nxt = 1 - cur
                    last = (it == 5)
                    pq = k.psum()
                    k.pe(lambda e, cur=cur: e.matmul(pq[:, 0:128], lhsT=PTm[cur][:], rhs=Pm[cur][:], start=True, stop=True), r=[PTm[cur], Pm[cur]], w=[pq])
                    if not last:
                        k.pe(lambda e, cur=cur: e.matmul(pq[:, 128:256], lhsT=Pm[cur][:], rhs=PTm[cur][:], start=True, stop=True), r=[PTm[cur], Pm[cur]], w=[pq])
                    k.act(lambda e, nxt=nxt: e.activation(out=Pm[nxt][:], in_=pq[:, 0:128], func=AF.Copy), r=[pq], w=[Pm[nxt]])
                    if not last:
                        k.dve(lambda e, nxt=nxt: e.tensor_copy(out=PTm[nxt][:], in_=pq[:, 128:256]), r=[pq], w=[PTm[nxt]])
                    pt3 = k.psum()
                    k.pe(lambda e, cur=cur, nxt=nxt: e.matmul(pt3[:, 0:128], lhsT=Pm[nxt][:], rhs=TTm[cur][:], start=True, stop=True), r=[Pm[nxt], TTm[cur]], w=[pt3])
                    if last:
                        k.dve(lambda e, cur=cur: e.tensor_tensor(out=TTb[:], in0=pt3[:, 0:128], in1=TTm[cur][:], op=ALU.add), r=[pt3, TTm[cur]], w=[TTb])
                    else:
                        k.dve(lambda e, cur=cur, nxt=nxt: e.tensor_tensor(out=TTm[nxt][:], in0=pt3[:, 0:128], in1=TTm[cur][:], op=ALU.add), r=[pt3, TTm[cur]], w=[TTm[nxt]])
                    cur = nxt
                pw = k.psum()
                k.pe(lambda e: e.matmul(pw[:, 0:128], lhsT=kbg[:], rhs=TTb[:], start=True, stop=True), r=[kbg, TTb], w=[pw])
                k.act(lambda e: e.activation(out=wTn[:], in_=pw[:, 0:128], func=AF.Copy, scale=-1.0), r=[pw], w=[wTn])
                k.pe(lambda e: e.matmul(pw[:, 128:256], lhsT=TTb[:], rhs=vb[:], start=True, stop=False), r=[TTb, vb], w=[pw])
                k.pe(lambda e, h=h: e.matmul(pw[:, 128:256], lhsT=wTn[:], rhs=Sb[h][:], start=False, stop=True), r=[wTn, Sb[h]], w=[pw])
                k.act(lambda e: e.activation(out=vnew[:], in_=pw[:, 128:256], func=AF.Copy), r=[pw], w=[vnew])
                po = k.psum()
                k.pe(lambda e, h=h: e.matmul(po[:, 0:128], lhsT=qdecT[:], rhs=Sb[h][:], start=True, stop=False), r=[qdecT, Sb[h]], w=[po])
                k.pe(lambda e: e.matmul(po[:, 0:128], lhsT=QKmT[:], rhs=vnew[:], start=False, stop=True), r=[QKmT, vnew], w=[po])
                k.pe(lambda e: e.matmul(po[:, 128:256], lhsT=kdec[:], rhs=vnew[:], start=True, stop=True), r=[kdec, vnew], w=[po])
                k.dve(lambda e, h=h: e.scalar_tensor_tensor(out=Sst[h][:], in0=Sst[h][:], scalar=gcol[:, 28 + h:29 + h], in1=po[:, 128:256],
                                                             op0=ALU.mult, op1=ALU.add), r=[Sst[h], gcol, po], w=[Sst[h]])
                k.act(lambda e, h=h: e.activation(out=Sb[h][:], in_=Sst[h][:], func=AF.Copy), r=[Sst[h]], w=[Sb[h]])
                k.act(lambda e: e.activation(out=junk[:], in_=po[:, 0:128], func=AF.Square, accum_out=colg[:, 12:13]), r=[po], w=[junk, colg])
                rstd_from_ss(colg[:, 12:13], colg[:, 13:14], 128.0, [colg], colg)
                k.dve(lambda e: e.tensor_scalar(out=on_b[:], in0=po[:, 0:128], scalar1=colg[:, 13:14], scalar2=None, op0=ALU.mult), r=[po, colg], w=[on_b])
                pot = k.psum()
                k.pe(lambda e: e.transpose(pot.bf[:, 0:128], on_b[:], identb[:]), r=[on_b, identb], w=[pot])
                k.dve(lambda e, h=h: e.scalar_tensor_tensor(out=oTa[h][:], in0=pot.bf[:, 0:128], scalar=prm2[:, 4:5], in1=gza[:, h, :], op0=ALU.mult, op1=ALU.mult),
                      r=[pot, prm2, gza], w=[oTa[h]])
            par = k.fork()
            for h_ in range(4):
                with k.branch(par, L0B[h_:h_ + 1], offset=0.05 * h_):
                    k.ctx["br"] = h_
                    gdn_head(h_)
            with k.branch(par, L0B[4:5], offset=CF_OFF, span=CF_SPAN):
                k.ctx["br"] = 0
                conformer()
            k.ctx["br"] = 0
            out_proj_residual(Wo0, PN0, oT_l0)

        def layer1(t):
            k.pin_pe = ("norm" not in L1_FLOAT)
            norm_transpose(P_PRE1)
            k.pin_pe = ("proj" not in L1_FLOAT)
            if dbg < 1.1:
                return
            for grp in range(2):
                p = k.psum()
                for s in range(4):
                    proj_fm(W1, (grp * 4 + s) * 128, p, s)
                k.act(lambda e, grp=grp, p=p: e.activation(out=qT1[:, grp * 4:(grp + 1) * 4, :], in_=p[:, 0:512].rearrange("p (c t) -> p c t", c=4),
                                                           func=AF.Copy, scale=0.125), r=[p], w=[qT1])
            if dbg < 1.3:
                return
            pkv = k.psum()
            proj_fm(W1, 1024, pkv, 0)
            if dbg != 1.36:
                k.act(lambda e: e.activation(out=kT1[0:64, 0, 128:256], in_=pkv[0:64, 0:128], func=AF.Copy), r=[pkv], w=[kT1])
                k.act(lambda e: e.activation(out=kT1[64:128, 1, 128:256], in_=pkv[64:128, 0:128], func=AF.Copy), r=[pkv], w=[kT1])
            if dbg == 1.35:
                return
            pkv2 = k.psum()
            proj_tm(W1, 1152, 128, pkv2, off=0)
            if dbg == 1.37:
                return
            k.dve(lambda e: e.tensor_scalar(out=Vt[:, 1, :], in0=pkv2[:, 0:128], scalar1=0.5, scalar2=None, op0=ALU.mult), r=[pkv2], w=[Vt])
            if dbg < 1.5:
                return
            if t == 0:
                k.dve(lambda e: e.tensor_copy(out=kT1[:, :, 256:272], in_=kT1[:, :, 240:256]), r=[kT1], w=[kT1])
                k.dve(lambda e: e.tensor_scalar(out=Vt[:, 2, :], in0=Vt[:, 1, :], scalar1=rowm[:, 0:1], scalar2=None, op0=ALU.mult), r=[Vt, rowm], w=[Vt])
            if dbg < 1.7:
                return
            for hh in range(2):
                p = k.psum()
                proj_tm(W1, 1280 + hh * 512, 512, p)
                silu2(gz1[:, hh * 512:(hh + 1) * 512], p[:, 0:512], tmpA[:, 0:512], [p], gz1)
            mbo = 272 * min(t, 2)
            k.pin_pe = ("attn" not in L1_FLOAT)
            if dbg < 3:
                return
            for c in range(8 if not (3 <= dbg < 4) else max(1, int(round((dbg - 3) * 10)))):
                for half in range(2):
                    hc = 2 * c + half
                    k.ctx["hp"] = hc % 2
                    lo, hi = half * 64, (half + 1) * 64
                    ps_ = k.psum()
                    k.pin_pe = ("attn" not in L1_FLOAT) and ("attn_s" not in L1_FLOAT)
                    k.pe(lambda e, c=c, half=half: e.matmul(ps_[:, 0:272], lhsT=qT1[:, c, :], rhs=kT1[:, half, :], start=True, stop=False),
                         r=[qT1, kT1], w=[ps_])
                    k.pe(lambda e: e.matmul(ps_[:, 0:272], lhsT=identb[:], rhs=maskb[:, mbo:mbo + 272], start=False, stop=True), r=[identb, maskb], w=[ps_])
                    k.dve(lambda e: e.reduce_max(out=acol[:, 0:1], in_=ps_[:, 0:272], axis=AX.X), r=[ps_], w=[acol])
                    k.dve(lambda e, hc=hc: e.tensor_scalar(out=acol[:, 2:3], in0=acol[:, 0:1], scalar1=prm[:, P_SINK + hc:P_SINK + hc + 1], scalar2=-1.0, op0=ALU.max, op1=ALU.mult),
                          r=[acol, prm], w=[acol])
                    k.act(lambda e: e.activation(out=Pb[:], in_=ps_[:, 0:272], func=AF.Exp, bias=acol[:, 2:3], accum_out=acol[:, 3:4]), r=[ps_, acol], w=[Pb, acol])
                    k.act(lambda e, hc=hc: e.activation(out=acol[:, 4:5], in_=prm[:, P_SINK + hc:P_SINK + hc + 1], func=AF.Exp, bias=acol[:, 2:3]), r=[prm, acol], w=[acol])
                    k.dve(lambda e: e.tensor_tensor(out=acol[:, 5:6], in0=acol[:, 3:4], in1=acol[:, 4:5], op=ALU.add), r=[acol], w=[acol])
                    k.dve(lambda e: e.reciprocal(out=acol[:, 7:8], in_=acol[:, 5:6]), r=[acol], w=[acol])
                    ptp = k.psum()
                    k.pin_pe = ("attn" not in L1_FLOAT) and ("attn_t" not in L1_FLOAT)
                    for g_, c0_ in enumerate((0, 128, 144)):
                        k.pe(lambda e, g_=g_, c0_=c0_: e.transpose(ptp.bf[:, g_ * 128:(g_ + 1) * 128], Pb[:, c0_:c0_ + 128], identb[:]), r=[Pb, identb], w=[ptp])
                    k.act(lambda e: e.activation(out=PTb[:], in_=ptp.bf[:, 0:384].rearrange("p (c t) -> p c t", c=3), func=AF.Copy), r=[ptp], w=[PTb])
                    pov = k.psum()
                    k.pin_pe = ("attn" not in L1_FLOAT) and ("attn_pv" not in L1_FLOAT)
                    for g_ in range(3):
                        k.pe(lambda e, g_=g_, lo=lo, hi=hi: e.matmul(pov[:, 0:64], lhsT=PTb[:, g_, :], rhs=Vt[:, g_, lo:hi], start=(g_ == 0), stop=(g_ == 2)), r=[PTb, Vt], w=[pov])
                    k.dve(lambda e, hc=hc: e.scalar_tensor_tensor(out=og[:, hc * 64:(hc + 1) * 64], in0=pov[:, 0:64], scalar=acol[:, 7:8], in1=gz1[:, hc * 64:(hc + 1) * 64],
                                                                   op0=ALU.mult, op1=ALU.mult), r=[pov, acol, gz1], w=[og])
            if dbg < 5:
                return
            k.pool(lambda e: e.tensor_copy(out=kT1[:, :, 0:128], in_=kT1[:, :, 128:256]), r=[kT1], w=[kT1])
            k.pool(lambda e: e.tensor_copy(out=Vt[:, 0, :], in_=Vt[:, 1, :]), r=[Vt], w=[Vt])
            k.pin_pe = ("tail" not in L1_FLOAT)
            pt_ = k.psum()
            for kc in range(8):
                k.pe(lambda e, kc=kc: e.transpose(pt_.bf[:, kc * 128:(kc + 1) * 128], og[:, kc * 128:(kc + 1) * 128], identb[:]), r=[og, identb], w=[pt_])
            k.act(lambda e: e.activation(out=oT1[:], in_=pt_.bf[:, 0:1024].rearrange("p (k t) -> p k t", k=8), func=AF.Copy), r=[pt_], w=[oT1])
            out_proj_residual(Wo1, PN1, oT_l1)

        steps = [(s_, t_) for s_ in range(nseq) for t_ in range(ntiles)]

        def load(i):
            s_, t = steps[i]
            if mode == "l1":
                k.dma("sp", lambda e: e.dma_start(out=xt[:], in_=hin_d[s_, t * 128:(t + 1) * 128, :]), w=[xt], group=xt)
            elif t == 0:
                k.pool(lambda e: e.memset(xt[:], 0.0), w=[xt])
                k.dma("sp", lambda e: e.dma_start(out=xt[112:128, :], in_=meta_d), w=[xt], group=xt)
            else:
                k.dma("sp", lambda e: e.dma_start(out=xt[:], in_=x_d[s_, (t - 1) * 128:t * 128, :]), w=[xt], group=xt)

        def store(i):
            s_, t = steps[i]
            if mode == "l0":
                k.dma("sp", lambda e: e.dma_start(out=out_d[s_, t * 128:(t + 1) * 128, :], in_=xt[:]), r=[xt], group=xt)
            elif t >= 1:
                k.dma("sp", lambda e: e.dma_start(out=out_d[s_, (t - 1) * 128:t * 128, :], in_=xt[:]), r=[xt], group=xt)

        def do_l0(i, th):
            s_, t = steps[i]
            k.ctx["th"] = th
            k.ctx["x"] = i % nth
            if t == 0:
                k.pool(lambda e: e.memset(qkv_pre[:, :, 0:3], 0.0), w=[qkv_pre])
                k.pool(lambda e: e.memset(cin[:, :, 0:30], 0.0), w=[cin])
                for h_ in range(4):
                    k.pool(lambda e, h_=h_: e.memset(Sst[h_][:], 0.0), w=[Sst[h_]])
                    k.pool(lambda e, h_=h_: e.memset(Sb[h_][:], 0.0), w=[Sb[h_]])
            load(i)
            layer0(t)
            if mode == "l0":
                store(i)

        def do_l1(i, th):
            s_, t = steps[i]
            k.pin_pe = True
            k.ctx["th"] = th
            k.ctx["x"] = i % nth
            if t == 0:
                k.pool(lambda e: e.memset(kT1[:], 0.0), w=[kT1])
                k.pool(lambda e: e.memset(Vt[:], 0.0), w=[Vt])
                k.pool(lambda e: e.memset(PTb[:], 0.0), w=[PTb])
            if mode == "l1":
                load(i)
            layer1(t)
            k.pin_pe = False
            store(i)

        n = len(steps)
        if mode == "full":
            BA, BB = L0B, [5, 6, 7]
            for i in range(n + 1):
                par = k.fork()
                if i < n:
                    with k.branch(par, BA):
                        do_l0(i, 0)
                if i >= 1:
                    with k.branch(par, BB, offset=L1_OFF, span=L1_SPAN):
                        do_l1(i - 1, 1)
        else:
            for i in range(n):
                if do0:
                    do_l0(i, 0)
                else:
                    do_l1(i, 0)
        k.finalize()
    return nc, k


_PERM = np.array([h * 64 + d for c in range(8) for h in (c, 8 + c) for d in range(64)])
_HPERM = np.array([h for c in range(8) for h in (c, 8 + c)])


def _common_inputs(inp):
    f = lambda a: np.ascontiguousarray(np.asarray(a, dtype=np.float32))
    w1 = f(inp["odd_w_in"])[0]
    w1p = np.concatenate([w1[:, 0:1024][:, _PERM], w1[:, 1024:1280], w1[:, 1280:2304][:, _PERM]], axis=1)
    d = {
        "meta": f(inp["meta_tokens"]),
        "consts": make_consts(),
        "w_in0": f(inp["even_w_in"])[0],
        "pre0c": f(f(inp["even_pre_norm"])[0].reshape(8, 128).T),
        "qkvw": f(f(inp["even_qkv_conv"])[0].T.reshape(12, 128, 4).transpose(1, 0, 2)),
        "a_log": f(inp["even_a_log"])[0],
        "dt_bias": f(inp["even_dt_bias"])[0],
        "onorm": f(f(inp["even_out_norm"])[0].reshape(128, 1)),
        "dww": f(f(inp["even_dw_conv"])[0].T.reshape(4, 128, 31).transpose(1, 0, 2)),
        "dwb": f(f(inp["even_dw_bias"])[0].reshape(4, 128).T),
        "lnw": f(f(inp["even_ln_w"])[0].reshape(4, 128).T),
        "lnb": f(f(inp["even_ln_b"])[0].reshape(4, 128).T),
        "w_out0": f(inp["even_w_out"])[0],
        "post0": f(inp["even_post_norm"])[0],
        "pre1c": f(f(inp["odd_pre_norm"])[0].reshape(8, 128).T),
        "w_in1": f(w1p),
        "sinks": f(f(inp["odd_sinks"])[0][_HPERM]),
        "w_out1": f(f(inp["odd_w_out"])[0][_PERM, :]),
        "post1": f(inp["odd_post_norm"])[0],
    }
    return d


_CACHE = {}


def _get(mode):
    if mode not in _CACHE:
        _CACHE[mode] = build(mode)[0]
    return _CACHE[mode]


def kernel(**inputs):
    x = np.ascontiguousarray(np.asarray(inputs["x"], dtype=np.float32))
    common = _common_inputs(inputs)
    nc = _get("full")
    in_maps = []
    for c in range(NCORES):
        m = dict(common)
        m["x"] = x[2 * c:2 * c + 2]
        in_maps.append(m)
    res = run_bass_kernel_spmd(nc, in_maps, core_ids=list(range(NCORES)))
    return np.concatenate([r["out"] for r in res.results], axis=0)
```

```python
import math
import numpy as np
from contextlib import ExitStack
import concourse.bass as bass
import concourse.mybir as mybir
from concourse.bass_utils import run_bass_kernel_spmd

F32 = mybir.dt.float32
BF16 = mybir.dt.bfloat16
ALU = mybir.AluOpType
AF = mybir.ActivationFunctionType
AX = mybir.AxisListType

ENGS = ("pe", "act", "dve", "pool", "sp")
NCORES = 8
L1_FLOAT = ("norm","proj","attn_s","tail")
L1_OFF, L1_SPAN, CF_OFF, CF_SPAN = 0.0, 1.0, 0.12, 0.9
NT = 17
EPS = 1e-6


class Buf:
    def __init__(self, ap, name):
        self.ap = ap
        self.name = name
        self.last_w = None
        self.readers = {}
        self.const = False
        self.gen = 0

    def __getitem__(self, key):
        return self.ap[key]


class PV:
    def __init__(self, bank):
        self.b = bank
        self.gen = bank.gen

    def _chk(self):
        assert self.b.gen == self.gen, f"stale psum handle {self.b.name}"

    def __getitem__(self, key):
        return self.b.ap[key]

    @property
    def bf(self):
        return self.b.bfv


class Sw:
    def __init__(self, k, key, bufs):
        self.k = k
        self.key = key
        self.bufs = bufs

    def cur(self):
        return self.bufs[self.k.ctx[self.key]]

    def __getitem__(self, key):
        return self.cur().ap[key]


class Instr:
    __slots__ = ("id", "eng", "fn", "deps", "dma", "group", "signals", "ordinal", "ctx")

    def __init__(self, id, eng, fn, deps, dma, group):
        self.ctx = None
        self.id = id
        self.eng = eng
        self.fn = fn
        self.deps = deps
        self.dma = dma
        self.group = group
        self.signals = False
        self.ordinal = 0


class _FakeIns:
    def then_inc(self, *a, **k):
        return self


class _FakeEng:
    def __init__(self):
        self.info = None

    def __getattr__(self, name):
        def f(*args, **kw):
            out = kw.get("out", args[0] if args else None)
            self.info = (name, out, kw, args)
            return _FakeIns()
        return f


def _est_cost(eng, fn, dma):
    fe = _FakeEng()
    try:
        fn(fe)
        name, out, kw, args = fe.info
        free = out.free_size()
        dt_ = out.dtype
    except Exception:
        return 0.3, 0.3, None
    if dma:
        nbytes = free * out.partition_size() * (2 if dt_ == BF16 else 4)
        return 0.08, 2.2 + nbytes / 150e3, None
    if eng == "pe":
        lhsT = kw.get("lhsT", None)
        mult = 2.2 if (lhsT is not None and lhsT.dtype == F32) else 1.0
        if name == "transpose" and args[1].dtype == F32:
            mult = 2.0
        c = max(0.1, free * mult / 1150.0) + 0.01
        grp = (bool(kw.get('start', True)), bool(kw.get('stop', True))) if name == 'matmul' else (True, True)
        return c, c + 0.1, grp
    if eng == "act":
        c = 0.2 + free / 1100.0 + (0.1 if kw.get("accum_out", None) is not None else 0.0)
        return c, c, None
    if eng == "dve":
        c = 0.08 + free / (1800.0 if dt_ == BF16 else 950.0)
        return c, c, None
    c = 0.3 + free / 520.0
    return c, c, None


class Seq:
    def __init__(self, banks, offset=0.0, span=1.0):
        self.items = []
        self.banks = banks
        self.ptr = 0
        self.offset = offset
        self.span = span


class Par:
    def __init__(self):
        self.branches = []


def _flatten(node):
    if isinstance(node, Seq):
        out = []
        for it in node.items:
            if isinstance(it, (Seq, Par)):
                out.extend(_flatten(it))
            else:
                out.append(it)
        return out
    lists = [_flatten(b) for b in node.branches]
    keyed = []
    for li, l in enumerate(lists):
        n = len(l)
        off = node.branches[li].offset
        spn = node.branches[li].span
        for j, it in enumerate(l):
            keyed.append((off + (spn - off) * (j + 0.5) / n, li, j, it))
    keyed.sort(key=lambda x: (x[0], x[1], x[2]))
    return [x[3] for x in keyed]


class _Branch:
    def __init__(self, k, seq):
        self.k = k
        self.seq = seq

    def __enter__(self):
        self.prev = self.k.cur
        self.k.cur = self.seq
        return self.seq

    def __exit__(self, *a):
        self.k.cur = self.prev
        return False


class Kern:
    def __init__(self, nc):
        self.nc = nc
        self.instrs = []
        self.stack = None
        self.banks = []
        self.root = Seq(list(range(8)))
        self.cur = self.root
        self.ctx = {"th": 0, "x": 0, "br": 0, "hp": 0}
        self.pe_inorder = False
        self.pe_tok = Buf(None, "pe_tok")
        self.pin_pe = False
        self.schedule = True

    def sbuf(self, name, shape, dtype):
        t = self.stack.enter_context(self.nc.sbuf_tensor(name, list(shape), dtype))
        return Buf(t, name)

    def psum_init(self):
        for i in range(8):
            t = self.stack.enter_context(self.nc.psum_tensor(f"psb{i}", [128, 512], F32))
            b = Buf(t, f"psb{i}")
            b.bfv = t.bitcast(BF16)
            b.is_psum = True
            self.banks.append(b)

    def psum(self):
        sq = self.cur
        b = self.banks[sq.banks[sq.ptr % len(sq.banks)]]
        sq.ptr += 1
        b.gen += 1
        return PV(b)

    def fork(self):
        p = Par()
        self.cur.items.append(p)
        return p

    def branch(self, par, banks, offset=0.0, span=1.0):
        sq = Seq(banks, offset, span)
        par.branches.append(sq)
        return _Branch(self, sq)

    def _emit(self, eng, fn, r=(), w=(), dma=False, group=None):
        rr = []
        for x in r:
            if isinstance(x, PV):
                x._chk()
                x = x.b
            elif isinstance(x, Sw):
                x = x.cur()
            rr.append(x)
        ww = []
        for x in w:
            if isinstance(x, PV):
                x._chk()
                x = x.b
            elif isinstance(x, Sw):
                x = x.cur()
            ww.append(x)
        if isinstance(group, Sw):
            group = group.cur()
        self.cur.items.append((eng, fn, rr, ww, dma, group, dict(self.ctx)))

    def _analyze(self, rec):
        eng, fn, rr, ww, dma, group, ctx = rec
        iid = len(self.instrs)
        deps = set()
        for b in rr:
            if b.last_w is not None:
                deps.add(b.last_w)
            if getattr(b, "is_psum", False):
                for key, rid in b.readers.items():
                    if key != eng:
                        deps.add(rid)
        for b in ww:
            for rid in b.readers.values():
                deps.add(rid)
            if b.last_w is not None:
                deps.add(b.last_w)
        fdeps = []
        for d in deps:
            di = self.instrs[d]
            if di.eng == eng and not di.dma:
                if eng == "pe":
                    continue
                israw = any(b.last_w == d for b in rr) or any(b.last_w == d for b in ww)
                if not israw:
                    continue
            fdeps.append(d)
        ins = Instr(iid, eng, fn, fdeps, dma, group)
        ins.ctx = ctx
        self.instrs.append(ins)
        for b in ww:
            b.last_w = iid
            b.readers = {}
        for b in rr:
            if b.const:
                continue
            key = ("dma", iid) if dma else eng
            b.readers[key] = iid

    def pe(self, fn, r=(), w=()):
        if self.pe_inorder or self.pin_pe:
            w = list(w) + [self.pe_tok]
        return self._emit("pe", fn, r, w)

    def act(self, fn, r=(), w=()):
        return self._emit("act", fn, r, w)

    def dve(self, fn, r=(), w=()):
        return self._emit("dve", fn, r, w)

    def pool(self, fn, r=(), w=()):
        return self._emit("pool", fn, r, w)

    def dma(self, eng, fn, r=(), w=(), group=None):
        return self._emit(eng, fn, r, w, dma=True, group=group)

    def _list_schedule(self, recs):
        import heapq
        n = len(recs)
        lastw = {}
        readers = {}
        preds = [None] * n
        for i, (eng, fn, rr, ww, dma, group, ctx) in enumerate(recs):
            d = set()
            for b in rr:
                if id(b) in lastw:
                    d.add(lastw[id(b)])
            for b in ww:
                if id(b) in lastw:
                    d.add(lastw[id(b)])
                for r_ in readers.get(id(b), ()):
                    d.add(r_)
            d.discard(i)
            preds[i] = d
            for b in ww:
                lastw[id(b)] = i
                readers[id(b)] = []
            for b in rr:
                readers.setdefault(id(b), []).append(i)
        succs = [[] for _ in range(n)]
        npred = [0] * n
        for i in range(n):
            npred[i] = len(preds[i])
            for p in preds[i]:
                succs[p].append(i)
        occ = [0.0] * n
        lat = [0.0] * n
        open_grp = {}
        grp_of = {}
        for i, (eng, fn, rr, ww, dma, group, ctx) in enumerate(recs):
            self.ctx.update(ctx)
            occ[i], lat[i], g_ = _est_cost(eng, fn, dma)
            if eng == "pe":
                bank = id(ww[0])
                if g_ is None:
                    g_ = (True, True)
                st_, sp_ = g_
                if st_ or bank not in open_grp:
                    open_grp[bank] = []
                open_grp[bank].append(i)
                grp_of[i] = open_grp[bank]
                if sp_:
                    del open_grp[bank]
        ready_t = [0.0] * n
        finish = [0.0] * n
        heaps = {e: [] for e in ENGS}
        for i in range(n):
            if npred[i] == 0:
                heapq.heappush(heaps[recs[i][0]], (0.0, i))
        eng_free = {e: 0.0 for e in ENGS}
        order = []
        done = 0
        released = [npred[i] == 0 for i in range(n)]
        lock = None
        self.lock_breaks = 0

        def commit(e, st, i):
            nonlocal done
            eng_free[e] = st + occ[i]
            finish[i] = st + lat[i]
            order.append((st, i))
            done += 1
            for s_ in succs[i]:
                ready_t[s_] = max(ready_t[s_], finish[i] + 0.06)
                npred[s_] -= 1
                if npred[s_] == 0:
                    released[s_] = True
                    heapq.heappush(heaps[recs[s_][0]], (ready_t[s_], s_))

        scheduled = [False] * n
        while done < n:
            if lock:
                nx = lock[0]
                if released[nx]:
                    lock.pop(0)
                    st = max(ready_t[nx], eng_free["pe"])
                    scheduled[nx] = True
                    commit("pe", st, nx)
                    continue
            best = None
            for e in ENGS:
                if e == "pe" and lock:
                    continue
                h = heaps[e]
                while h and scheduled[h[0][1]]:
                    heapq.heappop(h)
                if not h:
                    continue
                rt, i = h[0]
                st = max(rt, eng_free[e])
                if best is None or st < best[0] or (st == best[0] and i < best[2]):
                    best = (st, e, i)
            if best is None:
                self.lock_breaks += 1
                lock = None
                continue
            st, e, _ = best
            h = heaps[e]
            cands = []
            while h and h[0][0] <= st + 1e-9:
                c_ = heapq.heappop(h)
                if not scheduled[c_[1]]:
                    cands.append(c_)
            cands.sort(key=lambda x: x[1])
            rt, i = cands[0]
            for c_ in cands[1:]:
                heapq.heappush(h, c_)
            if e == "pe":
                g = grp_of[i]
                if g[0] != i:
                    pass
                rest = [m for m in g if m != i and not scheduled[m]]
                lock = rest if rest else None
            scheduled[i] = True
            commit(e, st, i)
        order.sort(key=lambda x: (x[0], x[1]))
        self.est_makespan = max(finish)
        return [recs[i] for (_, i) in order]

    def finalize(self):
        nc = self.nc
        recs = _flatten(self.root)
        if self.schedule:
            recs = self._list_schedule(recs)
        for rec in recs:
            self._analyze(rec)
        instrs = self.instrs
        for ins in instrs:
            for d in ins.deps:
                instrs[d].signals = True
        groups = {}
        cnt = {e: 0 for e in ENGS}
        for ins in instrs:
            if ins.dma:
                lst = groups.setdefault(id(ins.group), [ins.group, 0])
                lst[1] += 1
                ins.ordinal = lst[1]
            elif ins.signals:
                cnt[ins.eng] += 1
                ins.ordinal = cnt[ins.eng]
        self.counts = cnt
        sems = {e: self.stack.enter_context(nc.semaphore(f"sem_{e}")) for e in ENGS}
        gsems = {gid: self.stack.enter_context(nc.semaphore(f"ds_{g.name}")) for gid, (g, n) in groups.items()}
        per_eng = {e: [i for i in instrs if i.eng == e] for e in ENGS}

        def run_engine(ename, eng):
            seen = {}
            for ins in per_eng[ename]:
                waits = {}
                for d in ins.deps:
                    di = instrs[d]
                    if di.dma:
                        key = ("g", id(di.group))
                        val = 16 * di.ordinal
                    else:
                        key = ("e", di.eng)
                        val = di.ordinal
                    if waits.get(key, 0) < val:
                        waits[key] = val
                for key, val in waits.items():
                    if seen.get(key, 0) >= val:
                        continue
                    seen[key] = val
                    sem = gsems[key[1]] if key[0] == "g" else sems[key[1]]
                    eng.wait_ge(sem, val)
                self.ctx.update(ins.ctx)
                bi = ins.fn(eng)
                if ins.dma:
                    bi.then_inc(gsems[id(ins.group)], 16)
                elif ins.signals:
                    bi.then_inc(sems[ename], 1)
            if ename == "sp":
                for gid, (g, n) in groups.items():
                    if seen.get(("g", gid), 0) < 16 * n:
                        eng.wait_ge(gsems[gid], 16 * n)

        with nc.Block() as block:
            @block.sync
            def _(e):
                run_engine("sp", e)

            @block.tensor
            def _(e):
                run_engine("pe", e)

            @block.scalar
            def _(e):
                run_engine("act", e)

            @block.vector
            def _(e):
                run_engine("dve", e)

            @block.gpsimd
            def _(e):
                run_engine("pool", e)


C_ID, C_UI, C_LS, C_ON, C_MB = 0, 128, 256, 384, 512
NEG = -30000.0


def make_consts():
    c = np.zeros((128, 512 + 3 * 272), np.float32)
    p = np.arange(128)[:, None]
    f = np.arange(128)[None, :]
    c[:, C_ID:C_ID + 128] = (p == f)
    c[:, C_UI:C_UI + 128] = (f >= p)
    c[:, C_LS:C_LS + 128] = (f < p)
    c[:, C_ON:C_ON + 128] = 1.0
    m = np.arange(16)[None, :]
    KW = 272
    mb0 = np.full((128, KW), NEG, np.float32)
    mb0[:, 256:272] = np.where(p >= 112 + m, 0.0, NEG)
    mb1 = np.full((128, KW), NEG, np.float32)
    mb1[:, 256:272] = 0.0
    mb1[:, 128:256] = np.where(f <= p, 0.0, NEG)
    mb2 = np.full((128, KW), NEG, np.float32)
    mb2[:, 256:272] = 0.0
    mb2[:, 0:128] = np.where(f > p, 0.0, NEG)
    mb2[:, 128:256] = np.where(f <= p, 0.0, NEG)
    c[:, C_MB:C_MB + KW] = mb0
    c[:, C_MB + KW:C_MB + 2 * KW] = mb1
    c[:, C_MB + 2 * KW:C_MB + 3 * KW] = mb2
    return c


def build(mode, ntiles=NT, nseq=2, dbg=99):
    do0 = mode in ("l0", "full")
    do1 = mode in ("l1", "full")
    nc = bass.Bass("TRN2", target_bir_lowering=False)

    def din(name, shape):
        return nc.dram_tensor(name, list(shape), F32, kind="ExternalInput").ap()

    x_d = din("x", [nseq, 2048, 1024])
    hin_d = din("hin", [nseq, NT * 128, 1024]) if mode == "l1" else None
    meta_d = din("meta", [16, 1024])
    consts_d = din("consts", [128, 512 + 816])
    w0_d = din("w_in0", [1024, 3592])
    pre0_d = din("pre0c", [128, 8])
    qkvw_d = din("qkvw", [128, 12, 4])
    alog_d = din("a_log", [4])
    dtb_d = din("dt_bias", [4])
    onorm_d = din("onorm", [128, 1])
    dww_d = din("dww", [128, 4, 31])
    dwb_d = din("dwb", [128, 4])
    lnw_d = din("lnw", [128, 4])
    lnb_d = din("lnb", [128, 4])
    wo0_d = din("w_out0", [1024, 1024])
    post0_d = din("post0", [1024])
    pre1_d = din("pre1c", [128, 8])
    w1_d = din("w_in1", [1024, 2304])
    sinks_d = din("sinks", [16])
    wo1_d = din("w_out1", [1024, 1024])
    post1_d = din("post1", [1024])
    if mode == "l0":
        out_d = nc.dram_tensor("hout", [nseq, NT * 128, 1024], F32, kind="ExternalOutput").ap()
    else:
        out_d = nc.dram_tensor("out", [nseq, 2048, 1024], F32, kind="ExternalOutput").ap()

    k = Kern(nc)
    with ExitStack() as st:
        k.stack = st
        k.psum_init()

        cst = k.sbuf("cst", [128, 385], F32)
        k.dma("sp", lambda e: e.dma_start(out=cst[:], in_=consts_d[:, 0:385]), w=[cst], group=cst)
        maskb = k.sbuf("maskb", [128, 816], BF16)
        k.dma("pool", lambda e: e.dma_start(out=maskb[:], in_=consts_d[:, 512:512 + 816]), w=[maskb], group=maskb)
        rowm = k.sbuf("rowm", [128, 1], F32)
        k.dve(lambda e: e.reduce_sum(out=rowm[:], in_=ident[:, 112:128], axis=AX.X), r=[cst], w=[rowm])
        ident = cst[:, C_ID:C_ID + 128]
        uincl = cst[:, C_UI:C_UI + 128]
        lstrict = cst[:, C_LS:C_LS + 128]
        ones = cst[:, C_ON:C_ON + 1].to_broadcast([128, 128])
        identb = k.sbuf("identb", [128, 128], BF16)
        k.dve(lambda e: e.tensor_copy(out=identb[:], in_=ident), r=[cst], w=[identb])
        prm = k.sbuf("prm", [128, 256], F32)
        P_PRE0, P_PRE1, P_QKVW, P_DWB, P_LNW, P_LNB, P_ON, P_ALOG, P_DTB, P_SINK, P_DWW = 0, 8, 16, 64, 68, 72, 76, 80, 84, 88, 104
        loads = [
            (prm[:, P_PRE0:P_PRE0 + 8], pre0_d), (prm[:, P_PRE1:P_PRE1 + 8], pre1_d),
            (prm[:, P_QKVW:P_QKVW + 48], qkvw_d.rearrange("p c j -> p (c j)")),
            (prm[:, P_DWB:P_DWB + 4], dwb_d), (prm[:, P_LNW:P_LNW + 4], lnw_d), (prm[:, P_LNB:P_LNB + 4], lnb_d),
            (prm[:, P_ON:P_ON + 1], onorm_d),
            (prm[:, P_ALOG:P_ALOG + 4], alog_d.partition_broadcast(128)),
            (prm[:, P_DTB:P_DTB + 4], dtb_d.partition_broadcast(128)),
            (prm[:, P_SINK:P_SINK + 16], sinks_d.partition_broadcast(128)),
            (prm[:, P_DWW:P_DWW + 124], dww_d.rearrange("p c j -> p (c j)")),
        ]
        for (o_, i_) in loads:
            k.dma("sp", lambda e, o_=o_, i_=i_: e.dma_start(out=o_, in_=i_), w=[prm], group=prm)
        prm2 = k.sbuf("prm2", [128, 16], F32)
        k.act(lambda e: e.activation(out=prm2[:, 0:4], in_=prm[:, P_ALOG:P_ALOG + 4], func=AF.Exp), r=[prm], w=[prm2])
        k.dve(lambda e: e.tensor_scalar(out=prm2[:, 0:4], in0=prm2[:, 0:4], scalar1=-1.0, scalar2=None, op0=ALU.mult), r=[prm2], w=[prm2])
        k.dve(lambda e: e.tensor_scalar(out=prm2[:, 4:5], in0=prm[:, P_ON:P_ON + 1], scalar1=0.5, scalar2=None, op0=ALU.mult), r=[prm, prm2], w=[prm2])
        PN0 = k.sbuf("PN0", [128, 1024], BF16)
        PN1 = k.sbuf("PN1", [128, 1024], BF16)

        def load_w(name, d_ap, ncols):
            W = k.sbuf(name, [128, 8, ncols], BF16)
            for kc in range(8):
                k.dma("pool", lambda e, kc=kc: e.dma_start(out=W[:, kc, :], in_=d_ap[kc * 128:(kc + 1) * 128, :]), w=[W], group=W)
            return W

        if do0:
            W0 = load_w("W0", w0_d, 3592)
            Wo0 = load_w("Wo0", wo0_d, 1024)
        if do1:
            W1 = load_w("W1", w1_d, 2304)
            Wo1 = load_w("Wo1", wo1_d, 1024)

        nth = 2 if mode == "full" else 1
        xt = Sw(k, "x", [k.sbuf(f"xt{i}", [128, 1024], F32) for i in range(nth)])
        hn = Sw(k, "th", [k.sbuf(f"hn{i}", [128, 1024], BF16) for i in range(nth)])
        hnT = Sw(k, "th", [k.sbuf(f"hnT{i}", [128, 8, 128], BF16) for i in range(nth)])
        col = Sw(k, "th", [k.sbuf(f"col{i}", [128, 64], F32) for i in range(nth)])
        tmpA = Sw(k, "th", [k.sbuf(f"tmpA{i}", [128, 512], F32) for i in range(nth)])
        ytmp = tmpA
        oTa = [k.sbuf(f"oTa{h}", [128, 128], BF16) for h in range(4)]
        oTb = k.sbuf("oTb", [128, 4, 128], BF16)
        oT1 = k.sbuf("oT1", [128, 8, 128], BF16)
        oT_l0 = [(oTa[h][:], oTa[h]) for h in range(4)] + [(oTb[:, c, :], oTb) for c in range(4)]
        oT_l1 = [(oT1[:, e_, :], oT1) for e_ in range(8)]
        for PN_, pd_ in ((PN0, post0_d), (PN1, post1_d)):
            stg = xt.bufs[0]
            k.dma("sp", lambda e, pd_=pd_, stg=stg: e.dma_start(out=stg[:], in_=pd_.partition_broadcast(128)), w=[stg], group=stg)
            k.dve(lambda e, PN_=PN_, stg=stg: e.tensor_scalar(out=PN_[:], in0=stg[:], scalar1=-1.0, scalar2=None, op0=ALU.add), r=[stg], w=[PN_])
        if do0:
            qkv_pre = k.sbuf("qkv_pre", [128, 12, 132], BF16)
            NDG = 6
            dgq = [k.sbuf(f"dgq{i}", [128, 128], BF16) for i in range(NDG)]
            dgc = [k.sbuf(f"dgc{i}", [128, 128], BF16) for i in range(NDG)]
            qkvs = k.sbuf("qkvs", [128, 12, 128], BF16)
            cin = k.sbuf("cin", [128, 4, 158], BF16)
            gza = k.sbuf("gza", [128, 4, 128], BF16)
            gzb = k.sbuf("gzb", [128, 4, 128], BF16)
            cc = k.sbuf("cc", [128, 4, 128], F32)
            ccsq = k.sbuf("ccsq", [128, 4, 128], F32)
            lnm = k.sbuf("lnm", [128, 2, 128], F32)
            gcol = k.sbuf("gcol", [128, 32], F32)
            Sst = [k.sbuf(f"Sst{h}", [128, 128], F32) for h in range(4)]
            Sb = [k.sbuf(f"Sb{h}", [128, 128], BF16) for h in range(4)]
            NBR = 4
            def brb(name, dt_):
                return Sw(k, "br", [k.sbuf(f"{name}_{i}", [128, 128], dt_) for i in range(NBR)])
            qn_tok = brb("qn_tok", BF16)
            kn_tok = brb("kn_tok", BF16)
            vb = brb("vb", BF16)
            kbg = brb("kbg", BF16)
            kdec = brb("kdec", BF16)
            qnT = brb("qnT", BF16)
            knT = brb("knT", BF16)
            qdecT = brb("qdecT", BF16)
            QKmT = brb("QKmT", BF16)
            TTb = brb("TTb", BF16)
            wTn = kn_tok
            junk = qdecT
            vnew = kbg
            on_b = qn_tok
            EX = brb("EX", F32)
            EG = brb("EG", F32)
            M1 = brb("M1", F32)
            M2 = brb("M2", F32)
            Pm = [brb("Pm0", F32), EX]
            PTm = [brb("PTm0", F32), M1]
            TTm = [M2, EG]
            colg = Sw(k, "br", [k.sbuf(f"colg{i}", [128, 16], F32) for i in range(NBR)])
        if do1:
            qT1 = k.sbuf("qT1", [128, 8, 128], BF16)
            kT1 = k.sbuf("kT1", [128, 2, 272], BF16)
            Vt = k.sbuf("Vt", [128, 3, 128], BF16)
            gz1 = k.sbuf("gz1", [128, 1024], BF16)
            Pb = Sw(k, "hp", [k.sbuf(f"Pb{i}", [128, 272], BF16) for i in range(2)])
            PTb = k.sbuf("PTb", [128, 3, 128], BF16)
            og = hn.bufs[-1]
            acol = Sw(k, "hp", [k.sbuf(f"acol{i}", [128, 16], F32) for i in range(2)])
            pe_ser = Buf(None, "pe_ser")

        def rstd_from_ss(ss_ap, dst_ap, n, bufs_r, buf_w, extra_bias=0.0):
            k.act(lambda e: e.activation(out=dst_ap, in_=ss_ap, func=AF.Ln, bias=EPS, scale=1.0 / n), r=bufs_r, w=[buf_w])
            k.act(lambda e: e.activation(out=dst_ap, in_=dst_ap, func=AF.Exp, bias=extra_bias, scale=-0.5), r=[buf_w], w=[buf_w])

        def norm_transpose(prec_col):
            k.act(lambda e: e.activation(out=hn[:], in_=xt[:], func=AF.Square, accum_out=col[:, 0:1]), r=[xt], w=[hn, col])
            rstd_from_ss(col[:, 0:1], col[:, 1:2], 1024.0, [col], col)
            k.dve(lambda e: e.tensor_scalar(out=hn[:], in0=xt[:], scalar1=col[:, 1:2], scalar2=None, op0=ALU.mult), r=[xt, col], w=[hn])
            p = k.psum()
            for kc in range(8):
                k.pe(lambda e, kc=kc: e.transpose(p.bf[:, kc * 128:(kc + 1) * 128], hn[:, kc * 128:(kc + 1) * 128], identb[:]), r=[hn, identb], w=[p])
            k.dve(lambda e: e.tensor_tensor(out=hnT[:], in0=p.bf[:, 0:1024].rearrange("p (k t) -> p k t", k=8),
                                            in1=prm[:, prec_col:prec_col + 8].unsqueeze(2).to_broadcast([128, 8, 128]), op=ALU.mult),
                  r=[p, prm], w=[hnT])

        def proj_fm(W, col0, p, slot):
            for kc in range(8):
                k.pe(lambda e, kc=kc: e.matmul(p[:, slot * 128:(slot + 1) * 128], lhsT=W[:, kc, col0:col0 + 128], rhs=hnT[:, kc, :],
                                               start=(kc == 0), stop=(kc == 7)), r=[W, hnT], w=[p])

        def proj_tm(W, col0, n, p, off=0):
            for kc in range(8):
                k.pe(lambda e, kc=kc: e.matmul(p[:, off:off + n], lhsT=hnT[:, kc, :], rhs=W[:, kc, col0:col0 + n],
                                               start=(kc == 0), stop=(kc == 7)), r=[W, hnT], w=[p])

        def out_proj_residual(Wo, PN, oTl):
            ps = [k.psum(), k.psum()]
            for h in range(2):
                for e_ in range(8):
                    k.pe(lambda e, e_=e_, h=h: e.matmul(ps[h][:, 0:512], lhsT=oTl[e_][0], rhs=Wo[:, e_, h * 512:(h + 1) * 512],
                                                         start=(e_ == 0), stop=(e_ == 7)), r=[oTl[e_][1], Wo], w=[ps[h]])
            for h in range(2):
                k.act(lambda e, h=h: e.activation(out=ytmp[:], in_=ps[h][:, 0:512], func=AF.Square, accum_out=col[:, 4 + h:5 + h]),
                      r=[ps[h]], w=[ytmp, col])
            k.dve(lambda e: e.tensor_tensor(out=col[:, 6:7], in0=col[:, 4:5], in1=col[:, 5:6], op=ALU.add), r=[col], w=[col])
            rstd_from_ss(col[:, 6:7], col[:, 7:8], 1024.0, [col], col)
            for h in range(2):
                k.dve(lambda e, h=h: e.scalar_tensor_tensor(out=ytmp[:], in0=ps[h][:, 0:512], scalar=col[:, 7:8], in1=PN[:, h * 512:(h + 1) * 512],
                                                             op0=ALU.mult, op1=ALU.mult), r=[ps[h], col, PN], w=[ytmp])
                k.dve(lambda e, h=h: e.scalar_tensor_tensor(out=xt[:, h * 512:(h + 1) * 512], in0=ps[h][:, 0:512], scalar=col[:, 7:8], in1=xt[:, h * 512:(h + 1) * 512],
                                                             op0=ALU.mult, op1=ALU.add), r=[ps[h], col, xt], w=[xt])
                k.dve(lambda e, h=h: e.tensor_tensor(out=xt[:, h * 512:(h + 1) * 512], in0=xt[:, h * 512:(h + 1) * 512], in1=ytmp[:], op=ALU.add),
                      r=[xt, ytmp], w=[xt])

        def silu2(dst_ap, src_ap, n_shape_tmp, r_bufs, w_buf, src_psum=None):
            k.act(lambda e: e.activation(out=n_shape_tmp, in_=src_ap, func=AF.Tanh, scale=0.5), r=r_bufs, w=[tmpA])
            k.dve(lambda e: e.scalar_tensor_tensor(out=dst_ap, in0=n_shape_tmp, scalar=1.0, in1=src_ap, op0=ALU.add, op1=ALU.mult),
                  r=[tmpA] + list(r_bufs), w=[w_buf])

        L0B = [0, 1, 2, 3, 4] if mode == "full" else [0, 1, 2, 3, 4]
        def layer0(t):
            norm_transpose(P_PRE0)
            for grp in range(3):
                p = k.psum()
                for s in range(4):
                    proj_fm(W0, (grp * 4 + s) * 128, p, s)
                k.act(lambda e, grp=grp, p=p: e.activation(out=qkv_pre[:, grp * 4:(grp + 1) * 4, 3:131],
                                                           in_=p[:, 0:512].rearrange("p (c t) -> p c t", c=4), func=AF.Copy),
                      r=[p], w=[qkv_pre])
            pv = k.psum()
            pg = k.psum()
            for s in range(4):
                proj_fm(W0, 2056 + s * 128, pv, s)
            for s in range(4):
                proj_fm(W0, 2568 + s * 128, pg, s)
            k.act(lambda e: e.activation(out=tmpA[:, 0:512], in_=pg[:, 0:512], func=AF.Tanh, scale=0.5), r=[pg], w=[tmpA])
            k.dve(lambda e: e.scalar_tensor_tensor(out=cin[:, :, 30:158], in0=tmpA[:, 0:512].rearrange("p (c t) -> p c t", c=4), scalar=1.0,
                                                   in1=pv[:, 0:512].rearrange("p (c t) -> p c t", c=4), op0=ALU.add, op1=ALU.mult),
                  r=[tmpA, pv], w=[cin])
            for (c0, gz) in ((1536, gza), (3080, gzb)):
                p = k.psum()
                for s in range(4):
                    proj_fm(W0, c0 + s * 128, p, s)
                silu2(gz[:].rearrange("p c t -> p (c t)"), p[:, 0:512], tmpA[:, 0:512], [p], gz)
            pba = k.psum()
            proj_tm(W0, 2048, 8, pba)
            k.act(lambda e: e.activation(out=gcol[:, 0:4], in_=pba[:, 0:4], func=AF.Tanh, scale=0.5), r=[pba], w=[gcol])
            k.dve(lambda e: e.tensor_scalar(out=gcol[:, 0:4], in0=gcol[:, 0:4], scalar1=1.0, scalar2=0.5, op0=ALU.add, op1=ALU.mult), r=[gcol], w=[gcol])
            k.dve(lambda e: e.tensor_scalar(out=gcol[:, 4:8], in0=gcol[:, 0:4], scalar1=0.5, scalar2=None, op0=ALU.mult), r=[gcol], w=[gcol])
            k.dve(lambda e: e.tensor_tensor(out=gcol[:, 8:12], in0=pba[:, 4:8], in1=prm[:, P_DTB:P_DTB + 4], op=ALU.add), r=[pba, prm], w=[gcol])
            k.act(lambda e: e.activation(out=gcol[:, 8:12], in_=gcol[:, 8:12], func=AF.Exp), r=[gcol], w=[gcol])
            k.act(lambda e: e.activation(out=gcol[:, 8:12], in_=gcol[:, 8:12], func=AF.Ln, bias=1.0), r=[gcol], w=[gcol])
            k.dve(lambda e: e.tensor_tensor(out=gcol[:, 8:12], in0=gcol[:, 8:12], in1=prm2[:, 0:4], op=ALU.mult), r=[gcol, prm2], w=[gcol])
            pG = k.psum()
            k.pe(lambda e: e.matmul(pG[:, 0:4], lhsT=uincl, rhs=gcol[:, 8:12], start=True, stop=True), r=[cst, gcol], w=[pG])
            k.pe(lambda e: e.matmul(pG[:, 4:8], lhsT=ones, rhs=gcol[:, 8:12], start=True, stop=True), r=[cst, gcol], w=[pG])
            k.dve(lambda e: e.tensor_copy(out=gcol[:, 12:20], in_=pG[:, 0:8]), r=[pG], w=[gcol])
            k.act(lambda e: e.activation(out=gcol[:, 20:24], in_=gcol[:, 12:16], func=AF.Exp), r=[gcol], w=[gcol])
            k.dve(lambda e: e.tensor_tensor(out=gcol[:, 20:24], in0=gcol[:, 20:24], in1=gcol[:, 0:4], op=ALU.mult), r=[gcol], w=[gcol])
            k.dve(lambda e: e.tensor_tensor(out=gcol[:, 24:28], in0=gcol[:, 16:20], in1=gcol[:, 12:16], op=ALU.subtract), r=[gcol], w=[gcol])
            k.act(lambda e: e.activation(out=gcol[:, 24:28], in_=gcol[:, 24:28], func=AF.Exp), r=[gcol], w=[gcol])
            k.act(lambda e: e.activation(out=gcol[:, 28:32], in_=gcol[:, 16:20], func=AF.Exp), r=[gcol], w=[gcol])

            for g3 in range(3):
                pc = k.psum()
                for c4 in range(4):
                    c = g3 * 4 + c4
                    for j in range(4):
                        dg = dgq[(c * 4 + j) % NDG]
                        wcol = prm[:, P_QKVW + c * 4 + j:P_QKVW + c * 4 + j + 1]
                        k.dve(lambda e, dg=dg, wcol=wcol: e.tensor_scalar(out=dg[:], in0=identb[:], scalar1=wcol, scalar2=None, op0=ALU.mult), r=[identb, prm], w=[dg])
                        k.pe(lambda e, dg=dg, c=c, c4=c4, j=j, pc=pc: e.matmul(pc[:, c4 * 128:(c4 + 1) * 128], lhsT=dg[:], rhs=qkv_pre[:, c, j:j + 128],
                                                                              start=(j == 0), stop=(j == 3)), r=[dg, qkv_pre], w=[pc])
                silu2(qkvs[:, g3 * 4:(g3 + 1) * 4, :].rearrange("p c t -> p (c t)"), pc[:, 0:512], tmpA[:, 0:512], [pc], qkvs)
            k.pool(lambda e: e.tensor_copy(out=qkv_pre[:, :, 0:3], in_=qkv_pre[:, :, 128:131]), r=[qkv_pre], w=[qkv_pre])

            def conformer():
                pcv = k.psum()
                for c in range(4):
                    for j in range(31):
                        dg = dgc[(c * 31 + j) % NDG]
                        wcol = prm[:, P_DWW + c * 31 + j:P_DWW + c * 31 + j + 1]
                        k.dve(lambda e, dg=dg, wcol=wcol: e.tensor_scalar(out=dg[:], in0=identb[:], scalar1=wcol, scalar2=None, op0=ALU.mult), r=[identb, prm], w=[dg])
                        k.pe(lambda e, dg=dg, c=c, j=j: e.matmul(pcv[:, c * 128:(c + 1) * 128], lhsT=dg[:], rhs=cin[:, c, j:j + 128],
                                                                 start=(j == 0), stop=(j == 30)), r=[dg, cin], w=[pcv])
                for c in range(4):
                    k.dve(lambda e, c=c: e.tensor_scalar(out=cc[:, c, :], in0=pcv[:, c * 128:(c + 1) * 128], scalar1=0.5, scalar2=prm[:, P_DWB + c:P_DWB + c + 1],
                                                         op0=ALU.mult, op1=ALU.add), r=[pcv, prm], w=[cc])
                k.pool(lambda e: e.tensor_copy(out=cin[:, :, 0:30], in_=cin[:, :, 128:158]), r=[cin], w=[cin])
                k.pool(lambda e: e.tensor_tensor(out=ccsq[:], in0=cc[:], in1=cc[:], op=ALU.mult), r=[cc], w=[ccsq])
                pm = k.psum()
                for c in range(4):
                    k.pe(lambda e, c=c: e.matmul(pm[:, 0:128], lhsT=ones, rhs=cc[:, c, :], start=(c == 0), stop=(c == 3)), r=[cst, cc], w=[pm])
                for c in range(4):
                    k.pe(lambda e, c=c: e.matmul(pm[:, 128:256], lhsT=ones, rhs=ccsq[:, c, :], start=(c == 0), stop=(c == 3)), r=[cst, ccsq], w=[pm])
                k.act(lambda e: e.activation(out=lnm[:, 0, :], in_=pm[:, 0:128], func=AF.Copy, scale=1.0 / 512), r=[pm], w=[lnm])
                k.dve(lambda e: e.tensor_tensor(out=lnm[:, 1, :], in0=lnm[:, 0, :], in1=lnm[:, 0, :], op=ALU.mult), r=[lnm], w=[lnm])
                k.dve(lambda e: e.scalar_tensor_tensor(out=lnm[:, 1, :], in0=pm[:, 128:256], scalar=1.0 / 512, in1=lnm[:, 1, :], op0=ALU.mult, op1=ALU.subtract),
                      r=[pm, lnm], w=[lnm])
                k.act(lambda e: e.activation(out=lnm[:, 1, :], in_=lnm[:, 1, :], func=AF.Ln, bias=EPS), r=[lnm], w=[lnm])
                k.act(lambda e: e.activation(out=lnm[:, 1, :], in_=lnm[:, 1, :], func=AF.Exp, scale=-0.5), r=[lnm], w=[lnm])
                k.dve(lambda e: e.tensor_tensor(out=cc[:], in0=cc[:], in1=lnm[:, 0, :].unsqueeze(1).to_broadcast([128, 4, 128]), op=ALU.subtract), r=[cc, lnm], w=[cc])
                k.dve(lambda e: e.tensor_tensor(out=cc[:], in0=cc[:], in1=lnm[:, 1, :].unsqueeze(1).to_broadcast([128, 4, 128]), op=ALU.mult), r=[cc, lnm], w=[cc])
                for c in range(4):
                    k.dve(lambda e, c=c: e.tensor_scalar(out=cc[:, c, :], in0=cc[:, c, :], scalar1=prm[:, P_LNW + c:P_LNW + c + 1], scalar2=prm[:, P_LNB + c:P_LNB + c + 1],
                                                         op0=ALU.mult, op1=ALU.add), r=[cc, prm], w=[cc])
                silu2(ccsq[:].rearrange("p c t -> p (c t)"), cc[:].rearrange("p c t -> p (c t)"), tmpA[:, 0:512], [cc], ccsq)
                k.dve(lambda e: e.scalar_tensor_tensor(out=oTb[:], in0=ccsq[:], scalar=0.25, in1=gzb[:], op0=ALU.mult, op1=ALU.mult), r=[ccsq, gzb], w=[oTb])

            def gdn_head(h):
                pt = k.psum()
                for i_, cidx in enumerate((h, 4 + h, 8 + h)):
                    k.pe(lambda e, i_=i_, cidx=cidx: e.transpose(pt.bf[:, i_ * 128:(i_ + 1) * 128], qkvs[:, cidx, :], identb[:]), r=[qkvs, identb], w=[pt])
                for i_, dst, xb in ((0, qn_tok, math.log(128.0 ** -0.5)), (1, kn_tok, 0.0)):
                    k.act(lambda e, i_=i_: e.activation(out=junk[:], in_=pt.bf[:, i_ * 128:(i_ + 1) * 128], func=AF.Square, accum_out=colg[:, 8 + i_:9 + i_]),
                          r=[pt], w=[junk, colg])
                    k.act(lambda e, i_=i_: e.activation(out=colg[:, 10 + i_:11 + i_], in_=colg[:, 8 + i_:9 + i_], func=AF.Ln, bias=4 * EPS), r=[colg], w=[colg])
                    k.act(lambda e, i_=i_, xb=xb: e.activation(out=colg[:, 10 + i_:11 + i_], in_=colg[:, 10 + i_:11 + i_], func=AF.Exp, bias=xb, scale=-0.5), r=[colg], w=[colg])
                    k.dve(lambda e, i_=i_, dst=dst: e.tensor_scalar(out=dst[:], in0=pt.bf[:, i_ * 128:(i_ + 1) * 128], scalar1=colg[:, 10 + i_:11 + i_], scalar2=None, op0=ALU.mult),
                          r=[pt, colg], w=[dst])
                k.dve(lambda e, h=h: e.tensor_scalar(out=vb[:], in0=pt.bf[:, 256:384], scalar1=gcol[:, 4 + h:5 + h], scalar2=None, op0=ALU.mult), r=[pt, gcol], w=[vb])
                k.dve(lambda e, h=h: e.tensor_scalar(out=kbg[:], in0=kn_tok[:], scalar1=gcol[:, 20 + h:21 + h], scalar2=None, op0=ALU.mult), r=[kn_tok, gcol], w=[kbg])
                k.dve(lambda e, h=h: e.tensor_scalar(out=kdec[:], in0=kn_tok[:], scalar1=gcol[:, 24 + h:25 + h], scalar2=None, op0=ALU.mult), r=[kn_tok, gcol], w=[kdec])
                pt2 = k.psum()
                k.pe(lambda e: e.transpose(pt2.bf[:, 0:128], qn_tok[:], identb[:]), r=[qn_tok, identb], w=[pt2])
                k.pe(lambda e: e.transpose(pt2.bf[:, 128:256], kn_tok[:], identb[:]), r=[kn_tok, identb], w=[pt2])
                k.act(lambda e: e.activation(out=qnT[:], in_=pt2.bf[:, 0:128], func=AF.Copy), r=[pt2], w=[qnT])
                k.act(lambda e: e.activation(out=knT[:], in_=pt2.bf[:, 128:256], func=AF.Copy), r=[pt2], w=[knT])
                pg_ = k.psum()
                k.pe(lambda e, h=h: e.matmul(pg_[:, 0:128], lhsT=gcol[:, 8 + h:9 + h].to_broadcast([128, 128]), rhs=uincl, start=True, stop=True), r=[gcol, cst], w=[pg_])
                k.dve(lambda e, h=h: e.tensor_scalar(out=EX[:], in0=pg_[:, 0:128], scalar1=gcol[:, 12 + h:13 + h], scalar2=None, op0=ALU.subtract), r=[pg_, gcol], w=[EX])
                k.act(lambda e: e.activation(out=EX[:], in_=EX[:], func=AF.Abs), r=[EX], w=[EX])
                k.act(lambda e: e.activation(out=EX[:], in_=EX[:], func=AF.Exp, scale=-1.0), r=[EX], w=[EX])
                k.act(lambda e: e.activation(out=EG[:], in_=pg_[:, 0:128], func=AF.Exp), r=[pg_], w=[EG])
                k.pool(lambda e: e.tensor_tensor(out=M1[:], in0=EX[:], in1=lstrict, op=ALU.mult), r=[EX, cst], w=[M1])
                k.dve(lambda e: e.tensor_tensor(out=M2[:], in0=EX[:], in1=uincl, op=ALU.mult), r=[EX, cst], w=[M2])
                k.dve(lambda e: e.tensor_tensor(out=qdecT[:], in0=qnT[:], in1=EG[:], op=ALU.mult), r=[qnT, EG], w=[qdecT])
                pk = k.psum()
                k.pe(lambda e: e.matmul(pk[:, 0:128], lhsT=knT[:], rhs=knT[:], start=True, stop=True), r=[knT], w=[pk])
                k.pe(lambda e: e.matmul(pk[:, 128:256], lhsT=knT[:], rhs=qnT[:], start=True, stop=True), r=[knT, qnT], w=[pk])
                k.dve(lambda e, h=h: e.scalar_tensor_tensor(out=Pm[0][:], in0=pk[:, 0:128], scalar=gcol[:, h:h + 1], in1=M1[:], op0=ALU.mult, op1=ALU.mult),
                      r=[pk, gcol, M1], w=[Pm[0]])
                k.dve(lambda e: e.tensor_tensor(out=QKmT[:], in0=pk[:, 128:256], in1=M2[:], op=ALU.mult), r=[pk, M2], w=[QKmT])
                pa = k.psum()
                k.pe(lambda e: e.transpose(pa[:, 0:128], Pm[0][:], ident), r=[Pm[0], cst], w=[pa])
                k.act(lambda e: e.activation(out=PTm[0][:], in_=pa[:, 0:128], func=AF.Copy), r=[pa], w=[PTm[0]])
                k.dve(lambda e: e.tensor_tensor(out=TTm[0][:], in0=ident, in1=pa[:, 0:128], op=ALU.subtract), r=[cst, pa], w=[TTm[0]])
                cur = 0
                for it in range(6):
                    nxt = 1 - cur
                    last = (it == 5)
                    pq = k.psum()
                    k.pe(lambda e, cur=cur: e.matmul(pq[:, 0:128], lhsT=PTm[cur][:], rhs=Pm[cur][:], start=True, stop=True), r=[PTm[cur], Pm[cur]], w=[pq])
                    if not last:
                        k.pe(lambda e, cur=cur: e.matmul(pq[:, 128:256], lhsT=Pm[cur][:], rhs=PTm[cur][:], start=True, stop=True), r=[PTm[cur], Pm[cur]], w=[pq])
                    k.act(lambda e, nxt=nxt: e.activation(out=Pm[nxt][:], in_=pq[:, 0:128], func=AF.Copy), r=[pq], w=[Pm[nxt]])
                    if not last:
                        k.dve(lambda e, nxt=nxt: e.tensor_copy(out=PTm[nxt][:], in_=pq[:, 128:256]), r=[pq], w=[PTm[nxt]])
                    pt3 = k.psum()
                    k.pe(lambda e, cur=cur, nxt=nxt: e.matmul(pt3[:, 0:128], lhsT=Pm[nxt][:], rhs=TTm[cur][:], start=True, stop=True), r=[Pm[nxt], TTm[cur]], w=[pt3])
                    if last:
                        k.dve(lambda e, cur=cur: e.tensor_tensor(out=TTb[:], in0=pt3[:, 0:128], in1=TTm[cur][:], op=ALU.add), r=[pt3, TTm[cur]], w=[TTb])
                    else:
                        k.dve(lambda e, cur=cur, nxt=nxt: e.tensor_tensor(out=TTm[nxt][:], in0=pt3[:, 0:128], in1=TTm[cur][:], op=ALU.add), r=[pt3, TTm[cur]], w=[TTm[nxt]])
                    cur = nxt
                pw = k.psum()
                k.pe(lambda e: e.matmul(pw[:, 0:128], lhsT=kbg[:], rhs=TTb[:], start=True, stop=True), r=[kbg, TTb], w=[pw])
                k.act(lambda e: e.activation(out=wTn[:], in_=pw[:, 0:128], func=AF.Copy, scale=-1.0), r=[pw], w=[wTn])
                k.pe(lambda e: e.matmul(pw[:, 128:256], lhsT=TTb[:], rhs=vb[:], start=True, stop=False), r=[TTb, vb], w=[pw])
                k.pe(lambda e, h=h: e.matmul(pw[:, 128:256], lhsT=wTn[:], rhs=Sb[h][:], start=False, stop=True), r=[wTn, Sb[h]], w=[pw])
                k.act(lambda e: e.activation(out=vnew[:], in_=pw[:, 128:256], func=AF.Copy), r=[pw], w=[vnew])
                po = k.psum()
                k.pe(lambda e, h=h: e.matmul(po[:, 0:128], lhsT=qdecT[:], rhs=Sb[h][:], start=True, stop=False), r=[qdecT, Sb[h]], w=[po])
                k.pe(lambda e: e.matmul(po[:, 0:128], lhsT=QKmT[:], rhs=vnew[:], start=False, stop=True), r=[QKmT, vnew], w=[po])
                k.pe(lambda e: e.matmul(po[:, 128:256], lhsT=kdec[:], rhs=vnew[:], start=True, stop=True), r=[kdec, vnew], w=[po])
                k.dve(lambda e, h=h: e.scalar_tensor_tensor(out=Sst[h][:], in0=Sst[h][:], scalar=gcol[:, 28 + h:29 + h], in1=po[:, 128:256],
                                                             op0=ALU.mult, op1=ALU.add), r=[Sst[h], gcol, po], w=[Sst[h]])
                k.act(lambda e, h=h: e.activation(out=Sb[h][:], in_=Sst[h][:], func=AF.Copy), r=[Sst[h]], w=[Sb[h]])
                k.act(lambda e: e.activation(out=junk[:], in_=po[:, 0:128], func=AF.Square, accum_out=colg[:, 12:13]), r=[po], w=[junk, colg])
                rstd_from_ss(colg[:, 12:13], colg[:, 13:14], 128.0, [colg], colg)
                k.dve(lambda e: e.tensor_scalar(out=on_b[:], in0=po[:, 0:128], scalar1=colg[:, 13:14], scalar2=None, op0=ALU.mult), r=[po, colg], w=[on_b])
                pot = k.psum()
                k.pe(lambda e: e.transpose(pot.bf[:, 0:128], on_b[:], identb[:]), r=[on_b, identb], w=[pot])
                k.dve(lambda e, h=h: e.scalar_tensor_tensor(out=oTa[h][:], in0=pot.bf[:, 0:128], scalar=prm2[:, 4:5], in1=gza[:, h, :], op0=ALU.mult, op1=ALU.mult),
                      r=[pot, prm2, gza], w=[oTa[h]])
            par = k.fork()
            for h_ in range(4):
                with k.branch(par, L0B[h_:h_ + 1], offset=0.05 * h_):
                    k.ctx["br"] = h_
                    gdn_head(h_)
            with k.branch(par, L0B[4:5], offset=CF_OFF, span=CF_SPAN):
                k.ctx["br"] = 0
                conformer()
            k.ctx["br"] = 0
            out_proj_residual(Wo0, PN0, oT_l0)

        def layer1(t):
            k.pin_pe = ("norm" not in L1_FLOAT)
            norm_transpose(P_PRE1)
            k.pin_pe = ("proj" not in L1_FLOAT)
            if dbg < 1.1:
                return
            for grp in range(2):
                p = k.psum()
                for s in range(4):
                    proj_fm(W1, (grp * 4 + s) * 128, p, s)
                k.act(lambda e, grp=grp, p=p: e.activation(out=qT1[:, grp * 4:(grp + 1) * 4, :], in_=p[:, 0:512].rearrange("p (c t) -> p c t", c=4),
                                                           func=AF.Copy, scale=0.125), r=[p], w=[qT1])
            if dbg < 1.3:
                return
            pkv = k.psum()
            proj_fm(W1, 1024, pkv, 0)
            if dbg != 1.36:
                k.act(lambda e: e.activation(out=kT1[0:64, 0, 128:256], in_=pkv[0:64, 0:128], func=AF.Copy), r=[pkv], w=[kT1])
                k.act(lambda e: e.activation(out=kT1[64:128, 1, 128:256], in_=pkv[64:128, 0:128], func=AF.Copy), r=[pkv], w=[kT1])
            if dbg == 1.35:
                return
            pkv2 = k.psum()
            proj_tm(W1, 1152, 128, pkv2, off=0)
            if dbg == 1.37:
                return
            k.dve(lambda e: e.tensor_scalar(out=Vt[:, 1, :], in0=pkv2[:, 0:128], scalar1=0.5, scalar2=None, op0=ALU.mult), r=[pkv2], w=[Vt])
            if dbg < 1.5:
                return
            if t == 0:
                k.dve(lambda e: e.tensor_copy(out=kT1[:, :, 256:272], in_=kT1[:, :, 240:256]), r=[kT1], w=[kT1])
                k.dve(lambda e: e.tensor_scalar(out=Vt[:, 2, :], in0=Vt[:, 1, :], scalar1=rowm[:, 0:1], scalar2=None, op0=ALU.mult), r=[Vt, rowm], w=[Vt])
            if dbg < 1.7:
                return
            for hh in range(2):
                p = k.psum()
                proj_tm(W1, 1280 + hh * 512, 512, p)
                silu2(gz1[:, hh * 512:(hh + 1) * 512], p[:, 0:512], tmpA[:, 0:512], [p], gz1)
            mbo = 272 * min(t, 2)
            k.pin_pe = ("attn" not in L1_FLOAT)
            if dbg < 3:
                return
            for c in range(8 if not (3 <= dbg < 4) else max(1, int(round((dbg - 3) * 10)))):
                for half in range(2):
                    hc = 2 * c + half
                    k.ctx["hp"] = hc % 2
                    lo, hi = half * 64, (half + 1) * 64
                    ps_ = k.psum()
                    k.pin_pe = ("attn" not in L1_FLOAT) and ("attn_s" not in L1_FLOAT)
                    k.pe(lambda e, c=c, half=half: e.matmul(ps_[:, 0:272], lhsT=qT1[:, c, :], rhs=kT1[:, half, :], start=True, stop=False),
                         r=[qT1, kT1], w=[ps_])
                    k.pe(lambda e: e.matmul(ps_[:, 0:272], lhsT=identb[:], rhs=maskb[:, mbo:mbo + 272], start=False, stop=True), r=[identb, maskb], w=[ps_])
                    k.dve(lambda e: e.reduce_max(out=acol[:, 0:1], in_=ps_[:, 0:272], axis=AX.X), r=[ps_], w=[acol])
                    k.dve(lambda e, hc=hc: e.tensor_scalar(out=acol[:, 2:3], in0=acol[:, 0:1], scalar1=prm[:, P_SINK + hc:P_SINK + hc + 1], scalar2=-1.0, op0=ALU.max, op1=ALU.mult),
                          r=[acol, prm], w=[acol])
                    k.act(lambda e: e.activation(out=Pb[:], in_=ps_[:, 0:272], func=AF.Exp, bias=acol[:, 2:3], accum_out=acol[:, 3:4]), r=[ps_, acol], w=[Pb, acol])
                    k.act(lambda e, hc=hc: e.activation(out=acol[:, 4:5], in_=prm[:, P_SINK + hc:P_SINK + hc + 1], func=AF.Exp, bias=acol[:, 2:3]), r=[prm, acol], w=[acol])
                    k.dve(lambda e: e.tensor_tensor(out=acol[:, 5:6], in0=acol[:, 3:4], in1=acol[:, 4:5], op=ALU.add), r=[acol], w=[acol])
                    k.dve(lambda e: e.reciprocal(out=acol[:, 7:8], in_=acol[:, 5:6]), r=[acol], w=[acol])
                    ptp = k.psum()
                    k.pin_pe = ("attn" not in L1_FLOAT) and ("attn_t" not in L1_FLOAT)
                    for g_, c0_ in enumerate((0, 128, 144)):
                        k.pe(lambda e, g_=g_, c0_=c0_: e.transpose(ptp.bf[:, g_ * 128:(g_ + 1) * 128], Pb[:, c0_:c0_ + 128], identb[:]), r=[Pb, identb], w=[ptp])
                    k.act(lambda e: e.activation(out=PTb[:], in_=ptp.bf[:, 0:384].rearrange("p (c t) -> p c t", c=3), func=AF.Copy), r=[ptp], w=[PTb])
                    pov = k.psum()
                    k.pin_pe = ("attn" not in L1_FLOAT) and ("attn_pv" not in L1_FLOAT)
                    for g_ in range(3):
                        k.pe(lambda e, g_=g_, lo=lo, hi=hi: e.matmul(pov[:, 0:64], lhsT=PTb[:, g_, :], rhs=Vt[:, g_, lo:hi], start=(g_ == 0), stop=(g_ == 2)), r=[PTb, Vt], w=[pov])
                    k.dve(lambda e, hc=hc: e.scalar_tensor_tensor(out=og[:, hc * 64:(hc + 1) * 64], in0=pov[:, 0:64], scalar=acol[:, 7:8], in1=gz1[:, hc * 64:(hc + 1) * 64],
                                                                   op0=ALU.mult, op1=ALU.mult), r=[pov, acol, gz1], w=[og])
            if dbg < 5:
                return
            k.pool(lambda e: e.tensor_copy(out=kT1[:, :, 0:128], in_=kT1[:, :, 128:256]), r=[kT1], w=[kT1])
            k.pool(lambda e: e.tensor_copy(out=Vt[:, 0, :], in_=Vt[:, 1, :]), r=[Vt], w=[Vt])
            k.pin_pe = ("tail" not in L1_FLOAT)
            pt_ = k.psum()
            for kc in range(8):
                k.pe(lambda e, kc=kc: e.transpose(pt_.bf[:, kc * 128:(kc + 1) * 128], og[:, kc * 128:(kc + 1) * 128], identb[:]), r=[og, identb], w=[pt_])
            k.act(lambda e: e.activation(out=oT1[:], in_=pt_.bf[:, 0:1024].rearrange("p (k t) -> p k t", k=8), func=AF.Copy), r=[pt_], w=[oT1])
            out_proj_residual(Wo1, PN1, oT_l1)

        steps = [(s_, t_) for s_ in range(nseq) for t_ in range(ntiles)]

        def load(i):
            s_, t = steps[i]
            if mode == "l1":
                k.dma("sp", lambda e: e.dma_start(out=xt[:], in_=hin_d[s_, t * 128:(t + 1) * 128, :]), w=[xt], group=xt)
            elif t == 0:
                k.pool(lambda e: e.memset(xt[:], 0.0), w=[xt])
                k.dma("sp", lambda e: e.dma_start(out=xt[112:128, :], in_=meta_d), w=[xt], group=xt)
            else:
                k.dma("sp", lambda e: e.dma_start(out=xt[:], in_=x_d[s_, (t - 1) * 128:t * 128, :]), w=[xt], group=xt)

        def store(i):
            s_, t = steps[i]
            if mode == "l0":
                k.dma("sp", lambda e: e.dma_start(out=out_d[s_, t * 128:(t + 1) * 128, :], in_=xt[:]), r=[xt], group=xt)
            elif t >= 1:
                k.dma("sp", lambda e: e.dma_start(out=out_d[s_, (t - 1) * 128:t * 128, :], in_=xt[:]), r=[xt], group=xt)

        def do_l0(i, th):
            s_, t = steps[i]
            k.ctx["th"] = th
            k.ctx["x"] = i % nth
            if t == 0:
                k.pool(lambda e: e.memset(qkv_pre[:, :, 0:3], 0.0), w=[qkv_pre])
                k.pool(lambda e: e.memset(cin[:, :, 0:30], 0.0), w=[cin])
                for h_ in range(4):
                    k.pool(lambda e, h_=h_: e.memset(Sst[h_][:], 0.0), w=[Sst[h_]])
                    k.pool(lambda e, h_=h_: e.memset(Sb[h_][:], 0.0), w=[Sb[h_]])
            load(i)
            layer0(t)
            if mode == "l0":
                store(i)

        def do_l1(i, th):
            s_, t = steps[i]
            k.pin_pe = True
            k.ctx["th"] = th
            k.ctx["x"] = i % nth
            if t == 0:
                k.pool(lambda e: e.memset(kT1[:], 0.0), w=[kT1])
                k.pool(lambda e: e.memset(Vt[:], 0.0), w=[Vt])
                k.pool(lambda e: e.memset(PTb[:], 0.0), w=[PTb])
            if mode == "l1":
                load(i)
            layer1(t)
            k.pin_pe = False
            store(i)

        n = len(steps)
        if mode == "full":
            BA, BB = L0B, [5, 6, 7]
            for i in range(n + 1):
                par = k.fork()
                if i < n:
                    with k.branch(par, BA):
                        do_l0(i, 0)
                if i >= 1:
                    with k.branch(par, BB, offset=L1_OFF, span=L1_SPAN):
                        do_l1(i - 1, 1)
        else:
            for i in range(n):
                if do0:
                    do_l0(i, 0)
                else:
                    do_l1(i, 0)
        k.finalize()
    return nc, k


_PERM = np.array([h * 64 + d for c in range(8) for h in (c, 8 + c) for d in range(64)])
_HPERM = np.array([h for c in range(8) for h in (c, 8 + c)])


def _common_inputs(inp):
    f = lambda a: np.ascontiguousarray(np.asarray(a, dtype=np.float32))
    w1 = f(inp["odd_w_in"])[0]
    w1p = np.concatenate([w1[:, 0:1024][:, _PERM], w1[:, 1024:1280], w1[:, 1280:2304][:, _PERM]], axis=1)
    d = {
        "meta": f(inp["meta_tokens"]),
        "consts": make_consts(),
        "w_in0": f(inp["even_w_in"])[0],
        "pre0c": f(f(inp["even_pre_norm"])[0].reshape(8, 128).T),
        "qkvw": f(f(inp["even_qkv_conv"])[0].T.reshape(12, 128, 4).transpose(1, 0, 2)),
        "a_log": f(inp["even_a_log"])[0],
        "dt_bias": f(inp["even_dt_bias"])[0],
        "onorm": f(f(inp["even_out_norm"])[0].reshape(128, 1)),
        "dww": f(f(inp["even_dw_conv"])[0].T.reshape(4, 128, 31).transpose(1, 0, 2)),
        "dwb": f(f(inp["even_dw_bias"])[0].reshape(4, 128).T),
        "lnw": f(f(inp["even_ln_w"])[0].reshape(4, 128).T),
        "lnb": f(f(inp["even_ln_b"])[0].reshape(4, 128).T),
        "w_out0": f(inp["even_w_out"])[0],
        "post0": f(inp["even_post_norm"])[0],
        "pre1c": f(f(inp["odd_pre_norm"])[0].reshape(8, 128).T),
        "w_in1": f(w1p),
        "sinks": f(f(inp["odd_sinks"])[0][_HPERM]),
        "w_out1": f(f(inp["odd_w_out"])[0][_PERM, :]),
        "post1": f(inp["odd_post_norm"])[0],
    }
    return d


_CACHE = {}


def _get(mode):
    if mode not in _CACHE:
        _CACHE[mode] = build(mode)[0]
    return _CACHE[mode]


def kernel(**inputs):
    x = np.ascontiguousarray(np.asarray(inputs["x"], dtype=np.float32))
    common = _common_inputs(inputs)
    nc = _get("full")
    in_maps = []
    for c in range(NCORES):
        m = dict(common)
        m["x"] = x[2 * c:2 * c + 2]
        in_maps.append(m)
    res = run_bass_kernel_spmd(nc, in_maps, core_ids=list(range(NCORES)))
    return np.concatenate([r["out"] for r in res.results], axis=0)
```
